# Optimizing a Trainium2 kernel written in Bass

```python
import jax
import jax.numpy as jnp
from jax import lax
import numpy as np

D_MODEL = 1024
BATCH = 16
SEQ = 2048
DEPTH = 4

GRID_W = 64
CTX_LEN = 256
D_FF = 4 * D_MODEL
N_EVEN = (DEPTH + 1) // 2
N_ODD = DEPTH // 2
MIX_W = D_MODEL

MLSTM_HEADS = 4
MLSTM_DH = MIX_W // 2 // MLSTM_HEADS
MLSTM_W = MLSTM_HEADS * MLSTM_DH
MLSTM_CHUNK = 64
RWKV_W = MIX_W // 2
RWKV_DH = 64
RWKV_HEADS = RWKV_W // RWKV_DH
RWKV_DECAY_RANK = 64
RWKV_ICLR_RANK = 64
RWKV_GATE_RANK = 128
RWKV_LN_EPS = 64e-5
LRU_W = MIX_W // 2
LRU_BLOCKS = 8
LRU_BW = LRU_W // LRU_BLOCKS
LRU_CONV = 4
LRU_CONV_PAD = ((LRU_CONV - 1) // 2, LRU_CONV // 2)
LRU_C = 8.0
NA_W = MIX_W // 2
NA_DH = 64
NA_HEADS = NA_W // NA_DH
NA_KH = 8
NA_KW = 16
NA_COL_BLOCK = 16
NA_COL_BAND = 32
ROPE_BASE = 10000.0
NORM_EPS = 1e-6

MLSTM_SPLITS = [MLSTM_W] * 4 + [2 * MLSTM_HEADS] * 2
MLSTM_IN = sum(MLSTM_SPLITS)
RWKV_SPLITS = [RWKV_W] * 3 + [2 * RWKV_DECAY_RANK, 2 * RWKV_ICLR_RANK, RWKV_GATE_RANK]
RWKV_IN = sum(RWKV_SPLITS)
EVEN_IN = MLSTM_IN + RWKV_IN
ODD_SPLITS = [LRU_W, LRU_W, NA_W, NA_W, NA_W]
ODD_IN = sum(ODD_SPLITS)

kernel_name = 'hybrid_mlstm_rwkv7_rglru_natten_prefix_dit'


def _cuts(sizes):
    return [int(v) for v in np.cumsum(sizes)[:-1]]


def rms_norm(x):
    xf = x.astype(jnp.float32)
    return (xf * lax.rsqrt(jnp.mean(xf * xf, axis=-1, keepdims=True) + NORM_EPS)).astype(x.dtype)


def ada_modulation(cond, w, b):
    m = jax.nn.silu(cond) @ w + b
    return jnp.split(m[:, None, :], 6, axis=-1)


def modulate(h, shift, scale):
    return h * (1 + scale) + shift


def sq_relu_mlp(h, w1, w2):
    return jnp.square(jax.nn.relu(h @ w1)) @ w2


def axial_rope(n_tok, head_dim, dtype):
    t = jnp.arange(n_tok)
    row = (t // GRID_W).astype(jnp.float32)
    col = (t % GRID_W).astype(jnp.float32)
    n_freq = head_dim // 4
    inv = ROPE_BASE ** (-jnp.arange(n_freq, dtype=jnp.float32) / n_freq)
    ang = jnp.concatenate([row[:, None] * inv, col[:, None] * inv], axis=-1)
    return jnp.cos(ang).astype(dtype), jnp.sin(ang).astype(dtype)


def apply_rope(x, cos, sin):
    x1, x2 = jnp.split(x, 2, axis=-1)
    cos = cos[None, :, None, :]
    sin = sin[None, :, None, :]
    return jnp.concatenate([x1 * cos - x2 * sin, x1 * sin + x2 * cos], axis=-1)


def centred_shift(z):
    zp = jnp.pad(z, ((0, 0), (1, 1), (0, 0)))
    return 0.5 * (zp[:, :-2] + zp[:, 2:])


def mlstm_inputs(z, gate_b, rope):
    B, T, _ = z.shape
    q, k, v, og, ig, fg = jnp.split(z, _cuts(MLSTM_SPLITS), axis=-1)
    heads = lambda a: a.reshape(B, T, MLSTM_HEADS, MLSTM_DH)
    q, k, v = heads(q), heads(k) * (MLSTM_DH ** -0.5), heads(v)
    if rope is not None:
        q = apply_rope(q, *rope)
        k = apply_rope(k, *rope)
    log_i = ig.reshape(B, T, 2, MLSTM_HEADS).astype(jnp.float32) + gate_b[:, 0]
    log_f = jax.nn.log_sigmoid(fg.reshape(B, T, 2, MLSTM_HEADS).astype(jnp.float32) + gate_b[:, 1])
    return q, k, v, log_i, log_f, og


def mlstm_run(q, k, v, log_i, log_f, state):
    B, T, H, dh = q.shape
    L = MLSTM_CHUNK
    nc = T // L
    both = lambda a: jnp.stack([a, jnp.flip(a, 1)], 0)
    per = lambda a: jnp.stack([a[:, :, 0], jnp.flip(a[:, :, 1], 1)], 0)

    def chunks(a):
        a = jnp.moveaxis(a, 2, 3)
        a = a.reshape(a.shape[:3] + (nc, L) + a.shape[4:])
        return jnp.moveaxis(a, 3, 0)

    lower = jnp.tril(jnp.ones((L, L), dtype=bool))

    def step(carry, inp):
        C, n, m = carry
        qc, kc, vc, li, lf = inp
        b = jnp.cumsum(lf, axis=-1)
        d = jnp.where(lower, b[..., :, None] - b[..., None, :] + li[..., None, :], -jnp.inf)
        m_prev = b + m[..., None]
        m_t = jnp.maximum(m_prev, jnp.max(d, axis=-1))
        w_prev = jnp.exp(m_prev - m_t)
        s = jnp.einsum('zbhtd,zbhsd->zbhts', qc, kc) * jnp.exp(d - m_t[..., None])
        num = w_prev[..., None] * jnp.einsum('zbhtd,zbhde->zbhte', qc, C) + jnp.einsum('zbhts,zbhse->zbhte', s, vc)
        den = w_prev * jnp.einsum('zbhtd,zbhd->zbht', qc, n) + jnp.sum(s, axis=-1)
        h = num / jnp.maximum(jnp.abs(den), jnp.exp(-m_t))[..., None]
        b_end = b[..., -1]
        g = b_end[..., None] - b + li
        m_new = jnp.maximum(b_end + m, jnp.max(g, axis=-1))
        keep = jnp.exp(b_end + m - m_new)
        wk = jnp.exp(g - m_new[..., None])
        C = keep[..., None, None] * C + jnp.einsum('zbhs,zbhsd,zbhse->zbhde', wk, kc, vc)
        n = keep[..., None] * n + jnp.einsum('zbhs,zbhsd->zbhd', wk, kc)
        return (C, n, m_new), h

    xs = (chunks(both(q)), chunks(both(k)), chunks(both(v)), chunks(per(log_i)), chunks(per(log_f)))
    state, h = lax.scan(step, state, xs)
    h = jnp.moveaxis(h, 0, 3).reshape(2, B, H, T, dh)
    h = jnp.moveaxis(h, 3, 2)
    return (h[0] + jnp.flip(h[1], 1)).astype(q.dtype), state


def mlstm_out(h, og, norm_w):
    B, T = og.shape[:2]
    hf = h.astype(jnp.float32)
    hn = (hf * lax.rsqrt(jnp.mean(hf * hf, axis=-1, keepdims=True) + NORM_EPS)).astype(og.dtype)
    hn = hn * norm_w.reshape(MLSTM_HEADS, MLSTM_DH)
    return hn.reshape(B, T, MLSTM_W) * jax.nn.sigmoid(og)


def rwkv_inputs(z, mu, w0, w_up, a0, a_up, g_up, k_k, k_a):
    B, T, _ = z.shape
    z = z + mu * (centred_shift(z) - z)
    r, k, v, wd, ad, gd = jnp.split(z, _cuts(RWKV_SPLITS), axis=-1)
    wd = wd.reshape(B, T, 2, RWKV_DECAY_RANK)
    ad = ad.reshape(B, T, 2, RWKV_ICLR_RANK)
    w_raw = (w0 + jnp.einsum('btzr,zrc->btzc', jnp.tanh(wd), w_up)).astype(jnp.float32)
    decay = jnp.exp(-jnp.exp(-jax.nn.softplus(-w_raw) - 0.5))
    a_lr = jax.nn.sigmoid((a0 + jnp.einsum('btzr,zrc->btzc', ad, a_up)).astype(jnp.float32))
    g = jax.nn.sigmoid(gd) @ g_up
    kkf = (k * k_k).astype(jnp.float32).reshape(B, T, RWKV_HEADS, RWKV_DH)
    kk = kkf / jnp.maximum(jnp.sqrt(jnp.sum(kkf * kkf, axis=-1, keepdims=True)), 1e-12)
    kk = kk.reshape(B, T, RWKV_W)
    k_dir = k[:, :, None, :] * (1 + (a_lr - 1) * k_a)
    b_vec = kk[:, :, None, :] * a_lr
    return r, k, v, decay, kk, k_dir, b_vec, g


def rwkv_run(r, v, decay, kk, k_dir, b_vec, S0):
    B, T = r.shape[:2]
    heads = lambda a: a.reshape(B, T, RWKV_HEADS, RWKV_DH)
    both = lambda a: jnp.moveaxis(jnp.stack([heads(a), jnp.flip(heads(a), 1)], 0), 2, 0)
    per = lambda a: jnp.moveaxis(jnp.stack([heads(a[:, :, 0]), jnp.flip(heads(a[:, :, 1]), 1)], 0), 2, 0)
    xs = (both(r), per(decay), per(k_dir), both(v), both(-kk), per(b_vec))

    def step(S, inp):
        r_t, w_t, k_t, v_t, a_t, b_t = inp
        sa = jnp.einsum('zbhij,zbhj->zbhi', S, a_t)
        S = S * w_t[..., None, :] + sa[..., :, None] * b_t[..., None, :] + v_t[..., :, None] * k_t[..., None, :]
        return S, jnp.einsum('zbhij,zbhj->zbhi', S, r_t)

    S, y = lax.scan(step, S0, xs)
    y = y[:, 0] + jnp.flip(y[:, 1], 0)
    return jnp.moveaxis(y, 0, 1), S


def rwkv_out(y, r, k, v, g, r_k, ln_w, ln_b):
    B, T = r.shape[:2]
    heads = lambda a: a.reshape(B, T, RWKV_HEADS, RWKV_DH)
    yf = y.astype(jnp.float32)
    mean = jnp.mean(yf, axis=-1, keepdims=True)
    var = jnp.mean(jnp.square(yf - mean), axis=-1, keepdims=True)
    yn = ((yf - mean) * lax.rsqrt(var + RWKV_LN_EPS)).astype(r.dtype)
    yn = yn * ln_w.reshape(RWKV_HEADS, RWKV_DH) + ln_b.reshape(RWKV_HEADS, RWKV_DH)
    bonus = jnp.sum(heads(r) * heads(k) * r_k, axis=-1, keepdims=True) * heads(v)
    return (yn + bonus).reshape(B, T, RWKV_W) * g


def _linear_combine(e1, e2):
    a1, b1 = e1
    a2, b2 = e2
    return a1 * a2, a2 * b1 + b2


def rglru(u_pre, gate_pre, conv_w, conv_b, gate_w, gate_b, lam, h0):
    B, T, W = u_pre.shape
    u = lax.conv_general_dilated(u_pre, conv_w[:, None, :], (1,), [LRU_CONV_PAD],
                                 dimension_numbers=('NWC', 'WIO', 'NWC'), feature_group_count=W) + conv_b
    gates = jnp.einsum('btnc,zgncd->btzgnd', u.reshape(B, T, LRU_BLOCKS, LRU_BW), gate_w).reshape(B, T, 2, 2, W)
    gates = jax.nn.sigmoid((gates + gate_b).astype(jnp.float32))
    rg, ig = gates[:, :, :, 0], gates[:, :, :, 1]
    log_a = -LRU_C * rg * jax.nn.softplus(-lam.astype(jnp.float32))
    a = jnp.exp(log_a)
    xin = jnp.sqrt(-jnp.expm1(2 * log_a)) * ig * u[:, :, None, :]
    per = lambda t: jnp.stack([t[:, :, 0], jnp.flip(t[:, :, 1], 1)], 0)
    a2, x2 = per(a), per(xin)
    x2 = x2.at[:, :, 0].add(a2[:, :, 0] * h0)
    _, h = lax.associative_scan(_linear_combine, (a2, x2), axis=2)
    y = h[0] + jnp.flip(h[1], 1)
    return y.astype(u_pre.dtype) * jax.nn.gelu(gate_pre), h[:, :, -1]


def _na_column_tables():
    n_cb = GRID_W // NA_COL_BLOCK
    qcol = np.arange(GRID_W).reshape(n_cb, NA_COL_BLOCK)
    band0 = np.clip(qcol[:, 0] - NA_KW // 2, 0, GRID_W - NA_COL_BAND)
    kcol = band0[:, None] + np.arange(NA_COL_BAND)
    win0 = np.clip(qcol - NA_KW // 2, 0, GRID_W - NA_KW)
    rel = kcol[:, None, :] - win0[:, :, None]
    in_win = (rel >= 0) & (rel < NA_KW)
    dcol = np.clip(kcol[:, None, :] - qcol[:, :, None] + NA_KW - 1, 0, 2 * NA_KW - 2)
    return kcol, in_win, dcol


def neighbourhood_attention(q, k, v, k_ctx, v_ctx, rpb):
    B, T, H, dh = q.shape
    rows = T // GRID_W
    kh = min(NA_KH, rows)
    n_cb = GRID_W // NA_COL_BLOCK
    kcol, in_win, dcol = _na_column_tables()
    in_win = jnp.asarray(in_win)[:, :, None, :]
    dcol = jnp.asarray(dcol)[None]
    qg = (q * dh ** -0.5).reshape(B, rows, GRID_W, H, dh)
    kg = k.reshape(B, rows, GRID_W, H, dh)
    vg = v.reshape(B, rows, GRID_W, H, dh)
    n_loc = kh * NA_COL_BAND

    def row_block(r):
        r0 = jnp.clip(r - kh // 2, 0, rows - kh)
        q_r = lax.dynamic_index_in_dim(qg, r, axis=1, keepdims=False).reshape(B, n_cb, NA_COL_BLOCK, H, dh)
        k_b = lax.dynamic_slice_in_dim(kg, r0, kh, axis=1)[:, :, kcol]
        v_b = lax.dynamic_slice_in_dim(vg, r0, kh, axis=1)[:, :, kcol]
        s_loc = jnp.einsum('bnqhd,bknjhd->bhnqkj', q_r, k_b).astype(jnp.float32)
        drow = (r0 + jnp.arange(kh) - r + NA_KH - 1)[:, None, None, None]
        bias = jnp.moveaxis(rpb[:, drow, dcol], 1, 3).astype(jnp.float32)
        s_loc = jnp.where(in_win, s_loc + bias, -jnp.inf)
        s_ctx = jnp.einsum('bnqhd,bmhd->bhnqm', q_r, k_ctx).astype(jnp.float32)
        s = jnp.concatenate([s_loc.reshape(B, H, n_cb, NA_COL_BLOCK, n_loc), s_ctx], axis=-1)
        p = jax.nn.softmax(s, axis=-1).astype(v.dtype)
        p_loc = p[..., :n_loc].reshape(B, H, n_cb, NA_COL_BLOCK, kh, NA_COL_BAND)
        o = jnp.einsum('bhnqkj,bknjhd->bnqhd', p_loc, v_b) + jnp.einsum('bhnqm,bmhd->bnqhd', p[..., n_loc:], v_ctx)
        return o.reshape(B, GRID_W, H * dh)

    out = lax.map(row_block, jnp.arange(rows))
    return jnp.moveaxis(out, 0, 1).reshape(B, T, H * dh)


def context_attention(q, k, v):
    B, Tc, H, dh = q.shape
    s = jnp.einsum('bqhd,bkhd->bhqk', q * dh ** -0.5, k).astype(jnp.float32)
    p = jax.nn.softmax(s, axis=-1).astype(v.dtype)
    return jnp.einsum('bhqk,bkhd->bqhd', p, v).reshape(B, Tc, H * dh)


def even_mixer(hx, hc, rope, ctx_out, in_w, out_w, gate_b, norm_w, mu, w0, w_up, a0, a_up, g_up, k_k, k_a,
               r_k, ln_w, ln_b):
    B = hx.shape[0]
    f32 = jnp.float32
    zx, zc = hx @ in_w, hc @ in_w
    mq_c, mk_c, mv_c, li_c, lf_c, og_c = mlstm_inputs(zc[..., :MLSTM_IN], gate_b, None)
    mq_x, mk_x, mv_x, li_x, lf_x, og_x = mlstm_inputs(zx[..., :MLSTM_IN], gate_b, rope)
    st0 = (jnp.zeros((2, B, MLSTM_HEADS, MLSTM_DH, MLSTM_DH), f32),
           jnp.zeros((2, B, MLSTM_HEADS, MLSTM_DH), f32),
           jnp.zeros((2, B, MLSTM_HEADS), f32))
    mh_c, st_c = mlstm_run(mq_c, mk_c, mv_c, li_c, lf_c, st0)
    mh_x, _ = mlstm_run(mq_x, mk_x, mv_x, li_x, lf_x, st_c)
    r_c, k_c, v_c, w_c, kk_c, kd_c, b_c, g_c = rwkv_inputs(zc[..., MLSTM_IN:], mu, w0, w_up, a0, a_up, g_up, k_k, k_a)
    r_x, k_x, v_x, w_x, kk_x, kd_x, b_x, g_x = rwkv_inputs(zx[..., MLSTM_IN:], mu, w0, w_up, a0, a_up, g_up, k_k, k_a)
    S0 = jnp.zeros((2, B, RWKV_HEADS, RWKV_DH, RWKV_DH), f32)
    ry_c, S_c = rwkv_run(r_c, v_c, w_c, kk_c, kd_c, b_c, S0)
    ry_x, _ = rwkv_run(r_x, v_x, w_x, kk_x, kd_x, b_x, S_c)
    ox = jnp.concatenate([mlstm_out(mh_x, og_x, norm_w),
                          rwkv_out(ry_x, r_x, k_x, v_x, g_x, r_k, ln_w, ln_b)], axis=-1) @ out_w
    if not ctx_out:
        return ox, None
    oc = jnp.concatenate([mlstm_out(mh_c, og_c, norm_w),
                          rwkv_out(ry_c, r_c, k_c, v_c, g_c, r_k, ln_w, ln_b)], axis=-1) @ out_w
    return ox, oc


def odd_mixer(hx, hc, ctx_out, in_w, out_w, conv_w, conv_b, gate_w, gate_b, lam, rpb):
    B = hx.shape[0]
    cuts = _cuts(ODD_SPLITS)
    ux, gx, qx, kx, vx = jnp.split(hx @ in_w, cuts, axis=-1)
    uc, gc, qc, kc, vc = jnp.split(hc @ in_w, cuts, axis=-1)
    yc, h_c = rglru(uc, gc, conv_w, conv_b, gate_w, gate_b, lam, jnp.zeros((2, B, LRU_W), jnp.float32))
    yx, _ = rglru(ux, gx, conv_w, conv_b, gate_w, gate_b, lam, h_c)
    heads = lambda a: a.reshape(a.shape[0], a.shape[1], NA_HEADS, NA_DH)
    ax = neighbourhood_attention(heads(qx), heads(kx), heads(vx), heads(kc), heads(vc), rpb)
    ox = jnp.concatenate([yx, ax], axis=-1) @ out_w
    if not ctx_out:
        return ox, None
    ac = context_attention(heads(qc), heads(kc), heads(vc))
    return ox, jnp.concatenate([yc, ac], axis=-1) @ out_w


def setup_inputs(seed: int = 0) -> dict:
    key = jax.random.key(seed)
    ks = iter(jax.random.split(key, 40))
    nrm = lambda shape, std: std * jax.random.normal(next(ks), shape, jnp.float32)
    D = D_MODEL
    x = nrm((BATCH, SEQ, D), 1.0)
    c = nrm((BATCH, D), 1.0)
    ctx = nrm((BATCH, CTX_LEN, D), 1.0)
    c_ctx = nrm((D,), 1.0)
    ada_w = nrm((DEPTH, D, 6 * D), 0.5 * D ** -0.5)
    ada_b = nrm((DEPTH, 6 * D), 0.02)
    mix_out_w = nrm((DEPTH, MIX_W, D), MIX_W ** -0.5)
    mlp_w1 = nrm((DEPTH, D, D_FF), D ** -0.5)
    mlp_w2 = nrm((DEPTH, D_FF, D), D_FF ** -0.5)
    ev_in_w = nrm((N_EVEN, D, EVEN_IN), D ** -0.5)
    f_bias = jnp.linspace(3.0, 6.0, MLSTM_HEADS, dtype=jnp.float32)
    ml_gate_b = jnp.stack([nrm((N_EVEN, 2, MLSTM_HEADS), 0.1),
                           f_bias + nrm((N_EVEN, 2, MLSTM_HEADS), 0.1)], axis=2)
    ml_norm_w = 1.0 + nrm((N_EVEN, MLSTM_W), 0.02)
    rw_mu = jax.random.uniform(next(ks), (N_EVEN, RWKV_IN), jnp.float32, 0.0, 1.0)
    rw_w0 = jnp.linspace(-6.5, -1.5, RWKV_W, dtype=jnp.float32) + nrm((N_EVEN, 2, RWKV_W), 0.1)
    rw_w_up = nrm((N_EVEN, 2, RWKV_DECAY_RANK, RWKV_W), 0.1 * RWKV_DECAY_RANK ** -0.5)
    rw_a0 = nrm((N_EVEN, 2, RWKV_W), 0.1)
    rw_a_up = nrm((N_EVEN, 2, RWKV_ICLR_RANK, RWKV_W), 0.1 * RWKV_ICLR_RANK ** -0.5)
    rw_g_up = nrm((N_EVEN, RWKV_GATE_RANK, RWKV_W), RWKV_GATE_RANK ** -0.5)
    rw_k_k = 0.85 + nrm((N_EVEN, RWKV_W), 0.02)
    rw_k_a = 1.0 + nrm((N_EVEN, RWKV_W), 0.02)
    rw_r_k = nrm((N_EVEN, RWKV_HEADS, RWKV_DH), 0.1)
    rw_ln_w = 1.0 + nrm((N_EVEN, RWKV_W), 0.02)
    rw_ln_b = nrm((N_EVEN, RWKV_W), 0.02)
    od_in_w = nrm((N_ODD, D, ODD_IN), D ** -0.5)
    lru_conv_w = nrm((N_ODD, LRU_CONV, LRU_W), LRU_CONV ** -0.5)
    lru_conv_b = nrm((N_ODD, LRU_W), 0.02)
    lru_gate_w = nrm((N_ODD, 2, 2, LRU_BLOCKS, LRU_BW, LRU_BW), LRU_BW ** -0.5)
    lru_gate_b = nrm((N_ODD, 2, 2, LRU_W), 0.02)
    a_c = jax.random.uniform(next(ks), (N_ODD, 2, LRU_W), jnp.float32, 0.9, 0.999)
    s = a_c ** (1.0 / LRU_C)
    lru_lambda = jnp.log(s) - jnp.log1p(-s)
    na_rpb = nrm((N_ODD, NA_HEADS, 2 * NA_KH - 1, 2 * NA_KW - 1), 0.1)
    final_norm_w = 1.0 + nrm((D,), 0.02)
    return {'x': x, 'c': c, 'ctx': ctx, 'c_ctx': c_ctx, 'ada_w': ada_w, 'ada_b': ada_b,
            'mix_out_w': mix_out_w, 'mlp_w1': mlp_w1, 'mlp_w2': mlp_w2, 'ev_in_w': ev_in_w,
            'ml_gate_b': ml_gate_b, 'ml_norm_w': ml_norm_w, 'rw_mu': rw_mu, 'rw_w0': rw_w0,
            'rw_w_up': rw_w_up, 'rw_a0': rw_a0, 'rw_a_up': rw_a_up, 'rw_g_up': rw_g_up,
            'rw_k_k': rw_k_k, 'rw_k_a': rw_k_a, 'rw_r_k': rw_r_k, 'rw_ln_w': rw_ln_w, 'rw_ln_b': rw_ln_b,
            'od_in_w': od_in_w, 'lru_conv_w': lru_conv_w, 'lru_conv_b': lru_conv_b,
            'lru_gate_w': lru_gate_w, 'lru_gate_b': lru_gate_b, 'lru_lambda': lru_lambda,
            'na_rpb': na_rpb, 'final_norm_w': final_norm_w}


def reference(x, c, ctx, c_ctx, ada_w, ada_b, mix_out_w, mlp_w1, mlp_w2, ev_in_w, ml_gate_b, ml_norm_w,
              rw_mu, rw_w0, rw_w_up, rw_a0, rw_a_up, rw_g_up, rw_k_k, rw_k_a, rw_r_k, rw_ln_w, rw_ln_b,
              od_in_w, lru_conv_w, lru_conv_b, lru_gate_w, lru_gate_b, lru_lambda, na_rpb, final_norm_w):
    rope = axial_rope(x.shape[1], MLSTM_DH, x.dtype)
    h_lat, h_ctx = x, ctx
    for layer in range(DEPTH):
        last = layer == DEPTH - 1
        sx1, cx1, gx1, sx2, cx2, gx2 = ada_modulation(c, ada_w[layer], ada_b[layer])
        sc1, cc1, gc1, sc2, cc2, gc2 = ada_modulation(c_ctx[None], ada_w[layer], ada_b[layer])
        nx = modulate(rms_norm(h_lat), sx1, cx1)
        nc = modulate(rms_norm(h_ctx), sc1, cc1)
        if layer % 2 == 0:
            e = layer // 2
            ox, oc = even_mixer(nx, nc, rope, not last, ev_in_w[e], mix_out_w[layer], ml_gate_b[e], ml_norm_w[e],
                                rw_mu[e], rw_w0[e], rw_w_up[e], rw_a0[e], rw_a_up[e], rw_g_up[e], rw_k_k[e],
                                rw_k_a[e], rw_r_k[e], rw_ln_w[e], rw_ln_b[e])
        else:
            o = layer // 2
            ox, oc = odd_mixer(nx, nc, not last, od_in_w[o], mix_out_w[layer], lru_conv_w[o], lru_conv_b[o],
                               lru_gate_w[o], lru_gate_b[o], lru_lambda[o], na_rpb[o])
        h_lat = h_lat + gx1 * ox
        h_lat = h_lat + gx2 * sq_relu_mlp(modulate(rms_norm(h_lat), sx2, cx2), mlp_w1[layer], mlp_w2[layer])
        if not last:
            h_ctx = h_ctx + gc1 * oc
            h_ctx = h_ctx + gc2 * sq_relu_mlp(modulate(rms_norm(h_ctx), sc2, cc2), mlp_w1[layer], mlp_w2[layer])
    return rms_norm(h_lat) * final_norm_w
```

```python
import contextlib
import math
import numpy as np
import concourse.bass as bass
import concourse.mybir as mybir
from concourse.ap import AP
from concourse.bass_utils import run_bass_kernel_spmd

F32 = mybir.dt.float32
AF = mybir.ActivationFunctionType
ALU = mybir.AluOpType
AX = mybir.AxisListType

NCORES = 8
D = 1024
DEPTH = 4
NB = 2
CTX = 256
TL = 2048
TT = CTX + TL
NT = NB * TT
TN = 384
NTT = TT // TN
DFF = 4096
EVEN_IN = 3984
ODD_IN = 2560
ML_IN = 2064
RW_IN = 1920
EPS = 1e-6


class Buf:
    __slots__ = ("w", "r", "name")

    def __init__(self, name=""):
        self.w = {}
        self.r = {}
        self.name = name


class V:
    __slots__ = ("ap", "bufs")

    def __init__(self, ap, bufs):
        self.ap = ap
        self.bufs = bufs if isinstance(bufs, tuple) else (bufs,)

    def __getitem__(self, key):
        return V(self.ap[key], self.bufs)

    def rearrange(self, pattern, **kw):
        return V(self.ap.rearrange(pattern, **kw), self.bufs)

    def with_ap(self, ap):
        return V(ap, self.bufs)

    def raw(self, off, dims):
        return V(AP(self.ap.tensor, self.ap.offset + off, [list(d) for d in dims]), self.bufs)

    def ins(self, axis, n):
        dims = [list(d) for d in self.ap.ap]
        dims.insert(axis, [0, n])
        return V(AP(self.ap.tensor, self.ap.offset, dims), self.bufs)

    def pbc(self, n=128):
        return V(self.ap.partition_broadcast(n), self.bufs)

    @property
    def shape(self):
        return self.ap.shape


HMAP = {"pe": "tensor", "dve": "vector", "act": "scalar", "pool": "gpsimd", "sp": "sync"}


class Prog:
    NDSEM = 10

    def __init__(self, nc):
        self.nc = nc
        self.es = contextlib.ExitStack()
        self.engs = ["pe", "dve", "act", "pool", "sp"]
        self.h = {e: getattr(nc, HMAP[e]) for e in self.engs}
        self.sem = {e: self.es.enter_context(nc.semaphore("s_" + e)) for e in self.engs}
        self.cnt = {e: 0 for e in self.engs}
        self.seen = {e: {} for e in self.engs}
        self.dq = ["sp", "act", "pool"]
        self.dsem = {}
        self.dcount = {q: 0 for q in self.dq}
        for q in self.dq:
            for i in range(self.NDSEM):
                self.dsem[(q, i)] = self.es.enter_context(nc.semaphore(f"d_{q}_{i}"))
        self.nparts = {}
        self.nwaits = 0
        self.nops = 0
        self.scopes = []

    @contextlib.contextmanager
    def scope(self):
        es = contextlib.ExitStack()
        self.scopes.append(es)
        try:
            yield
        finally:
            self.barrier()
            self.scopes.pop()
            es.close()

    def _es(self):
        return self.scopes[-1] if self.scopes else self.es

    def _nm(self, name):
        self.uid = getattr(self, "uid", 0) + 1
        return f"{name}_{self.uid}"

    def tile(self, name, shape, dtype=F32):
        t = self._es().enter_context(self.nc.sbuf_tensor(self._nm("t_" + name), list(shape), dtype))
        return V(t[:], Buf(name))

    def psum(self, name, shape, dtype=F32):
        t = self._es().enter_context(self.nc.psum_tensor(self._nm("p_" + name), list(shape), dtype))
        return V(t[:], Buf(name))

    def dram(self, name, shape, dtype=F32, kind="Internal"):
        t = self.nc.dram_tensor("d_" + name, list(shape), dtype, kind=kind)
        return V(t.ap(), Buf(name))

    def part(self, v, key):
        k = (id(v.bufs[0]), key)
        if k not in self.nparts:
            self.nparts[k] = Buf(str(key))
        return V(v.ap, self.nparts[k])

    def _semh(self, k):
        return self.sem[k] if isinstance(k, str) else self.dsem[k]

    def _wait(self, eng, tok):
        if tok is None:
            return
        semkey, val, src = tok
        if src == eng and eng == "pe":
            return
        if self.seen[eng].get(semkey, 0) >= val:
            return
        self.seen[eng][semkey] = val
        self.h[eng].wait_ge(self._semh(semkey), val)
        self.nwaits += 1

    def _deps(self, eng, reads, writes):
        for v in reads:
            for b in v.bufs:
                for sk, (val, src) in b.w.items():
                    self._wait(eng, (sk, val, src))
        for v in writes:
            for b in v.bufs:
                for sk, (val, src) in b.w.items():
                    self._wait(eng, (sk, val, src))
                for sk, (val, src) in b.r.items():
                    self._wait(eng, (sk, val, src))

    def _mark(self, tok, reads, writes):
        wb = set()
        for v in writes:
            for b in v.bufs:
                b.w[tok[0]] = (tok[1], tok[2])
                b.r = {}
                wb.add(id(b))
        for v in reads:
            for b in v.bufs:
                if id(b) in wb:
                    continue
                b.r[tok[0]] = (tok[1], tok[2])

    def op(self, eng, method, **kw):
        reads, writes = [], []
        kw2 = {}
        for k, a in kw.items():
            if isinstance(a, V):
                (writes if k in ("out", "accum_out", "ap") else reads).append(a)
                kw2[k] = a.ap
            else:
                kw2[k] = a
        self._deps(eng, reads, writes)
        self.cnt[eng] += 1
        tok = (eng, self.cnt[eng], eng)
        getattr(self.h[eng], method)(**kw2).then_inc(self.sem[eng], 1)
        self._mark(tok, reads, writes)
        self.nops += 1
        return tok

    def dma(self, q, out, in_, **kw):
        k = self.dcount[q]
        self.dcount[q] += 1
        idx = k % self.NDSEM
        rnd = k // self.NDSEM
        semkey = (q, idx)
        if rnd > 0:
            self._wait(q, (semkey, 16 * rnd, "dma"))
        self._deps(q, [in_], [out])
        tok = (semkey, 16 * (rnd + 1), "dma")
        self.h[q].dma_start(out=out.ap, in_=in_.ap, **kw).then_inc(self.dsem[semkey], 16)
        self._mark(tok, [in_], [out])
        return tok

    def pe(self, method, **kw):
        return self.op("pe", method, **kw)

    def dve(self, method, **kw):
        return self.op("dve", method, **kw)

    def act(self, method, **kw):
        return self.op("act", method, **kw)

    def pool(self, method, **kw):
        return self.op("pool", method, **kw)

    def mm(self, out, lhsT, rhs, start=True, stop=True):
        return self.op("pe", "matmul", out=out, lhsT=lhsT, rhs=rhs, start=start, stop=stop)

    def barrier(self, engs=None):
        engs = engs or self.engs
        for e in engs:
            for q in self.dq:
                k = self.dcount[q]
                for idx in range(self.NDSEM):
                    n = (k - idx + self.NDSEM - 1) // self.NDSEM if k > idx else 0
                    if n > 0:
                        self._wait(e, ((q, idx), 16 * n, "dma"))
            for e2 in ["pe", "dve", "act", "pool"]:
                if e2 != e and self.cnt[e2] > 0:
                    self._wait(e, (e2, self.cnt[e2], e2))

    def finish(self):
        self.barrier(["sp"])
        self.es.close()


def segs(b, lo, n):
    hi = lo + n
    out = []
    if lo < CTX:
        out.append((0, min(hi, CTX) - lo, 2))
    if hi > CTX:
        out.append((max(lo, CTX) - lo, n, b))
    return out


def phase_mod(P, C):
    M = C.M
    with P.scope():
        cs = P.tile("cs", [128, 8, 3])
        P.dma("sp", cs, C.condT)
        P.act("activation", out=cs, in_=cs, func=AF.Silu)
        bt = P.tile("adab", [128, 4, 48])
        P.dma("sp", bt, C.ada_bT.rearrange("l p f -> p l f"))
        wb = [P.tile(f"mw{i}", [128, 8, 512]) for i in range(2)]
        ps = [P.psum(f"mps{i}", [128, 4, 4]) for i in range(2)]
        it = 0
        for l in range(DEPTH):
            W3 = C.ada_w[l].rearrange("(kc p) f -> p kc f", p=128)
            for g in range(12):
                w = wb[it % 2]
                pp = ps[it % 2]
                P.dma("sp" if it % 2 == 0 else "pool", w, W3[:, :, g * 512:(g + 1) * 512])
                for f4 in range(4):
                    for kc in range(8):
                        P.mm(pp[:, f4, 0:3], lhsT=w[:, kc, f4 * 128:(f4 + 1) * 128], rhs=cs[:, kc, :],
                             start=(kc == 0), stop=(kc == 7))
                P.dve("tensor_tensor", out=M[:, l, g * 4:(g + 1) * 4, :], in0=pp[:, :, 0:3],
                      in1=bt[:, l, g * 4:(g + 1) * 4].ins(2, 3), op=ALU.add)
                it += 1
            for j in (1, 4):
                P.dve("tensor_scalar", out=M[:, l, j * 8:(j + 1) * 8, :], in0=M[:, l, j * 8:(j + 1) * 8, :],
                      scalar1=1.0, scalar2=None, op0=ALU.add)


def norm_tile(P, C, src3, g0, n, Ht, SQ, pss, rt, out3, kcs=8):
    P.dma("sp", Ht[:, :, 0:n], src3[:, :, g0:g0 + n])
    P.act("activation", out=SQ[:, :, 0:n], in_=Ht[:, :, 0:n], func=AF.Square)
    for kc in range(8):
        P.mm(pss[:, 0:n], lhsT=C.ones, rhs=SQ[:, kc, 0:n], start=(kc == 0), stop=(kc == 7))
    P.act("activation", out=rt[:, 0:n], in_=pss[:, 0:n], func=AF.Sqrt, scale=1.0 / D, bias=C.epsc)
    P.dve("reciprocal", out=rt[:, 0:n], in_=rt[:, 0:n])
    P.dve("tensor_tensor", out=out3, in0=Ht[:, :, 0:n], in1=rt[:, 0:n].ins(1, 8), op=ALU.mult)


def modulate(P, C, l, jsh, jsc, b, lo, n, x3):
    for (s0, s1, col) in segs(b, lo, n):
        for kc in range(8):
            P.act("activation", out=x3[:, kc, s0:s1], in_=x3[:, kc, s0:s1], func=AF.Identity,
                  scale=C.M[:, l, jsc * 8 + kc, col:col + 1], bias=C.M[:, l, jsh * 8 + kc, col:col + 1])


def phase_normmod(P, C, src, b, l, jsh, jsc, XT):
    src3 = src.rearrange("(kc p) t -> p kc t", p=128)
    with P.scope():
        Ht = [P.tile(f"nm_h{i}", [128, 8, TN]) for i in range(2)]
        SQ = [P.tile(f"nm_sq{i}", [128, 8, TN]) for i in range(2)]
        rt = [P.tile(f"nm_r{i}", [128, TN]) for i in range(2)]
        pss = [P.psum(f"nm_ps{i}", [128, 512]) for i in range(2)]
        for tt in range(NTT):
            i = tt % 2
            x3 = XT[:, :, tt * TN:(tt + 1) * TN]
            norm_tile(P, C, src3, b * TT + tt * TN, TN, Ht[i], SQ[i], pss[i], rt[i], x3)
            modulate(P, C, l, jsh, jsc, b, tt * TN, TN, x3)


def lin_fm(P, C, XT, KC, Wd, col0, ncols, handler, GW=512, ntt=NTT, tn=TN, name="lf"):
    W3 = Wd.rearrange("(kc p) n -> p kc n", p=128)
    with P.scope():
        wb = [P.tile(f"{name}_w{i}", [128, KC, GW]) for i in range(2)]
        pss = [P.psum(f"{name}_ps{i}", [128, 512]) for i in range(3)]
        k = 0
        ng = (ncols + GW - 1) // GW
        for g in range(ng):
            c0 = col0 + g * GW
            gw = min(GW, col0 + ncols - c0)
            w = wb[g % 2]
            P.dma("sp", w[:, :, 0:gw], W3[:, :, c0:c0 + gw])
            for oc in range((gw + 127) // 128):
                m = min(128, gw - oc * 128)
                for tt in range(ntt):
                    ps = pss[k % 3]
                    k += 1
                    for kc in range(KC):
                        P.mm(ps[0:m, 0:tn], lhsT=w[:, kc, oc * 128:oc * 128 + m],
                             rhs=XT[:, kc, tt * tn:(tt + 1) * tn], start=(kc == 0), stop=(kc == KC - 1))
                    handler(c0 + oc * 128, m, tt, ps)


EXT_SHAPES = {
    "xT": [D, NT], "condT": [128, 8, 3], "ada_w": [DEPTH, D, 6 * D], "ada_bT": [DEPTH, 128, 48],
    "mix_out_w": [DEPTH, D, D], "mlp_w1": [DEPTH, D, DFF], "mlp_w2": [DEPTH, DFF, D],
    "ev_in_w": [2, D, EVEN_IN], "od_in_w": [2, D, ODD_IN], "consts": [128, 4, 128],
    "fnwT": [128, 8],
}


class Ctx:
    def __init__(self, nc, P):
        self.nc = nc
        self.P = P
        self._ext = {}

    def ext(self, name):
        if name not in self._ext:
            t = self.nc.dram_tensor(name, list(EXT_SHAPES[name]), F32, kind="ExternalInput")
            self._ext[name] = V(t.ap(), Buf(name))
        return self._ext[name]

    def __getattr__(self, name):
        if name in EXT_SHAPES:
            return self.ext(name)
        raise AttributeError(name)


def build(dbg=None):
    nc = bass.Bass("TRN2", target_bir_lowering=False)
    P = Prog(nc)
    C = Ctx(nc, P)
    C.dbg = {}
    for nm, shp in (dbg or {}).items():
        C.dbg[nm] = V(nc.dram_tensor("dbg_" + nm, list(shp), F32, kind="ExternalOutput").ap(), Buf(nm))
    cst = P.tile("cst", [128, 4, 128])
    P.dma("sp", cst, C.consts)
    C.ident = cst[:, 0, :]
    C.ones = cst[:, 1, :]
    C.blk64 = cst[:, 2, :]
    C.negones = cst[:, 3, :]
    C.epsc = P.tile("epsc", [128, 1])
    P.dve("memset", ap=C.epsc, constant=EPS)
    C.M = P.tile("M", [128, DEPTH, 48, 3])
    return nc, P, C


def phase_outproj(P, C, MIXT, b, l, hsrc, hdst, XT):
    m3 = MIXT.rearrange("(kc p) t -> p kc t", p=128)
    for kc in range(8):
        P.dma("sp" if kc % 2 == 0 else "pool", XT[:, kc, :], m3[:, kc, b * TT:(b + 1) * TT])
    with P.scope():
        ho = [P.tile(f"op_ho{i}", [128, TN]) for i in range(3)]
        hn = [P.tile(f"op_hn{i}", [128, TN]) for i in range(3)]
        cnt = [0]

        def handler(col, m, tt, ps):
            i = cnt[0] % 3
            cnt[0] += 1
            oc = col // 128
            g0 = b * TT + tt * TN
            key = ("h", oc, b, tt)
            P.dma("sp", ho[i], P.part(hsrc, key)[col:col + 128, g0:g0 + TN])
            for (s0, s1, cc) in segs(b, tt * TN, TN):
                P.dve("scalar_tensor_tensor", out=hn[i][:, s0:s1], in0=ps[:, s0:s1],
                      scalar=C.M[:, l, 16 + oc, cc:cc + 1], in1=ho[i][:, s0:s1], op0=ALU.mult, op1=ALU.add)
            P.dma("pool", P.part(hdst, key)[col:col + 128, g0:g0 + TN], hn[i])

        lin_fm(P, C, XT, 8, C.mix_out_w[l], 0, D, handler, name="op")


def phase_mlp(P, C, H, b, l):
    src3 = H.rearrange("(kc p) t -> p kc t", p=128)
    with P.scope():
        Ht = P.tile("ml_h", [128, 8, TN])
        SQ = P.tile("ml_sq", [128, 8, TN])
        Xt = P.tile("ml_x", [128, 8, TN])
        rt = P.tile("ml_r", [128, TN])
        pss = P.psum("ml_ps", [128, 512])
        H1 = P.tile("ml_h1", [128, 32, TN])
        hn = [P.tile(f"ml_hn{i}", [128, TN]) for i in range(2)]
        cnt = [0]
        for tt in range(NTT):
            g0 = b * TT + tt * TN
            for kc in range(8):
                pass
            hv = V(src3.ap, tuple(P.part(H, ("h", kc, b, tt)).bufs[0] for kc in range(8)))
            norm_tile(P, C, hv, g0, TN, Ht, SQ, pss, rt, Xt)
            modulate(P, C, l, 3, 4, b, tt * TN, TN, Xt)

            def h1(col, m, _t, ps):
                oc = col // 128
                P.act("activation", out=H1[:, oc, :], in_=ps[:, 0:TN], func=AF.Relu)
                P.pool("tensor_tensor", out=H1[:, oc, :], in0=H1[:, oc, :], in1=H1[:, oc, :], op=ALU.mult)

            lin_fm(P, C, Xt, 8, C.mlp_w1[l], 0, DFF, h1, ntt=1, name="m1")

            def h2(col, m, _t, ps):
                i = cnt[0] % 2
                cnt[0] += 1
                oc = col // 128
                for (s0, s1, cc) in segs(b, tt * TN, TN):
                    P.dve("scalar_tensor_tensor", out=hn[i][:, s0:s1], in0=ps[:, s0:s1],
                          scalar=C.M[:, l, 40 + oc, cc:cc + 1], in1=Ht[:, oc, s0:s1], op0=ALU.mult, op1=ALU.add)
                P.dma("pool", P.part(H, ("h", oc, b, tt))[col:col + 128, g0:g0 + TN], hn[i])

            lin_fm(P, C, H1, 32, C.mlp_w2[l], 0, D, h2, GW=256, ntt=1, name="m2")


def phase_final(P, C, H):
    src3 = H.rearrange("(kc p) t -> p kc t", p=128)
    o3 = C.outT.rearrange("(kc p) t -> p kc t", p=128)
    with P.scope():
        fw = P.tile("fn_w", [128, 8])
        P.dma("sp", fw, C.fnwT)
        Ht = [P.tile(f"fn_h{i}", [128, 8, TN]) for i in range(2)]
        SQ = [P.tile(f"fn_sq{i}", [128, 8, TN]) for i in range(2)]
        Xo = [P.tile(f"fn_x{i}", [128, 8, TN]) for i in range(2)]
        rt = [P.tile(f"fn_r{i}", [128, TN]) for i in range(2)]
        pss = [P.psum(f"fn_ps{i}", [128, 512]) for i in range(2)]
        k = 0
        for b in range(NB):
            for lo in range(CTX, TT, TN):
                n = min(TN, TT - lo)
                i = k % 2
                k += 1
                norm_tile(P, C, src3, b * TT + lo, n, Ht[i], SQ[i], pss[i], rt[i], Xo[i][:, :, 0:n])
                P.dve("tensor_tensor", out=Xo[i][:, :, 0:n], in0=Xo[i][:, :, 0:n], in1=fw.ins(2, n), op=ALU.mult)
                P.dma("pool", o3[:, :, b * TL + lo - CTX:b * TL + lo - CTX + n], Xo[i][:, :, 0:n])


def store_fm(P, dst, row_off, b, name):
    st = [P.tile(f"{name}_st{i}", [128, TN]) for i in range(3)]
    cnt = [0]

    def handler(col, m, tt, ps):
        s = st[cnt[0] % 3]
        cnt[0] += 1
        if cnt[0] % 2:
            P.act("activation", out=s[0:m, :], in_=ps[0:m, 0:TN], func=AF.Copy)
        else:
            P.dve("tensor_copy", out=s[0:m, :], in_=ps[0:m, 0:TN])
        g0 = b * TT + tt * TN
        P.dma("pool", dst[row_off + col:row_off + col + m, g0:g0 + TN], s[0:m, :])
    return handler


def lin_tm(P, C, XT, Wd, col0, ncols, dst, b, name="lt"):
    W3 = Wd.rearrange("(kc p) n -> p kc n", p=128)
    with P.scope():
        w = P.tile(f"{name}_w", [128, 8, ncols])
        P.dma("sp", w, W3[:, :, col0:col0 + ncols])
        pss = [P.psum(f"{name}_ps{i}", [128, 512]) for i in range(2)]
        st = [P.tile(f"{name}_st{i}", [128, ncols]) for i in range(2)]
        for t in range(TT // 128):
            ps = pss[t % 2]
            for kc in range(8):
                P.mm(ps[:, 0:ncols], lhsT=XT[:, kc, t * 128:(t + 1) * 128], rhs=w[:, kc, :],
                     start=(kc == 0), stop=(kc == 7))
            s = st[t % 2]
            if t % 2:
                P.act("activation", out=s, in_=ps[:, 0:ncols], func=AF.Copy)
            else:
                P.dve("tensor_copy", out=s, in_=ps[:, 0:ncols])
            P.dma("pool", dst[b * TT + t * 128:b * TT + (t + 1) * 128, 0:ncols], s)


def rev(v, a0, a1):
    d = v.ap.ap
    return v.raw(a1 - 1, [[d[0][0], d[0][1]], [-1, a1 - a0]])


GELU_C = 2.0 * math.sqrt(2.0 / math.pi)


def phase_lru(P, C, ZO, MIXT, b, o):
    with P.scope():
        sv = P.tile("lru_sv", [128, 4, 11])
        P.dma("sp", sv, C.lru_small[o])
        c8 = P.tile("lru_c8", [128, 4, 2])
        P.act("activation", out=c8, in_=sv[:, :, 9:11], func=AF.Exp, scale=-1.0)
        P.dve("tensor_scalar", out=c8, in0=c8, scalar1=1.0, scalar2=None, op0=ALU.add)
        P.act("activation", out=c8, in_=c8, func=AF.Ln)
        P.dve("tensor_scalar", out=c8, in0=c8, scalar1=-8.0, scalar2=None, op0=ALU.mult)
        bd = P.tile("lru_bd", [128, 4, 128])
        nm = ["up", "gp", "u", "rg0", "rg1", "ig0", "ig1", "a", "x", "t", "h0", "h1"]
        T = {n: P.tile("lru_" + n, [128, TT]) for n in nm}
        pss = [P.psum(f"lru_ps{i}", [128, 512]) for i in range(2)]
        up, gp, u, A, X, Tm = T["up"], T["gp"], T["u"], T["a"], T["x"], T["t"]
        k = 0
        for c in range(4):
            P.dma("sp", bd, C.lru_bd[o, :, :, c].rearrange("z g p n -> p (z g) n"))
            P.dma("sp", up, ZO[c * 128:(c + 1) * 128, b * TT:(b + 1) * TT])
            P.dma("pool", gp, ZO[512 + c * 128:512 + (c + 1) * 128, b * TT:(b + 1) * TT])
            P.act("activation", out=u, in_=up, func=AF.Identity, scale=sv[:, c, 1:2], bias=sv[:, c, 4:5])
            for (a0, a1) in ((0, CTX), (CTX, TT)):
                for (kk, off) in ((0, -1), (2, 1), (3, 2)):
                    lo = a0 + max(0, -off)
                    hi = a1 - max(0, off)
                    P.dve("scalar_tensor_tensor", out=u[:, lo:hi], in0=up[:, lo + off:hi + off],
                          scalar=sv[:, c, kk:kk + 1], in1=u[:, lo:hi], op0=ALU.mult, op1=ALU.add)
            for z in range(2):
                for g in range(2):
                    dst = T[("rg" if g == 0 else "ig") + str(z)]
                    for tt in range(NTT):
                        ps = pss[k % 2]
                        k += 1
                        P.mm(ps[:, 0:TN], lhsT=bd[:, z * 2 + g, :], rhs=u[:, tt * TN:(tt + 1) * TN])
                        P.act("activation", out=dst[:, tt * TN:(tt + 1) * TN], in_=ps[:, 0:TN], func=AF.Sigmoid,
                              bias=sv[:, c, 5 + z * 2 + g:6 + z * 2 + g])
            for z in range(2):
                rg, ig, Hh = T["rg" + str(z)], T["ig" + str(z)], T["h" + str(z)]
                P.act("activation", out=A, in_=rg, func=AF.Exp, scale=c8[:, c, z:z + 1])
                P.pool("tensor_tensor", out=Tm, in0=A, in1=A, op=ALU.mult)
                P.dve("tensor_scalar", out=Tm, in0=Tm, scalar1=-1.0, scalar2=1.0, op0=ALU.mult, op1=ALU.add)
                P.act("activation", out=Tm, in_=Tm, func=AF.Sqrt)
                P.pool("tensor_tensor", out=X, in0=Tm, in1=ig, op=ALU.mult)
                P.dve("tensor_tensor", out=X, in0=X, in1=u, op=ALU.mult)
                if z == 0:
                    P.dve("tensor_tensor_scan", out=Hh[:, 0:CTX], data0=A[:, 0:CTX], data1=X[:, 0:CTX],
                          initial=0.0, op0=ALU.mult, op1=ALU.add)
                    P.dve("tensor_tensor_scan", out=Hh[:, CTX:TT], data0=A[:, CTX:TT], data1=X[:, CTX:TT],
                          initial=Hh[:, CTX - 1:CTX], op0=ALU.mult, op1=ALU.add)
                else:
                    P.dve("tensor_tensor_scan", out=rev(Hh, 0, CTX), data0=rev(A, 0, CTX), data1=rev(X, 0, CTX),
                          initial=0.0, op0=ALU.mult, op1=ALU.add)
                    P.dve("tensor_tensor_scan", out=rev(Hh, CTX, TT), data0=rev(A, CTX, TT), data1=rev(X, CTX, TT),
                          initial=Hh[:, 0:1], op0=ALU.mult, op1=ALU.add)
            Y = T["h0"]
            P.dve("tensor_tensor", out=Y, in0=T["h0"], in1=T["h1"], op=ALU.add)
            P.pool("tensor_tensor", out=Tm, in0=gp, in1=gp, op=ALU.mult)
            P.dve("tensor_scalar", out=Tm, in0=Tm, scalar1=0.044715, scalar2=1.0, op0=ALU.mult, op1=ALU.add)
            P.pool("tensor_tensor", out=Tm, in0=Tm, in1=gp, op=ALU.mult)
            P.act("activation", out=Tm, in_=Tm, func=AF.Sigmoid, scale=GELU_C)
            P.pool("tensor_tensor", out=Tm, in0=Tm, in1=gp, op=ALU.mult)
            P.dve("tensor_tensor", out=Y, in0=Y, in1=Tm, op=ALU.mult)
            P.dma("pool", MIXT[c * 128:(c + 1) * 128, b * TT:(b + 1) * TT], Y)


def phase_na_table(P, C, o, TB2):
    with P.scope():
        R = P.tile("nt_r", [31, 2, 8, 14])
        for krp in range(2):
            P.dma("sp", R[:, krp], C.rpbT[o, :, :, krp:krp + 14])
        sel = P.tile("nt_sel", [31, 2, 64, 128])
        P.dma("sp", sel[:, 0], C.na_sel[0])
        P.dma("pool", sel[:, 1], C.na_sel[1])
        negm = P.tile("nt_neg", [128, 64])
        P.dma("sp", negm, C.na_negm)
        pss = [P.psum(f"nt_ps{i}", [128, 8, 14]) for i in range(2)]
        for c in range(64):
            ps = pss[c % 2]
            P.mm(ps, lhsT=sel[:, 0, c, :], rhs=R[:, 0], start=True, stop=False)
            P.mm(ps, lhsT=sel[:, 1, c, :], rhs=R[:, 1], start=False, stop=True)
            P.dve("tensor_scalar", out=TB2[:, :, :, c], in0=ps, scalar1=negm[:, c:c + 1], scalar2=None, op0=ALU.add)


def phase_na(P, C, ZO, VTOK, MIXT, b, TB2, do_ctx):
    with P.scope():
        QT = [P.tile(f"na_q{i}", [128, TT]) for i in range(4)]
        KT = [P.tile(f"na_k{i}", [128, TT]) for i in range(4)]
        for hp in range(4):
            P.dma("sp", QT[hp], ZO[1024 + hp * 128:1024 + (hp + 1) * 128, b * TT:(b + 1) * TT])
            P.dma("pool", KT[hp], ZO[1536 + hp * 128:1536 + (hp + 1) * 128, b * TT:(b + 1) * TT])
            P.act("activation", out=QT[hp], in_=QT[hp], func=AF.Copy, scale=0.125)
        VC = P.tile("na_vc", [128, 2, 512])
        P.dma("sp", VC, VTOK[b * TT:b * TT + CTX, :].rearrange("(j p) n -> p j n", p=128))
        VW = [P.tile(f"na_vw{i}", [128, 4, 512]) for i in range(2)]
        QK = [P.tile(f"na_qk{i}", [128, 64]) for i in range(2)]
        Tt = [P.tile(f"na_t{i}", [128, 4, 64]) for i in range(2)]
        E = [P.tile(f"na_e{i}", [128, 6, 64]) for i in range(2)]
        RS = [P.tile(f"na_rs{i}", [64, 64]) for i in range(2)]
        OB = [P.tile(f"na_ob{i}", [64, 8, 64]) for i in range(2)]
        STp = [P.psum(f"na_st{i}", [128, 6, 64]) for i in range(2)]
        OSp = [P.psum(f"na_os{i}", [64, 2, 64]) for i in range(2)]
        negones = C.negones
        it = [0]

        def block(h, q0, kchunks, vch, bias, ob):
            i = it[0] % 2
            it[0] += 1
            hp, hb = h // 2, (h % 2) * 64
            nk = len(kchunks)
            qk, st, os_, e = QK[i], STp[i], OSp[i], E[i]
            P.dve("tensor_tensor", out=qk[hb:hb + 64, :], in0=QT[hp][hb:hb + 64, q0:q0 + 64],
                  in1=KT[hp][hb:hb + 64, q0:q0 + 64], op=ALU.mult)
            for j, k0 in enumerate(kchunks):
                P.mm(st[:, j, :], lhsT=KT[hp][hb:hb + 64, k0:k0 + 128], rhs=QT[hp][hb:hb + 64, q0:q0 + 64],
                     start=True, stop=False)
                P.mm(st[:, j, :], lhsT=negones[hb:hb + 64, :], rhs=qk[hb:hb + 64, :], start=False, stop=True)
            nl = 0
            if bias is not None:
                nl = 4
                P.dve("tensor_tensor", out=Tt[i], in0=st[:, 0:4, :], in1=bias, op=ALU.add)
                P.act("activation", out=e[:, 0:4, :], in_=Tt[i], func=AF.Exp)
            P.act("activation", out=e[:, nl:nk, :], in_=st[:, nl:nk, :], func=AF.Exp)
            for j in range(nk):
                P.mm(os_[:, 0, :], lhsT=vch[j], rhs=e[:, j, :], start=(j == 0), stop=(j == nk - 1))
            for j in range(nk):
                P.mm(os_[:, 1, :], lhsT=C.ones[:, 0:64], rhs=e[:, j, :], start=(j == 0), stop=(j == nk - 1))
            P.dve("reciprocal", out=RS[i], in_=os_[:, 1, :])
            P.dve("tensor_tensor", out=ob[:, h, :], in0=os_[:, 0, :], in1=RS[i], op=ALU.mult)

        mo = MIXT[512:1024, :].rearrange("(h d) t -> d h t", d=64)
        n = 0
        for r in range(32):
            r0 = min(max(r - 4, 0), 24)
            off = r0 - r + 7
            vw = VW[n % 2]
            ob = OB[n % 2]
            n += 1
            t0 = b * TT + CTX + r0 * 64
            P.dma("sp", vw, VTOK[t0:t0 + 512, :].rearrange("(j p) n -> p j n", p=128))
            q0 = CTX + r * 64
            kch = [CTX + r0 * 64 + 128 * j for j in range(4)] + [0, 128]
            for h in range(8):
                vch = [vw[:, j, h * 64:(h + 1) * 64] for j in range(4)] + [VC[:, j, h * 64:(h + 1) * 64] for j in range(2)]
                d = TB2.ap.ap
                bias = TB2[:, h, off:off + 7:2, :] if False else TB2[:, h].raw(off * 64, [[d[0][0], 128], [128, 4], [1, 64]])
                block(h, q0, kch, vch, bias, ob)
            P.dma("pool", mo[:, :, b * TT + q0:b * TT + q0 + 64], ob)
        if do_ctx:
            for qi in range(4):
                ob = OB[n % 2]
                n += 1
                for h in range(8):
                    vch = [VC[:, j, h * 64:(h + 1) * 64] for j in range(2)]
                    block(h, qi * 64, [0, 128], vch, None, ob)
                P.dma("pool", mo[:, :, b * TT + qi * 64:b * TT + qi * 64 + 64], ob)


EXT_SHAPES.update({
    "lru_small": [2, 128, 4, 11], "lru_bd": [2, 2, 2, 4, 128, 128], "rpbT": [2, 31, 8, 15],
    "na_sel": [2, 31, 64, 128], "na_negm": [128, 64], "dbg_in": [D, NT],
})


def _na_consts():
    sel = np.zeros((2, 31, 64, 128), np.float32)
    negm = np.full((128, 64), -30000.0, np.float32)
    for c in range(64):
        w0 = min(max(c - 8, 0), 48)
        for kc in range(w0, w0 + 16):
            x = kc - c + 15
            sel[0, x, c, kc] = 1.0
            sel[1, x, c, 64 + kc] = 1.0
            negm[kc, c] = 0.0
            negm[64 + kc, c] = 0.0
    return sel, negm


def host_inputs(inputs, names=None):
    f = lambda a: np.ascontiguousarray(np.asarray(a, dtype=np.float32))
    g = lambda k: np.asarray(inputs[k], np.float32)
    x, c, ctx, c_ctx = g("x"), g("c"), g("ctx"), g("c_ctx")
    consts = np.zeros((128, 4, 128), np.float32)
    consts[:, 0, :] = np.eye(128, dtype=np.float32)
    consts[:, 1, :] = 1.0
    consts[:64, 2, :64] = 1.0
    consts[64:, 2, 64:] = 1.0
    consts[:, 3, :] = -1.0
    def pf(a):
        return np.moveaxis(a.reshape(a.shape[:-1] + (4, 128)), -1, 0)
    lru_small = np.zeros((2, 128, 4, 11), np.float32)
    cw, cb, gb, lam = g("lru_conv_w"), g("lru_conv_b"), g("lru_gate_b"), g("lru_lambda")
    for o in range(2):
        lru_small[o, :, :, 0:4] = np.moveaxis(pf(cw[o]), 1, 2)
        lru_small[o, :, :, 4] = pf(cb[o])
        lru_small[o, :, :, 5:9] = np.moveaxis(pf(gb[o]).reshape(128, 4, 4), 1, 2)
        lru_small[o, :, :, 9:11] = np.moveaxis(pf(lam[o]), 1, 2)
    gw = g("lru_gate_w")
    lru_bd = np.zeros((2, 2, 2, 4, 128, 128), np.float32)
    for ch in range(4):
        for n2 in range(2):
            lru_bd[:, :, :, ch, n2 * 64:(n2 + 1) * 64, n2 * 64:(n2 + 1) * 64] = gw[:, :, :, ch * 2 + n2]
    sel, negm = _na_consts()
    shared = {
        "ada_w": g("ada_w"),
        "ada_bT": f(g("ada_b").reshape(DEPTH, 48, 128).transpose(0, 2, 1)),
        "mix_out_w": g("mix_out_w"), "mlp_w1": g("mlp_w1"), "mlp_w2": g("mlp_w2"),
        "ev_in_w": g("ev_in_w"), "od_in_w": g("od_in_w"),
        "consts": consts,
        "fnwT": f(g("final_norm_w").reshape(8, 128).T),
        "lru_small": lru_small, "lru_bd": lru_bd,
        "rpbT": f(g("na_rpb").transpose(0, 3, 1, 2)),
        "na_sel": sel, "na_negm": negm,
    }
    shared.update(host_even(inputs))
    maps = []
    for core in range(NCORES):
        bs = [core * NB + i for i in range(NB)]
        tok = np.concatenate([np.concatenate([ctx[b], x[b]], axis=0) for b in bs], axis=0)
        cond = np.stack([c[bs[0]], c[bs[1]], c_ctx], axis=1)
        m = dict(shared)
        m["xT"] = f(tok.T)
        m["condT"] = f(cond.reshape(8, 128, 3).transpose(1, 0, 2))
        if names is not None:
            m = {k: v for k, v in m.items() if k in names}
        maps.append({k: f(v) for k, v in m.items()})
    return maps


EXT_SHAPES.update({
    "rope": [2, TT, 64], "tri": [2, 128, 128], "ml_gb": [2, 16], "ml_nw": [2, 512],
})
ML_ORDER = [list(range(18)), [1, 0] + list(range(17, 1, -1))]


def phase_mlstm(P, C, XT, HD, b, e):
    W3 = C.ev_in_w[e].rearrange("(kc p) n -> p kc n", p=128)
    with P.scope():
        Wq = P.tile("m_wq", [128, 8, 512])
        Wk = P.tile("m_wk", [128, 8, 512])
        Wv = P.tile("m_wv", [128, 8, 512])
        Wg = P.tile("m_wg", [128, 8, 16])
        P.dma("sp", Wq, W3[:, :, 0:512])
        P.dma("pool", Wk, W3[:, :, 512:1024])
        P.dma("sp", Wv, W3[:, :, 1024:1536])
        P.dma("pool", Wg, W3[:, :, 2048:2064])
        CS = P.tile("m_cs", [128, 18, 2, 64])
        for i in range(2):
            P.dma("sp", CS[:, :, i, :], C.rope[i].rearrange("(t p) f -> p t f", p=128))
        tri = P.tile("m_tri", [128, 2, 128])
        P.dma("sp", tri, C.tri.rearrange("d p n -> p d n"))
        GB = P.tile("m_gb", [128, 16])
        P.dma("sp", GB, C.ml_gb[e:e + 1, :].pbc(128))
        Caug = P.tile("m_c", [128, 4, 129])
        VA = P.tile("m_va", [128, 4, 129])
        P.dve("memset", ap=VA, constant=1.0)
        G = P.tile("m_g", [128, 16])
        LF = P.tile("m_lf", [128, 4])
        QS = P.tile("m_qs", [128, 4])
        KS = P.tile("m_ks", [128, 4])
        EBT = P.tile("m_ebt", [128, 4])
        Qc = P.tile("m_qc", [128, 4, 2, 64])
        Kc = P.tile("m_kc", [128, 4, 2, 64])
        QR = P.tile("m_qr", [128, 4, 2, 64])
        KR = P.tile("m_kr", [128, 4, 2, 64])
        A1 = P.tile("m_a1", [128, 4, 64])
        A2 = P.tile("m_a2", [128, 4, 64])
        A3 = P.tile("m_a3", [128, 4, 64])
        A4 = P.tile("m_a4", [128, 4, 64])
        Qs = P.tile("m_qsc", [128, 4, 128])
        Ks = P.tile("m_ksc", [128, 4, 128])
        Ke = P.tile("m_ke", [128, 4, 128])
        QsT = P.tile("m_qst", [128, 4, 128])
        KsT = P.tile("m_kst", [128, 4, 128])
        ST = [P.tile(f"m_st{i}", [128, 128]) for i in range(2)]
        dn = P.tile("m_dn", [128, 4])
        Hout = [P.tile(f"m_ho{i}", [128, 4, 128]) for i in range(2)]
        ps_g = P.psum("m_psg", [128, 3, 16])
        ps_q = P.psum("m_psq", [128, 512])
        ps_k = P.psum("m_psk", [128, 512])
        ps_v = P.psum("m_psv", [128, 512])
        ps_t = P.psum("m_pst", [128, 4, 128])
        ps_s = P.psum("m_pss", [128, 128])
        ps_n = [P.psum(f"m_psn{i}", [128, 129]) for i in range(2)]
        it = 0
        for d in range(2):
            P.dve("memset", ap=Caug, constant=0.0)
            for t in ML_ORDER[d]:
                xt = lambda kc: XT[:, kc, t * 128:(t + 1) * 128]
                for kc in range(8):
                    P.mm(ps_g[:, 0, :], lhsT=xt(kc), rhs=Wg[:, kc, :], start=(kc == 0), stop=(kc == 7))
                P.dve("tensor_tensor", out=G, in0=ps_g[:, 0, :], in1=GB, op=ALU.add)
                P.act("activation", out=LF, in_=G[:, 8 + d * 4:12 + d * 4], func=AF.Sigmoid)
                P.act("activation", out=LF, in_=LF, func=AF.Ln)
                P.mm(ps_g[:, 1, 0:4], lhsT=tri[:, d, :], rhs=LF)
                P.mm(ps_g[:, 2, 0:4], lhsT=C.ones, rhs=LF)
                P.act("activation", out=QS, in_=ps_g[:, 1, 0:4], func=AF.Exp)
                P.dve("tensor_tensor", out=KS, in0=G[:, d * 4:d * 4 + 4], in1=ps_g[:, 1, 0:4], op=ALU.subtract)
                P.act("activation", out=KS, in_=KS, func=AF.Exp)
                P.dve("tensor_scalar", out=KS, in0=KS, scalar1=float(128 ** -0.5), scalar2=None, op0=ALU.mult)
                P.act("activation", out=EBT, in_=ps_g[:, 2, 0:4], func=AF.Exp)
                for (ps, W) in ((ps_q, Wq), (ps_k, Wk), (ps_v, Wv)):
                    for kc in range(8):
                        P.mm(ps, lhsT=xt(kc), rhs=W[:, kc, :], start=(kc == 0), stop=(kc == 7))
                P.act("activation", out=Qc, in_=ps_q.rearrange("p (h s f) -> p h s f", h=4, s=2), func=AF.Copy)
                P.act("activation", out=Kc, in_=ps_k.rearrange("p (h s f) -> p h s f", h=4, s=2), func=AF.Copy)
                P.act("activation", out=VA[:, :, 0:128], in_=ps_v.rearrange("p (h f) -> p h f", h=4), func=AF.Copy)
                cosb = CS[:, t, 0, :].ins(1, 4)
                sinb = CS[:, t, 1, :].ins(1, 4)
                for (src, dst, eng1, eng2) in ((Qc, QR, "dve", "pool"), (Kc, KR, "pool", "dve")):
                    P.op(eng1, "tensor_tensor", out=A1, in0=src[:, :, 0, :], in1=cosb, op=ALU.mult)
                    P.op(eng2, "tensor_tensor", out=A2, in0=src[:, :, 1, :], in1=sinb, op=ALU.mult)
                    P.op(eng1, "tensor_tensor", out=dst[:, :, 0, :], in0=A1, in1=A2, op=ALU.subtract)
                    P.op(eng2, "tensor_tensor", out=A3, in0=src[:, :, 0, :], in1=sinb, op=ALU.mult)
                    P.op(eng1, "tensor_tensor", out=A4, in0=src[:, :, 1, :], in1=cosb, op=ALU.mult)
                    P.op(eng2, "tensor_tensor", out=dst[:, :, 1, :], in0=A3, in1=A4, op=ALU.add)
                P.dve("tensor_tensor", out=Qs, in0=QR.rearrange("p h s f -> p h (s f)"), in1=QS.ins(2, 128), op=ALU.mult)
                P.pool("tensor_tensor", out=Ks, in0=KR.rearrange("p h s f -> p h (s f)"), in1=KS.ins(2, 128), op=ALU.mult)
                P.pool("tensor_tensor", out=Ke, in0=Ks, in1=EBT.ins(2, 128), op=ALU.mult)
                for h in range(4):
                    P.pe("transpose", out=ps_t[:, h, :], in_=Qs[:, h, :], identity=C.ident)
                P.act("activation", out=QsT, in_=ps_t, func=AF.Copy)
                for h in range(4):
                    P.pe("transpose", out=ps_t[:, h, :], in_=Ks[:, h, :], identity=C.ident)
                P.dve("tensor_copy", out=KsT, in_=ps_t)
                ho = Hout[it % 2]
                it += 1
                for h in range(4):
                    st = ST[h % 2]
                    pn = ps_n[h % 2]
                    P.mm(ps_s, lhsT=KsT[:, h, :], rhs=QsT[:, h, :])
                    P.dve("tensor_tensor", out=st, in0=ps_s, in1=tri[:, d, :], op=ALU.mult)
                    P.mm(pn, lhsT=QsT[:, h, :], rhs=Caug[:, h, :], start=True, stop=False)
                    P.mm(pn, lhsT=st, rhs=VA[:, h, :], start=False, stop=True)
                    P.act("activation", out=dn[:, h:h + 1], in_=pn[:, 128:129], func=AF.Abs)
                    P.dve("tensor_scalar", out=dn[:, h:h + 1], in0=dn[:, h:h + 1], scalar1=1.0, scalar2=None,
                          op0=ALU.max)
                    P.dve("reciprocal", out=dn[:, h:h + 1], in_=dn[:, h:h + 1])
                    P.act("activation", out=ho[:, h, :], in_=pn[:, 0:128], func=AF.Copy, scale=dn[:, h:h + 1])
                    P.mm(pn, lhsT=Ke[:, h, :], rhs=VA[:, h, :])
                    P.dve("scalar_tensor_tensor", out=Caug[:, h, :], in0=Caug[:, h, :], scalar=EBT[:, h:h + 1],
                          in1=pn, op0=ALU.mult, op1=ALU.add)
                P.dma("pool", HD[d][b * TT + t * 128:b * TT + (t + 1) * 128, :], ho.rearrange("p h f -> p (h f)"))


def phase_mlstm_out(P, C, XT, HD, MIXT, b, e):
    W3 = C.ev_in_w[e].rearrange("(kc p) n -> p kc n", p=128)
    with P.scope():
        Wo = P.tile("mo_wo", [128, 8, 512])
        P.dma("sp", Wo, W3[:, :, 1536:2048])
        NW = P.tile("mo_nw", [128, 512])
        P.dma("sp", NW, C.ml_nw[e:e + 1, :].pbc(128))
        H0 = [P.tile(f"mo_h0{i}", [128, 4, 128]) for i in range(2)]
        H1 = [P.tile(f"mo_h1{i}", [128, 4, 128]) for i in range(2)]
        SQ = P.tile("mo_sq", [128, 4, 128])
        ss = P.tile("mo_ss", [128, 4])
        SG = P.tile("mo_sg", [128, 4, 128])
        OT = [P.tile(f"mo_ot{i}", [128, 4, 128]) for i in range(2)]
        ps_o = P.psum("mo_pso", [128, 512])
        ps_t = P.psum("mo_pst", [128, 4, 128])
        mo = MIXT[0:512, :].rearrange("(c p) t -> p c t", p=128)
        for t in range(18):
            i = t % 2
            rows = slice(b * TT + t * 128, b * TT + (t + 1) * 128)
            P.dma("sp", H0[i], HD[0][rows, :].rearrange("p (h f) -> p h f", h=4))
            P.dma("pool", H1[i], HD[1][rows, :].rearrange("p (h f) -> p h f", h=4))
            for kc in range(8):
                P.mm(ps_o, lhsT=XT[:, kc, t * 128:(t + 1) * 128], rhs=Wo[:, kc, :], start=(kc == 0), stop=(kc == 7))
            P.act("activation", out=SG, in_=ps_o.rearrange("p (h f) -> p h f", h=4), func=AF.Sigmoid)
            P.dve("tensor_tensor", out=H0[i], in0=H0[i], in1=H1[i], op=ALU.add)
            P.pool("tensor_tensor", out=SQ, in0=H0[i], in1=H0[i], op=ALU.mult)
            P.dve("tensor_reduce", out=ss, in_=SQ, axis=AX.X, op=ALU.add)
            P.act("activation", out=ss, in_=ss, func=AF.Sqrt, scale=1.0 / 128, bias=C.epsc)
            P.dve("reciprocal", out=ss, in_=ss)
            P.dve("tensor_tensor", out=H0[i], in0=H0[i], in1=ss.ins(2, 128), op=ALU.mult)
            P.pool("tensor_tensor", out=SG, in0=SG, in1=NW.rearrange("p (h f) -> p h f", h=4), op=ALU.mult)
            P.dve("tensor_tensor", out=H0[i], in0=H0[i], in1=SG, op=ALU.mult)
            for h in range(4):
                P.pe("transpose", out=ps_t[:, h, :], in_=H0[i][:, h, :], identity=C.ident)
            P.act("activation", out=OT[i], in_=ps_t, func=AF.Copy)
            P.dma("pool", mo[:, :, b * TT + t * 128:b * TT + (t + 1) * 128], OT[i])


def host_even(inputs):
    g = lambda k: np.asarray(inputs[k], np.float32)
    t = np.arange(TL)
    row = (t // 64).astype(np.float32)
    col = (t % 64).astype(np.float32)
    n_freq = 32
    inv = (np.float32(10000.0) ** (-np.arange(n_freq, dtype=np.float32) / np.float32(n_freq))).astype(np.float32)
    ang = np.concatenate([row[:, None] * inv, col[:, None] * inv], axis=-1).astype(np.float32)
    rope = np.zeros((2, TT, 64), np.float32)
    rope[0, :CTX] = 1.0
    rope[0, CTX:] = np.cos(ang)
    rope[1, CTX:] = np.sin(ang)
    tri = np.zeros((2, 128, 128), np.float32)
    tri[0] = np.triu(np.ones((128, 128), np.float32))
    tri[1] = np.tril(np.ones((128, 128), np.float32))
    gb = g("ml_gate_b")
    ml_gb = np.concatenate([gb[:, :, 0, :].reshape(2, 8), gb[:, :, 1, :].reshape(2, 8)], axis=1)
    pf = lambda a: np.moveaxis(a.reshape(a.shape[:-1] + (4, 128)), -1, 0)
    rw_small = np.zeros((2, 128, 4, 9), np.float32)
    w0, a0 = g("rw_w0"), g("rw_a0")
    for e in range(2):
        rw_small[e, :, :, 0:2] = np.moveaxis(pf(w0[e]), 1, 2)
        rw_small[e, :, :, 2:4] = np.moveaxis(pf(a0[e]), 1, 2)
        rw_small[e, :, :, 4] = pf(g("rw_k_k")[e])
        rw_small[e, :, :, 5] = pf(g("rw_k_a")[e])
        rw_small[e, :, :, 6] = pf(g("rw_r_k")[e].reshape(512))
        rw_small[e, :, :, 7] = pf(g("rw_ln_w")[e])
        rw_small[e, :, :, 8] = pf(g("rw_ln_b")[e])
    return {"rope": rope, "tri": tri, "ml_gb": ml_gb, "ml_nw": g("ml_norm_w"),
            "rw_muT": np.ascontiguousarray(g("rw_mu").reshape(2, 15, 128).transpose(0, 2, 1)),
            "rw_small": rw_small,
            "rw_w_up": g("rw_w_up").reshape(2, 128, 512), "rw_a_up": g("rw_a_up").reshape(2, 128, 512),
            "rw_g_up": g("rw_g_up")}


EXT_SHAPES.update({
    "rw_muT": [2, 128, 15], "rw_small": [2, 128, 4, 9], "rw_w_up": [2, 128, 512], "rw_a_up": [2, 128, 512],
    "rw_g_up": [2, 128, 512],
})
SEGS = ((0, CTX), (CTX, TT))
RW_DECAY_SCALE = -math.exp(-0.5)


def phase_rwkv_prep(P, C, ZR, SC, VT, b, e):
    cols = slice(b * TT, (b + 1) * TT)
    with P.scope():
        mu = P.tile("rp_mu", [128, 15])
        P.dma("sp", mu, C.rw_muT[e])
        sm = P.tile("rp_sm", [128, 4, 9])
        P.dma("sp", sm, C.rw_small[e])
        WU = P.tile("rp_wu", [128, 512])
        AU = P.tile("rp_au", [128, 512])
        GU = P.tile("rp_gu", [128, 512])
        P.dma("sp", WU, C.rw_w_up[e])
        P.dma("sp", AU, C.rw_a_up[e])
        P.dma("sp", GU, C.rw_g_up[e])
        nm = ["z", "s", "tw", "ad", "sg", "r", "k", "v", "kk", "nr", "x1", "x2", "x3"]
        T = {n: P.tile("rp_" + n, [128, TT]) for n in nm}
        pss = [P.psum(f"rp_ps{i}", [128, 512]) for i in range(4)]
        pst = [P.psum(f"rp_pt{i}", [128, 128]) for i in range(2)]
        vst = [P.tile(f"rp_vst{i}", [128, 128]) for i in range(2)]
        kctr = [0]

        def nps():
            kctr[0] += 1
            return pss[kctr[0] % 4]

        def shiftmix(j, out):
            Zt, S = T["z"], T["s"]
            P.dma("sp", Zt, ZR[j * 128:(j + 1) * 128, cols])
            for (a0, a1) in SEGS:
                P.pool("tensor_copy", out=S[:, a0 + 1:a1], in_=Zt[:, a0:a1 - 1])
                P.pool("memset", ap=S[:, a0:a0 + 1], constant=0.0)
                P.dve("tensor_tensor", out=S[:, a0:a1 - 1], in0=S[:, a0:a1 - 1], in1=Zt[:, a0 + 1:a1], op=ALU.add)
            P.dve("scalar_tensor_tensor", out=S, in0=S, scalar=0.5, in1=Zt, op0=ALU.mult, op1=ALU.subtract)
            P.dve("scalar_tensor_tensor", out=out, in0=S, scalar=mu[:, j:j + 1], in1=Zt, op0=ALU.mult, op1=ALU.add)

        TW, AD, SG = T["tw"], T["ad"], T["sg"]
        shiftmix(12, TW)
        P.act("activation", out=TW, in_=TW, func=AF.Tanh)
        shiftmix(13, AD)
        shiftmix(14, SG)
        P.act("activation", out=SG, in_=SG, func=AF.Sigmoid)
        Rt, Kt, Vt, KK, NR, X1, X2, X3 = (T[n] for n in ("r", "k", "v", "kk", "nr", "x1", "x2", "x3"))
        for hp in range(4):
            rows = slice(hp * 128, (hp + 1) * 128)
            shiftmix(hp, Rt)
            P.dma("pool", SC["r"][rows, cols], Rt)
            shiftmix(4 + hp, Kt)
            P.dma("pool", SC["k"][rows, cols], Kt)
            shiftmix(8 + hp, Vt)
            P.dma("pool", SC["v"][rows, cols], Vt)
            for t in range(18):
                pt = pst[t % 2]
                vs = vst[t % 2]
                P.pe("transpose", out=pt, in_=Vt[:, t * 128:(t + 1) * 128], identity=C.ident)
                P.act("activation", out=vs, in_=pt, func=AF.Copy)
                P.dma("pool", VT.raw((b * TT + t * 128) * 512 + hp * 64, [[512, 128], [256, 2], [1, 64]]),
                      vs.rearrange("p (h f) -> p h f", h=2))
            P.act("activation", out=KK, in_=Kt, func=AF.Copy, scale=sm[:, hp, 4:5])
            P.pool("tensor_tensor", out=X1, in0=KK, in1=KK, op=ALU.mult)
            for tt in range(NTT):
                ps = nps()
                P.mm(ps[:, 0:TN], lhsT=C.blk64, rhs=X1[:, tt * TN:(tt + 1) * TN])
                P.act("activation", out=NR[:, tt * TN:(tt + 1) * TN], in_=ps[:, 0:TN], func=AF.Sqrt)
            P.dve("tensor_scalar", out=NR, in0=NR, scalar1=1e-12, scalar2=None, op0=ALU.max)
            P.dve("reciprocal", out=NR, in_=NR)
            P.dve("tensor_tensor", out=KK, in0=KK, in1=NR, op=ALU.mult)
            P.pool("tensor_scalar", out=X1, in0=KK, scalar1=-1.0, scalar2=None, op0=ALU.mult)
            P.dma("pool", SC["a"][rows, cols], X1)
            for d in range(2):
                pr = slice(d * 64, (d + 1) * 64)
                for tt in range(NTT):
                    tc_ = slice(tt * TN, (tt + 1) * TN)
                    ps = nps()
                    P.mm(ps[:, 0:TN], lhsT=WU[pr, rows], rhs=TW[pr, tc_])
                    P.act("activation", out=X2[:, tc_], in_=ps[:, 0:TN], func=AF.Sigmoid, bias=sm[:, hp, d:d + 1])
                    ps = nps()
                    P.mm(ps[:, 0:TN], lhsT=AU[pr, rows], rhs=AD[pr, tc_])
                    P.act("activation", out=X3[:, tc_], in_=ps[:, 0:TN], func=AF.Sigmoid, bias=sm[:, hp, 2 + d:3 + d])
                P.act("activation", out=X2, in_=X2, func=AF.Exp, scale=RW_DECAY_SCALE)
                P.dma("pool", SC[f"w{d}"][rows, cols], X2)
                P.pool("tensor_tensor", out=NR, in0=KK, in1=X3, op=ALU.mult)
                P.dma("pool", SC[f"bv{d}"][rows, cols], NR)
                P.dve("tensor_scalar", out=X3, in0=X3, scalar1=-1.0, scalar2=sm[:, hp, 5:6], op0=ALU.add, op1=ALU.mult)
                P.dve("scalar_tensor_tensor", out=X3, in0=X3, scalar=1.0, in1=Kt, op0=ALU.add, op1=ALU.mult)
                P.dma("pool", SC[f"kd{d}"][rows, cols], X3)
            for tt in range(NTT):
                tc_ = slice(tt * TN, (tt + 1) * TN)
                ps = nps()
                P.mm(ps[:, 0:TN], lhsT=GU[:, rows], rhs=SG[:, tc_])
                P.act("activation", out=X1[:, tc_], in_=ps[:, 0:TN], func=AF.Copy)
            P.dma("pool", SC["g"][rows, cols], X1)


RW_CS = 64
RW_CV = 8


def tau(d, n):
    if d == 0:
        return n
    return CTX - 1 - n if n < CTX else TT + CTX - 1 - n


def phase_rwkv_scan(P, C, SC, VT, YT, nsteps=TT):
    CS, CV = RW_CS, RW_CV
    qn = ["a", "w", "kd", "bv", "r"]
    with P.scope():
        X = {q: [P.tile(f"rs_{q}{i}", [128, 16, CS]) for i in range(2)] for q in qn}
        VB = [P.tile(f"rs_vb{i}", [128, CV, 16, 64]) for i in range(2)]
        S = [P.tile(f"rs_s{i}", [128, 16, 64]) for i in range(2)]
        P1 = P.tile("rs_p1", [128, 16, 64])
        Q1 = P.tile("rs_q1", [128, 16, 64])
        T2 = [P.tile(f"rs_t2{i}", [128, 16, 64]) for i in range(2)]
        Tt = P.tile("rs_t", [128, 16, 64])
        RB = [P.tile(f"rs_rb{i}", [128, CS, 16, 2]) for i in range(2)]
        YB = [P.tile(f"rs_yb{i}", [64, 16, 2, 64]) for i in range(2)]
        sa_ps = [P.psum(f"rs_sa{i}", [128, 2, 512]) for i in range(2)]
        y_ps = [P.psum(f"rs_y{i}", [64, 16, 16, 2]) for i in range(2)]
        for i in range(2):
            P.pool("memset", ap=RB[i], constant=0.0)
        P.dve("memset", ap=S[0], constant=0.0)

        def bc(xt, i):
            ps_ = xt.ap.ap[0][0]
            return xt.raw(i, [[ps_, 128], [8 * CS + CS - 1 - 2 * i, 2], [CS, 8], [0, 64]])

        for n in range(nsteps):
            i = n % CS
            ci = (n // CS) % 2
            if i == 0:
                for q in qn:
                    for d in range(2):
                        src = SC[q if q in ("a", "r") else f"{q}{d}"]
                        t0 = tau(d, n) if d == 0 else tau(d, n + CS - 1)
                        for b in range(NB):
                            g0 = d * 8 + b * 4
                            P.dma("sp", X[q][ci][:, g0:g0 + 4, :],
                                  src[:, b * TT + t0:b * TT + t0 + CS].rearrange("(hp p) t -> p hp t", p=128))
                rb = RB[ci]
                xr = X["r"][ci]
                ps_r = rb.ap.ap[0][0]
                ps_x = xr.ap.ap[0][0]
                for h2 in range(2):
                    for d in range(2):
                        o = rb[h2 * 64:(h2 + 1) * 64].raw(d * 16 + h2, [[ps_r, 64], [32, CS], [2, 8]])
                        if d == 0:
                            s_ = xr[h2 * 64:(h2 + 1) * 64].raw(0, [[ps_x, 64], [1, CS], [CS, 8]])
                        else:
                            s_ = xr[h2 * 64:(h2 + 1) * 64].raw(8 * CS + CS - 1, [[ps_x, 64], [-1, CS], [CS, 8]])
                        P.act("activation", out=o, in_=s_, func=AF.Copy)
            iv = n % CV
            vi = (n // CV) % 2
            if iv == 0:
                vb = VB[vi]
                ps_v = vb.ap.ap[0][0]
                for b in range(NB):
                    for d in range(2):
                        sgn = 1 if d == 0 else -1
                        for h2 in range(2):
                            o = vb[h2 * 64:(h2 + 1) * 64].raw((d * 8 + b * 4) * 64, [[ps_v, 64], [16 * 64, CV], [1, 256]])
                            s_ = VT.raw((b * TT + tau(d, n)) * 512 + h2 * 256, [[0, 64], [sgn * 512, CV], [1, 256]])
                            P.dma("act" if (b + d + h2) % 2 else "sp", o, s_)
            Sc, Sn = S[n % 2], S[(n + 1) % 2]
            t2 = T2[n % 2]
            P.pool("tensor_tensor", out=t2, in0=VB[vi][:, iv], in1=bc(X["kd"][ci], i), op=ALU.mult)
            P.dve("tensor_tensor", out=P1, in0=Sc, in1=bc(X["a"][ci], i), op=ALU.mult)
            P.pool("tensor_tensor", out=Q1, in0=Sc, in1=bc(X["w"][ci], i), op=ALU.mult)
            P.pool("tensor_tensor", out=Q1, in0=Q1, in1=t2, op=ALU.add)
            sp = sa_ps[n % 2]
            p1f = P1.rearrange("p g f -> p (g f)")
            for hf in range(2):
                P.mm(sp[:, hf, :], lhsT=C.blk64, rhs=p1f[:, hf * 512:(hf + 1) * 512])
            P.dve("tensor_tensor", out=Tt, in0=sp.rearrange("p a (g f) -> p (a g) f", f=64),
                  in1=bc(X["bv"][ci], i), op=ALU.mult)
            P.dve("tensor_tensor", out=Sn, in0=Q1, in1=Tt, op=ALU.add)
            il = n % 16
            yp = y_ps[(n // 16) % 2]
            for g in range(16):
                P.mm(yp[:, il, g, :], lhsT=Sn[:, g, :], rhs=RB[ci][:, i, g, :])
            if il == 15:
                yb = YB[(n // 64) % 2]
                q16 = (n % 64) // 16
                ps_y = yb.ap.ap[0][0]
                ps_p = yp.ap.ap[0][0]
                for d in range(2):
                    if d == 0:
                        o = yb.raw(d * 8 * 128 + 16 * q16, [[ps_y, 64], [128, 8], [64, 2], [1, 16]])
                    else:
                        o = yb.raw(d * 8 * 128 + 63 - 16 * q16, [[ps_y, 64], [128, 8], [64, 2], [-1, 16]])
                    s_ = yp.raw(d * 16, [[ps_p, 64], [2, 8], [1, 2], [32, 16]])
                    P.act("activation", out=o, in_=s_, func=AF.Copy)
                if n % 64 == 63:
                    n64 = n - 63
                    for b in range(NB):
                        for d in range(2):
                            tok0 = n64 if d == 0 else tau(1, n64) - 63
                            s_ = yb.raw((d * 8 + b * 4) * 128, [[ps_y, 64], [64, 8], [1, 64]])
                            o = YT[d].raw(b * TT + tok0, [[NT, 64], [64 * NT, 8], [1, 64]])
                            P.dma("pool", o, s_)


def phase_rwkv_out(P, C, SC, YT, MIXT, b, e):
    with P.scope():
        sm = P.tile("ro_sm", [128, 4, 9])
        P.dma("sp", sm, C.rw_small[e])
        epl = P.tile("ro_eps", [128, 1])
        P.dve("memset", ap=epl, constant=64e-5)
        nm = ["y0", "y1", "r", "k", "v", "g", "yc", "sq", "rs", "o"]
        T = [{n: P.tile(f"ro_{n}{i}", [128, TN]) for n in nm} for i in range(2)]
        pss = [P.psum(f"ro_ps{i}", [128, 512]) for i in range(6)]
        it = 0
        for hp in range(4):
            rows = slice(hp * 128, (hp + 1) * 128)
            for tt in range(NTT):
                t = T[it % 2]
                pm, pv, pb = pss[(it % 2) * 3], pss[(it % 2) * 3 + 1], pss[(it % 2) * 3 + 2]
                it += 1
                cs = slice(b * TT + tt * TN, b * TT + (tt + 1) * TN)
                P.dma("sp", t["y0"], YT[0][rows, cs])
                P.dma("sp", t["y1"], YT[1][rows, cs])
                for q in ("r", "k", "v", "g"):
                    P.dma("act", t[q], SC[q][rows, cs])
                P.dve("tensor_tensor", out=t["y0"], in0=t["y0"], in1=t["y1"], op=ALU.add)
                P.mm(pm[:, 0:TN], lhsT=C.blk64, rhs=t["y0"])
                P.dve("scalar_tensor_tensor", out=t["yc"], in0=pm[:, 0:TN], scalar=-1.0 / 64, in1=t["y0"],
                      op0=ALU.mult, op1=ALU.add)
                P.pool("tensor_tensor", out=t["sq"], in0=t["yc"], in1=t["yc"], op=ALU.mult)
                P.mm(pv[:, 0:TN], lhsT=C.blk64, rhs=t["sq"])
                P.act("activation", out=t["rs"], in_=pv[:, 0:TN], func=AF.Sqrt, scale=1.0 / 64, bias=epl)
                P.dve("reciprocal", out=t["rs"], in_=t["rs"])
                P.dve("tensor_tensor", out=t["yc"], in0=t["yc"], in1=t["rs"], op=ALU.mult)
                P.act("activation", out=t["yc"], in_=t["yc"], func=AF.Identity, scale=sm[:, hp, 7:8], bias=sm[:, hp, 8:9])
                P.dve("scalar_tensor_tensor", out=t["sq"], in0=t["r"], scalar=sm[:, hp, 6:7], in1=t["k"],
                      op0=ALU.mult, op1=ALU.mult)
                P.mm(pb[:, 0:TN], lhsT=C.blk64, rhs=t["sq"])
                P.dve("tensor_tensor", out=t["o"], in0=pb[:, 0:TN], in1=t["v"], op=ALU.mult)
                P.pool("tensor_tensor", out=t["o"], in0=t["o"], in1=t["yc"], op=ALU.add)
                P.pool("tensor_tensor", out=t["o"], in0=t["o"], in1=t["g"], op=ALU.mult)
                P.dma("pool", MIXT[512 + hp * 128:512 + (hp + 1) * 128, cs], t["o"])


def build_full(nlayers=DEPTH, final=True):
    nc, P, C = build()
    C.outT = V(nc.dram_tensor("outT", [D, NB * TL], F32, kind="ExternalOutput").ap(), Buf("outT"))
    H = P.dram("H", [D, NT])
    MIXT = P.dram("MIXT", [D, NT])
    ZR = P.dram("ZR", [RW_IN, NT])
    SC = {q: P.dram("SC_" + q, [512, NT]) for q in ["w0", "w1", "kd0", "kd1", "bv0", "bv1", "a", "r", "k", "v", "g"]}
    VT = P.dram("VT", [NT, 512])
    YT = [P.dram("YT0", [512, NT]), P.dram("YT1", [512, NT])]
    HD = [P.dram("HD0", [NT, 512]), P.dram("HD1", [NT, 512])]
    ZO = P.dram("ZO", [2048, NT])
    VTOK = P.dram("VTOK", [NT, 512])
    phase_mod(P, C)
    for l in range(nlayers):
        hsrc = C.xT if l == 0 else H
        if l % 2 == 0:
            e = l // 2
            for b in range(NB):
                with P.scope():
                    XT = P.tile("XT", [128, 8, TT])
                    phase_normmod(P, C, hsrc, b, l, 0, 1, XT)
                    with P.scope():
                        lin_fm(P, C, XT, 8, C.ev_in_w[e], ML_IN, RW_IN, store_fm(P, ZR, -ML_IN, b, "zr"), name="ei")
                    phase_mlstm(P, C, XT, HD, b, e)
                    phase_mlstm_out(P, C, XT, HD, MIXT, b, e)
                phase_rwkv_prep(P, C, ZR, SC, VT, b, e)
            phase_rwkv_scan(P, C, SC, VT, YT)
            for b in range(NB):
                phase_rwkv_out(P, C, SC, YT, MIXT, b, e)
        else:
            o = l // 2
            with P.scope():
                TB2 = P.tile("TB2", [128, 8, 14, 64])
                phase_na_table(P, C, o, TB2)
                for b in range(NB):
                    with P.scope():
                        XT = P.tile("XT", [128, 8, TT])
                        phase_normmod(P, C, hsrc, b, l, 0, 1, XT)
                        with P.scope():
                            lin_fm(P, C, XT, 8, C.od_in_w[o], 0, 2048, store_fm(P, ZO, 0, b, "zo"), name="oi")
                        lin_tm(P, C, XT, C.od_in_w[o], 2048, 512, VTOK, b)
                    phase_lru(P, C, ZO, MIXT, b, o)
                    phase_na(P, C, ZO, VTOK, MIXT, b, TB2, l != DEPTH - 1)
        for b in range(NB):
            with P.scope():
                XT = P.tile("XT", [128, 8, TT])
                phase_outproj(P, C, MIXT, b, l, hsrc, H, XT)
            phase_mlp(P, C, H, b, l)
    if final:
        phase_final(P, C, H)
    P.finish()
    return nc, P, C


def kernel(**inputs):
    nc, P, C = build_full()
    names = set(C._ext.keys())
    maps = host_inputs(inputs, names)
    res = run_bass_kernel_spmd(nc, maps, core_ids=list(range(NCORES)))
    out = np.empty((NCORES * NB, TL, D), np.float32)
    for core in range(NCORES):
        o = np.asarray(res.results[core]["outT"], np.float32)
        for i in range(NB):
            out[core * NB + i] = o[:, i * TL:(i + 1) * TL].T
    return out
```

```python
import contextlib
import math
import numpy as np
import concourse.bass as bass
import concourse.mybir as mybir
from concourse.ap import AP
from concourse.bass_utils import run_bass_kernel_spmd

F32 = mybir.dt.float32
AF = mybir.ActivationFunctionType
ALU = mybir.AluOpType
AX = mybir.AxisListType

NCORES = 8
D = 1024
DEPTH = 4
NB = 2
CTX = 256
TL = 2048
TT = CTX + TL
NT = NB * TT
TN = 384
NTT = TT // TN
DFF = 4096
EVEN_IN = 3984
ODD_IN = 2560
ML_IN = 2064
RW_IN = 1920
EPS = 1e-6


class Buf:
    __slots__ = ("w", "r", "name")

    def __init__(self, name=""):
        self.w = {}
        self.r = {}
        self.name = name


class V:
    __slots__ = ("ap", "bufs")

    def __init__(self, ap, bufs):
        self.ap = ap
        self.bufs = bufs if isinstance(bufs, tuple) else (bufs,)

    def __getitem__(self, key):
        return V(self.ap[key], self.bufs)

    def rearrange(self, pattern, **kw):
        return V(self.ap.rearrange(pattern, **kw), self.bufs)

    def with_ap(self, ap):
        return V(ap, self.bufs)

    def raw(self, off, dims):
        return V(AP(self.ap.tensor, self.ap.offset + off, [list(d) for d in dims]), self.bufs)

    def ins(self, axis, n):
        dims = [list(d) for d in self.ap.ap]
        dims.insert(axis, [0, n])
        return V(AP(self.ap.tensor, self.ap.offset, dims), self.bufs)

    def pbc(self, n=128):
        return V(self.ap.partition_broadcast(n), self.bufs)

    @property
    def shape(self):
        return self.ap.shape


HMAP = {"pe": "tensor", "dve": "vector", "act": "scalar", "pool": "gpsimd", "sp": "sync"}


class Prog:
    NDSEM = 10

    def __init__(self, nc):
        self.nc = nc
        self.es = contextlib.ExitStack()
        self.engs = ["pe", "dve", "act", "pool", "sp"]
        self.h = {e: getattr(nc, HMAP[e]) for e in self.engs}
        self.sem = {e: self.es.enter_context(nc.semaphore("s_" + e)) for e in self.engs}
        self.cnt = {e: 0 for e in self.engs}
        self.seen = {e: {} for e in self.engs}
        self.dq = ["sp", "act", "pool"]
        self.dsem = {}
        self.dcount = {q: 0 for q in self.dq}
        for q in self.dq:
            for i in range(self.NDSEM):
                self.dsem[(q, i)] = self.es.enter_context(nc.semaphore(f"d_{q}_{i}"))
        self.nparts = {}
        self.nwaits = 0
        self.nops = 0
        self.scopes = []

    @contextlib.contextmanager
    def scope(self):
        es = contextlib.ExitStack()
        self.scopes.append(es)
        try:
            yield
        finally:
            self.barrier()
            self.scopes.pop()
            es.close()

    def _es(self):
        return self.scopes[-1] if self.scopes else self.es

    def _nm(self, name):
        self.uid = getattr(self, "uid", 0) + 1
        return f"{name}_{self.uid}"

    def tile(self, name, shape, dtype=F32):
        t = self._es().enter_context(self.nc.sbuf_tensor(self._nm("t_" + name), list(shape), dtype))
        return V(t[:], Buf(name))

    def psum(self, name, shape, dtype=F32):
        t = self._es().enter_context(self.nc.psum_tensor(self._nm("p_" + name), list(shape), dtype))
        return V(t[:], Buf(name))

    def dram(self, name, shape, dtype=F32, kind="Internal"):
        t = self.nc.dram_tensor("d_" + name, list(shape), dtype, kind=kind)
        return V(t.ap(), Buf(name))

    def part(self, v, key):
        k = (id(v.bufs[0]), key)
        if k not in self.nparts:
            self.nparts[k] = Buf(str(key))
        return V(v.ap, self.nparts[k])

    def _semh(self, k):
        return self.sem[k] if isinstance(k, str) else self.dsem[k]

    def _wait(self, eng, tok):
        if tok is None:
            return
        semkey, val, src = tok
        if src == eng and eng == "pe":
            return
        if self.seen[eng].get(semkey, 0) >= val:
            return
        self.seen[eng][semkey] = val
        self.h[eng].wait_ge(self._semh(semkey), val)
        self.nwaits += 1

    def _deps(self, eng, reads, writes):
        for v in reads:
            for b in v.bufs:
                for sk, (val, src) in b.w.items():
                    self._wait(eng, (sk, val, src))
        for v in writes:
            for b in v.bufs:
                for sk, (val, src) in b.w.items():
                    self._wait(eng, (sk, val, src))
                for sk, (val, src) in b.r.items():
                    self._wait(eng, (sk, val, src))

    def _mark(self, tok, reads, writes):
        wb = set()
        for v in writes:
            for b in v.bufs:
                b.w[tok[0]] = (tok[1], tok[2])
                b.r = {}
                wb.add(id(b))
        for v in reads:
            for b in v.bufs:
                if id(b) in wb:
                    continue
                b.r[tok[0]] = (tok[1], tok[2])

    def op(self, eng, method, **kw):
        reads, writes = [], []
        kw2 = {}
        for k, a in kw.items():
            if isinstance(a, V):
                (writes if k in ("out", "accum_out", "ap") else reads).append(a)
                kw2[k] = a.ap
            else:
                kw2[k] = a
        self._deps(eng, reads, writes)
        self.cnt[eng] += 1
        tok = (eng, self.cnt[eng], eng)
        getattr(self.h[eng], method)(**kw2).then_inc(self.sem[eng], 1)
        self._mark(tok, reads, writes)
        self.nops += 1
        return tok

    def dma(self, q, out, in_, **kw):
        k = self.dcount[q]
        self.dcount[q] += 1
        idx = k % self.NDSEM
        rnd = k // self.NDSEM
        semkey = (q, idx)
        if rnd > 0:
            self._wait(q, (semkey, 16 * rnd, "dma"))
        self._deps(q, [in_], [out])
        tok = (semkey, 16 * (rnd + 1), "dma")
        self.h[q].dma_start(out=out.ap, in_=in_.ap, **kw).then_inc(self.dsem[semkey], 16)
        self._mark(tok, [in_], [out])
        return tok

    def pe(self, method, **kw):
        return self.op("pe", method, **kw)

    def dve(self, method, **kw):
        return self.op("dve", method, **kw)

    def act(self, method, **kw):
        return self.op("act", method, **kw)

    def pool(self, method, **kw):
        return self.op("pool", method, **kw)

    def mm(self, out, lhsT, rhs, start=True, stop=True):
        return self.op("pe", "matmul", out=out, lhsT=lhsT, rhs=rhs, start=start, stop=stop)

    def barrier(self, engs=None):
        engs = engs or self.engs
        for e in engs:
            for q in self.dq:
                k = self.dcount[q]
                for idx in range(self.NDSEM):
                    n = (k - idx + self.NDSEM - 1) // self.NDSEM if k > idx else 0
                    if n > 0:
                        self._wait(e, ((q, idx), 16 * n, "dma"))
            for e2 in ["pe", "dve", "act", "pool"]:
                if e2 != e and self.cnt[e2] > 0:
                    self._wait(e, (e2, self.cnt[e2], e2))

    def finish(self):
        self.barrier(["sp"])
        self.es.close()


def segs(b, lo, n):
    hi = lo + n
    out = []
    if lo < CTX:
        out.append((0, min(hi, CTX) - lo, 2))
    if hi > CTX:
        out.append((max(lo, CTX) - lo, n, b))
    return out


def phase_mod(P, C):
    M = C.M
    with P.scope():
        cs = P.tile("cs", [128, 8, 3])
        P.dma("sp", cs, C.condT)
        P.act("activation", out=cs, in_=cs, func=AF.Silu)
        bt = P.tile("adab", [128, 4, 48])
        P.dma("sp", bt, C.ada_bT.rearrange("l p f -> p l f"))
        wb = [P.tile(f"mw{i}", [128, 8, 512]) for i in range(2)]
        ps = [P.psum(f"mps{i}", [128, 4, 4]) for i in range(2)]
        it = 0
        for l in range(DEPTH):
            W3 = C.ada_w[l].rearrange("(kc p) f -> p kc f", p=128)
            for g in range(12):
                w = wb[it % 2]
                pp = ps[it % 2]
                P.dma("sp" if it % 2 == 0 else "pool", w, W3[:, :, g * 512:(g + 1) * 512])
                for f4 in range(4):
                    for kc in range(8):
                        P.mm(pp[:, f4, 0:3], lhsT=w[:, kc, f4 * 128:(f4 + 1) * 128], rhs=cs[:, kc, :],
                             start=(kc == 0), stop=(kc == 7))
                P.dve("tensor_tensor", out=M[:, l, g * 4:(g + 1) * 4, :], in0=pp[:, :, 0:3],
                      in1=bt[:, l, g * 4:(g + 1) * 4].ins(2, 3), op=ALU.add)
                it += 1
            for j in (1, 4):
                P.dve("tensor_scalar", out=M[:, l, j * 8:(j + 1) * 8, :], in0=M[:, l, j * 8:(j + 1) * 8, :],
                      scalar1=1.0, scalar2=None, op0=ALU.add)


def norm_tile(P, C, src3, g0, n, Ht, SQ, pss, rt, out3, kcs=8):
    P.dma("sp", Ht[:, :, 0:n], src3[:, :, g0:g0 + n])
    P.act("activation", out=SQ[:, :, 0:n], in_=Ht[:, :, 0:n], func=AF.Square)
    for kc in range(8):
        P.mm(pss[:, 0:n], lhsT=C.ones, rhs=SQ[:, kc, 0:n], start=(kc == 0), stop=(kc == 7))
    P.act("activation", out=rt[:, 0:n], in_=pss[:, 0:n], func=AF.Sqrt, scale=1.0 / D, bias=C.epsc)
    P.dve("reciprocal", out=rt[:, 0:n], in_=rt[:, 0:n])
    P.dve("tensor_tensor", out=out3, in0=Ht[:, :, 0:n], in1=rt[:, 0:n].ins(1, 8), op=ALU.mult)


def modulate(P, C, l, jsh, jsc, b, lo, n, x3):
    for (s0, s1, col) in segs(b, lo, n):
        for kc in range(8):
            P.act("activation", out=x3[:, kc, s0:s1], in_=x3[:, kc, s0:s1], func=AF.Identity,
                  scale=C.M[:, l, jsc * 8 + kc, col:col + 1], bias=C.M[:, l, jsh * 8 + kc, col:col + 1])


def phase_normmod(P, C, src, b, l, jsh, jsc, XT):
    src3 = src.rearrange("(kc p) t -> p kc t", p=128)
    with P.scope():
        Ht = [P.tile(f"nm_h{i}", [128, 8, TN]) for i in range(2)]
        SQ = [P.tile(f"nm_sq{i}", [128, 8, TN]) for i in range(2)]
        rt = [P.tile(f"nm_r{i}", [128, TN]) for i in range(2)]
        pss = [P.psum(f"nm_ps{i}", [128, 512]) for i in range(2)]
        for tt in range(NTT):
            i = tt % 2
            x3 = XT[:, :, tt * TN:(tt + 1) * TN]
            norm_tile(P, C, src3, b * TT + tt * TN, TN, Ht[i], SQ[i], pss[i], rt[i], x3)
            modulate(P, C, l, jsh, jsc, b, tt * TN, TN, x3)


def lin_fm(P, C, XT, KC, Wd, col0, ncols, handler, GW=512, ntt=NTT, tn=TN, name="lf"):
    W3 = Wd.rearrange("(kc p) n -> p kc n", p=128)
    with P.scope():
        wb = [P.tile(f"{name}_w{i}", [128, KC, GW]) for i in range(2)]
        pss = [P.psum(f"{name}_ps{i}", [128, 512]) for i in range(3)]
        k = 0
        ng = (ncols + GW - 1) // GW
        for g in range(ng):
            c0 = col0 + g * GW
            gw = min(GW, col0 + ncols - c0)
            w = wb[g % 2]
            P.dma("sp", w[:, :, 0:gw], W3[:, :, c0:c0 + gw])
            for oc in range((gw + 127) // 128):
                m = min(128, gw - oc * 128)
                for tt in range(ntt):
                    ps = pss[k % 3]
                    k += 1
                    for kc in range(KC):
                        P.mm(ps[0:m, 0:tn], lhsT=w[:, kc, oc * 128:oc * 128 + m],
                             rhs=XT[:, kc, tt * tn:(tt + 1) * tn], start=(kc == 0), stop=(kc == KC - 1))
                    handler(c0 + oc * 128, m, tt, ps)


EXT_SHAPES = {
    "xT": [D, NT], "condT": [128, 8, 3], "ada_w": [DEPTH, D, 6 * D], "ada_bT": [DEPTH, 128, 48],
    "mix_out_w": [DEPTH, D, D], "mlp_w1": [DEPTH, D, DFF], "mlp_w2": [DEPTH, DFF, D],
    "ev_in_w": [2, D, EVEN_IN], "od_in_w": [2, D, ODD_IN], "consts": [128, 4, 128],
    "fnwT": [128, 8],
}


class Ctx:
    def __init__(self, nc, P):
        self.nc = nc
        self.P = P
        self._ext = {}

    def ext(self, name):
        if name not in self._ext:
            t = self.nc.dram_tensor(name, list(EXT_SHAPES[name]), F32, kind="ExternalInput")
            self._ext[name] = V(t.ap(), Buf(name))
        return self._ext[name]

    def __getattr__(self, name):
        if name in EXT_SHAPES:
            return self.ext(name)
        raise AttributeError(name)


def build(dbg=None):
    nc = bass.Bass("TRN2", target_bir_lowering=False)
    P = Prog(nc)
    C = Ctx(nc, P)
    C.dbg = {}
    for nm, shp in (dbg or {}).items():
        C.dbg[nm] = V(nc.dram_tensor("dbg_" + nm, list(shp), F32, kind="ExternalOutput").ap(), Buf(nm))
    cst = P.tile("cst", [128, 4, 128])
    P.dma("sp", cst, C.consts)
    C.ident = cst[:, 0, :]
    C.ones = cst[:, 1, :]
    C.blk64 = cst[:, 2, :]
    C.negones = cst[:, 3, :]
    C.epsc = P.tile("epsc", [128, 1])
    P.dve("memset", ap=C.epsc, constant=EPS)
    C.M = P.tile("M", [128, DEPTH, 48, 3])
    return nc, P, C


def phase_outproj(P, C, MIXT, b, l, hsrc, hdst, XT):
    m3 = MIXT.rearrange("(kc p) t -> p kc t", p=128)
    for kc in range(8):
        P.dma("sp" if kc % 2 == 0 else "pool", XT[:, kc, :], m3[:, kc, b * TT:(b + 1) * TT])
    with P.scope():
        ho = [P.tile(f"op_ho{i}", [128, TN]) for i in range(3)]
        hn = [P.tile(f"op_hn{i}", [128, TN]) for i in range(3)]
        cnt = [0]

        def handler(col, m, tt, ps):
            i = cnt[0] % 3
            cnt[0] += 1
            oc = col // 128
            g0 = b * TT + tt * TN
            key = ("h", oc, b, tt)
            P.dma("sp", ho[i], P.part(hsrc, key)[col:col + 128, g0:g0 + TN])
            for (s0, s1, cc) in segs(b, tt * TN, TN):
                P.dve("scalar_tensor_tensor", out=hn[i][:, s0:s1], in0=ps[:, s0:s1],
                      scalar=C.M[:, l, 16 + oc, cc:cc + 1], in1=ho[i][:, s0:s1], op0=ALU.mult, op1=ALU.add)
            P.dma("pool", P.part(hdst, key)[col:col + 128, g0:g0 + TN], hn[i])

        lin_fm(P, C, XT, 8, C.mix_out_w[l], 0, D, handler, name="op")


def phase_mlp(P, C, H, b, l):
    src3 = H.rearrange("(kc p) t -> p kc t", p=128)
    with P.scope():
        Ht = P.tile("ml_h", [128, 8, TN])
        SQ = P.tile("ml_sq", [128, 8, TN])
        Xt = P.tile("ml_x", [128, 8, TN])
        rt = P.tile("ml_r", [128, TN])
        pss = P.psum("ml_ps", [128, 512])
        H1 = P.tile("ml_h1", [128, 32, TN])
        hn = [P.tile(f"ml_hn{i}", [128, TN]) for i in range(2)]
        cnt = [0]
        for tt in range(NTT):
            g0 = b * TT + tt * TN
            for kc in range(8):
                pass
            hv = V(src3.ap, tuple(P.part(H, ("h", kc, b, tt)).bufs[0] for kc in range(8)))
            norm_tile(P, C, hv, g0, TN, Ht, SQ, pss, rt, Xt)
            modulate(P, C, l, 3, 4, b, tt * TN, TN, Xt)

            def h1(col, m, _t, ps):
                oc = col // 128
                P.act("activation", out=H1[:, oc, :], in_=ps[:, 0:TN], func=AF.Relu)
                P.pool("tensor_tensor", out=H1[:, oc, :], in0=H1[:, oc, :], in1=H1[:, oc, :], op=ALU.mult)

            lin_fm(P, C, Xt, 8, C.mlp_w1[l], 0, DFF, h1, ntt=1, name="m1")

            def h2(col, m, _t, ps):
                i = cnt[0] % 2
                cnt[0] += 1
                oc = col // 128
                for (s0, s1, cc) in segs(b, tt * TN, TN):
                    P.dve("scalar_tensor_tensor", out=hn[i][:, s0:s1], in0=ps[:, s0:s1],
                          scalar=C.M[:, l, 40 + oc, cc:cc + 1], in1=Ht[:, oc, s0:s1], op0=ALU.mult, op1=ALU.add)
                P.dma("pool", P.part(H, ("h", oc, b, tt))[col:col + 128, g0:g0 + TN], hn[i])

            lin_fm(P, C, H1, 32, C.mlp_w2[l], 0, D, h2, GW=256, ntt=1, name="m2")


def phase_final(P, C, H):
    src3 = H.rearrange("(kc p) t -> p kc t", p=128)
    o3 = C.outT.rearrange("(kc p) t -> p kc t", p=128)
    with P.scope():
        fw = P.tile("fn_w", [128, 8])
        P.dma("sp", fw, C.fnwT)
        Ht = [P.tile(f"fn_h{i}", [128, 8, TN]) for i in range(2)]
        SQ = [P.tile(f"fn_sq{i}", [128, 8, TN]) for i in range(2)]
        Xo = [P.tile(f"fn_x{i}", [128, 8, TN]) for i in range(2)]
        rt = [P.tile(f"fn_r{i}", [128, TN]) for i in range(2)]
        pss = [P.psum(f"fn_ps{i}", [128, 512]) for i in range(2)]
        k = 0
        for b in range(NB):
            for lo in range(CTX, TT, TN):
                n = min(TN, TT - lo)
                i = k % 2
                k += 1
                norm_tile(P, C, src3, b * TT + lo, n, Ht[i], SQ[i], pss[i], rt[i], Xo[i][:, :, 0:n])
                P.dve("tensor_tensor", out=Xo[i][:, :, 0:n], in0=Xo[i][:, :, 0:n], in1=fw.ins(2, n), op=ALU.mult)
                P.dma("pool", o3[:, :, b * TL + lo - CTX:b * TL + lo - CTX + n], Xo[i][:, :, 0:n])


def store_fm(P, dst, row_off, b, name):
    st = [P.tile(f"{name}_st{i}", [128, TN]) for i in range(3)]
    cnt = [0]

    def handler(col, m, tt, ps):
        s = st[cnt[0] % 3]
        cnt[0] += 1
        if cnt[0] % 2:
            P.act("activation", out=s[0:m, :], in_=ps[0:m, 0:TN], func=AF.Copy)
        else:
            P.dve("tensor_copy", out=s[0:m, :], in_=ps[0:m, 0:TN])
        g0 = b * TT + tt * TN
        P.dma("pool", dst[row_off + col:row_off + col + m, g0:g0 + TN], s[0:m, :])
    return handler


def lin_tm(P, C, XT, Wd, col0, ncols, dst, b, name="lt"):
    W3 = Wd.rearrange("(kc p) n -> p kc n", p=128)
    with P.scope():
        w = P.tile(f"{name}_w", [128, 8, ncols])
        P.dma("sp", w, W3[:, :, col0:col0 + ncols])
        pss = [P.psum(f"{name}_ps{i}", [128, 512]) for i in range(2)]
        st = [P.tile(f"{name}_st{i}", [128, ncols]) for i in range(2)]
        for t in range(TT // 128):
            ps = pss[t % 2]
            for kc in range(8):
                P.mm(ps[:, 0:ncols], lhsT=XT[:, kc, t * 128:(t + 1) * 128], rhs=w[:, kc, :],
                     start=(kc == 0), stop=(kc == 7))
            s = st[t % 2]
            if t % 2:
                P.act("activation", out=s, in_=ps[:, 0:ncols], func=AF.Copy)
            else:
                P.dve("tensor_copy", out=s, in_=ps[:, 0:ncols])
            P.dma("pool", dst[b * TT + t * 128:b * TT + (t + 1) * 128, 0:ncols], s)


def rev(v, a0, a1):
    d = v.ap.ap
    return v.raw(a1 - 1, [[d[0][0], d[0][1]], [-1, a1 - a0]])


GELU_C = 2.0 * math.sqrt(2.0 / math.pi)


def phase_lru(P, C, ZO, MIXT, b, o):
    with P.scope():
        sv = P.tile("lru_sv", [128, 4, 11])
        P.dma("sp", sv, C.lru_small[o])
        c8 = P.tile("lru_c8", [128, 4, 2])
        P.act("activation", out=c8, in_=sv[:, :, 9:11], func=AF.Exp, scale=-1.0)
        P.dve("tensor_scalar", out=c8, in0=c8, scalar1=1.0, scalar2=None, op0=ALU.add)
        P.act("activation", out=c8, in_=c8, func=AF.Ln)
        P.dve("tensor_scalar", out=c8, in0=c8, scalar1=-8.0, scalar2=None, op0=ALU.mult)
        bd = P.tile("lru_bd", [128, 4, 128])
        nm = ["up", "gp", "u", "rg0", "rg1", "ig0", "ig1", "a", "x", "t", "h0", "h1"]
        T = {n: P.tile("lru_" + n, [128, TT]) for n in nm}
        pss = [P.psum(f"lru_ps{i}", [128, 512]) for i in range(2)]
        up, gp, u, A, X, Tm = T["up"], T["gp"], T["u"], T["a"], T["x"], T["t"]
        k = 0
        for c in range(4):
            P.dma("sp", bd, C.lru_bd[o, :, :, c].rearrange("z g p n -> p (z g) n"))
            P.dma("sp", up, ZO[c * 128:(c + 1) * 128, b * TT:(b + 1) * TT])
            P.dma("pool", gp, ZO[512 + c * 128:512 + (c + 1) * 128, b * TT:(b + 1) * TT])
            P.act("activation", out=u, in_=up, func=AF.Identity, scale=sv[:, c, 1:2], bias=sv[:, c, 4:5])
            for (a0, a1) in ((0, CTX), (CTX, TT)):
                for (kk, off) in ((0, -1), (2, 1), (3, 2)):
                    lo = a0 + max(0, -off)
                    hi = a1 - max(0, off)
                    P.dve("scalar_tensor_tensor", out=u[:, lo:hi], in0=up[:, lo + off:hi + off],
                          scalar=sv[:, c, kk:kk + 1], in1=u[:, lo:hi], op0=ALU.mult, op1=ALU.add)
            for z in range(2):
                for g in range(2):
                    dst = T[("rg" if g == 0 else "ig") + str(z)]
                    for tt in range(NTT):
                        ps = pss[k % 2]
                        k += 1
                        P.mm(ps[:, 0:TN], lhsT=bd[:, z * 2 + g, :], rhs=u[:, tt * TN:(tt + 1) * TN])
                        P.act("activation", out=dst[:, tt * TN:(tt + 1) * TN], in_=ps[:, 0:TN], func=AF.Sigmoid,
                              bias=sv[:, c, 5 + z * 2 + g:6 + z * 2 + g])
            for z in range(2):
                rg, ig, Hh = T["rg" + str(z)], T["ig" + str(z)], T["h" + str(z)]
                P.act("activation", out=A, in_=rg, func=AF.Exp, scale=c8[:, c, z:z + 1])
                P.pool("tensor_tensor", out=Tm, in0=A, in1=A, op=ALU.mult)
                P.dve("tensor_scalar", out=Tm, in0=Tm, scalar1=-1.0, scalar2=1.0, op0=ALU.mult, op1=ALU.add)
                P.act("activation", out=Tm, in_=Tm, func=AF.Sqrt)
                P.pool("tensor_tensor", out=X, in0=Tm, in1=ig, op=ALU.mult)
                P.dve("tensor_tensor", out=X, in0=X, in1=u, op=ALU.mult)
                if z == 0:
                    P.dve("tensor_tensor_scan", out=Hh[:, 0:CTX], data0=A[:, 0:CTX], data1=X[:, 0:CTX],
                          initial=0.0, op0=ALU.mult, op1=ALU.add)
                    P.dve("tensor_tensor_scan", out=Hh[:, CTX:TT], data0=A[:, CTX:TT], data1=X[:, CTX:TT],
                          initial=Hh[:, CTX - 1:CTX], op0=ALU.mult, op1=ALU.add)
                else:
                    P.dve("tensor_tensor_scan", out=rev(Hh, 0, CTX), data0=rev(A, 0, CTX), data1=rev(X, 0, CTX),
                          initial=0.0, op0=ALU.mult, op1=ALU.add)
                    P.dve("tensor_tensor_scan", out=rev(Hh, CTX, TT), data0=rev(A, CTX, TT), data1=rev(X, CTX, TT),
                          initial=Hh[:, 0:1], op0=ALU.mult, op1=ALU.add)
            Y = T["h0"]
            P.dve("tensor_tensor", out=Y, in0=T["h0"], in1=T["h1"], op=ALU.add)
            P.pool("tensor_tensor", out=Tm, in0=gp, in1=gp, op=ALU.mult)
            P.dve("tensor_scalar", out=Tm, in0=Tm, scalar1=0.044715, scalar2=1.0, op0=ALU.mult, op1=ALU.add)
            P.pool("tensor_tensor", out=Tm, in0=Tm, in1=gp, op=ALU.mult)
            P.act("activation", out=Tm, in_=Tm, func=AF.Sigmoid, scale=GELU_C)
            P.pool("tensor_tensor", out=Tm, in0=Tm, in1=gp, op=ALU.mult)
            P.dve("tensor_tensor", out=Y, in0=Y, in1=Tm, op=ALU.mult)
            P.dma("pool", MIXT[c * 128:(c + 1) * 128, b * TT:(b + 1) * TT], Y)


def phase_na_table(P, C, o, TB2):
    with P.scope():
        R = P.tile("nt_r", [31, 2, 8, 14])
        for krp in range(2):
            P.dma("sp", R[:, krp], C.rpbT[o, :, :, krp:krp + 14])
        sel = P.tile("nt_sel", [31, 2, 64, 128])
        P.dma("sp", sel[:, 0], C.na_sel[0])
        P.dma("pool", sel[:, 1], C.na_sel[1])
        negm = P.tile("nt_neg", [128, 64])
        P.dma("sp", negm, C.na_negm)
        pss = [P.psum(f"nt_ps{i}", [128, 8, 14]) for i in range(2)]
        for c in range(64):
            ps = pss[c % 2]
            P.mm(ps, lhsT=sel[:, 0, c, :], rhs=R[:, 0], start=True, stop=False)
            P.mm(ps, lhsT=sel[:, 1, c, :], rhs=R[:, 1], start=False, stop=True)
            P.dve("tensor_scalar", out=TB2[:, :, :, c], in0=ps, scalar1=negm[:, c:c + 1], scalar2=None, op0=ALU.add)


def phase_na(P, C, ZO, VTOK, MIXT, b, TB2, do_ctx):
    with P.scope():
        QT = [P.tile(f"na_q{i}", [128, TT]) for i in range(4)]
        KT = [P.tile(f"na_k{i}", [128, TT]) for i in range(4)]
        for hp in range(4):
            P.dma("sp", QT[hp], ZO[1024 + hp * 128:1024 + (hp + 1) * 128, b * TT:(b + 1) * TT])
            P.dma("pool", KT[hp], ZO[1536 + hp * 128:1536 + (hp + 1) * 128, b * TT:(b + 1) * TT])
            P.act("activation", out=QT[hp], in_=QT[hp], func=AF.Copy, scale=0.125)
        VC = P.tile("na_vc", [128, 2, 8, 65])
        P.dve("memset", ap=VC, constant=1.0)
        for j in range(2):
            P.dma("sp", VC[:, j, :, 0:64],
                  VTOK[b * TT + j * 128:b * TT + (j + 1) * 128, :].rearrange("p (h f) -> p h f", h=8))
        VW = [P.tile(f"na_vw{i}", [128, 4, 8, 65]) for i in range(2)]
        for i in range(2):
            P.dve("memset", ap=VW[i], constant=1.0)
        QK = [P.tile(f"na_qk{i}", [128, 64]) for i in range(2)]
        Tt = [P.tile(f"na_t{i}", [128, 6, 64]) for i in range(2)]
        E = [P.tile(f"na_e{i}", [128, 6, 64]) for i in range(2)]
        RS = [P.tile(f"na_rs{i}", [64, 1]) for i in range(2)]
        MBS = [P.tile(f"na_mbs{i}", [128, 64]) for i in range(2)]
        OB = [P.tile(f"na_ob{i}", [64, 8, 64]) for i in range(2)]
        OT = [P.tile(f"na_ot{i}", [128, 4, 64]) for i in range(2)]
        STp = [P.psum(f"na_st{i}", [128, 6, 64]) for i in range(2)]
        MBp = [P.psum(f"na_mb{i}", [128, 64]) for i in range(2)]
        OSp = [P.psum(f"na_os{i}", [64, 65]) for i in range(2)]
        TPp = P.psum("na_tp", [128, 4, 64])
        it = [0]

        def block(h, q0, kchunks, vch, bias, ob):
            i = it[0] % 2
            it[0] += 1
            hp, hb = h // 2, (h % 2) * 64
            nk = len(kchunks)
            qk, st, mb, os_, e, tt = QK[i], STp[i], MBp[i], OSp[i], E[i], Tt[i]
            P.dve("tensor_tensor", out=qk[hb:hb + 64, :], in0=QT[hp][hb:hb + 64, q0:q0 + 64],
                  in1=KT[hp][hb:hb + 64, q0:q0 + 64], op=ALU.mult)
            P.mm(mb, lhsT=C.ones[hb:hb + 64, :], rhs=qk[hb:hb + 64, :])
            mbs = MBS[i]
            P.act("activation", out=mbs, in_=mb, func=AF.Copy)
            mb = mbs
            for j, k0 in enumerate(kchunks):
                P.mm(st[:, j, :], lhsT=KT[hp][hb:hb + 64, k0:k0 + 128], rhs=QT[hp][hb:hb + 64, q0:q0 + 64])
            if bias is not None:
                P.dve("tensor_tensor", out=tt[:, 0:4, :], in0=st[:, 0:4, :], in1=bias, op=ALU.add)
                P.dve("tensor_tensor", out=tt[:, 0:4, :], in0=tt[:, 0:4, :], in1=mb.ins(1, 4), op=ALU.subtract)
                P.dve("tensor_tensor", out=tt[:, 4:6, :], in0=st[:, 4:6, :], in1=mb.ins(1, 2), op=ALU.subtract)
            else:
                P.dve("tensor_tensor", out=tt[:, 0:nk, :], in0=st[:, 0:nk, :], in1=mb.ins(1, nk), op=ALU.subtract)
            P.act("activation", out=e[:, 0:nk, :], in_=tt[:, 0:nk, :], func=AF.Exp)
            for j in range(nk):
                P.mm(os_, lhsT=e[:, j, :], rhs=vch[j], start=(j == 0), stop=(j == nk - 1))
            P.dve("reciprocal", out=RS[i], in_=os_[:, 64:65])
            P.act("activation", out=ob[:, h, :], in_=os_[:, 0:64], func=AF.Copy, scale=RS[i])

        mo = MIXT[512:1024, :].rearrange("(c p) t -> p c t", p=128)
        nrow = [0]

        def flush(ob, g0):
            k = nrow[0] % 2
            nrow[0] += 1
            obf = ob.rearrange("p h f -> p (h f)")
            for c in range(4):
                P.pe("transpose", out=TPp[:, c, :], in_=obf[:, c * 128:(c + 1) * 128], identity=C.ident[0:64, 0:64])
            P.dve("tensor_copy", out=OT[k], in_=TPp)
            P.dma("pool", mo[:, :, g0:g0 + 64], OT[k])

        n = 0
        for r in range(32):
            r0 = min(max(r - 4, 0), 24)
            off = r0 - r + 7
            vw = VW[n % 2]
            ob = OB[n % 2]
            n += 1
            t0 = b * TT + CTX + r0 * 64
            for j in range(4):
                P.dma("sp" if j % 2 == 0 else "act", vw[:, j, :, 0:64],
                      VTOK[t0 + j * 128:t0 + (j + 1) * 128, :].rearrange("p (h f) -> p h f", h=8))
            q0 = CTX + r * 64
            kch = [CTX + r0 * 64 + 128 * j for j in range(4)] + [0, 128]
            d = TB2.ap.ap
            for h in range(8):
                vch = [vw[:, j, h, :] for j in range(4)] + [VC[:, j, h, :] for j in range(2)]
                bias = TB2[:, h].raw(off * 64, [[d[0][0], 128], [128, 4], [1, 64]])
                block(h, q0, kch, vch, bias, ob)
            flush(ob, b * TT + q0)
        if do_ctx:
            for qi in range(4):
                ob = OB[n % 2]
                n += 1
                for h in range(8):
                    vch = [VC[:, j, h, :] for j in range(2)]
                    block(h, qi * 64, [0, 128], vch, None, ob)
                flush(ob, b * TT + qi * 64)


EXT_SHAPES.update({
    "lru_small": [2, 128, 4, 11], "lru_bd": [2, 2, 2, 4, 128, 128], "rpbT": [2, 31, 8, 15],
    "na_sel": [2, 31, 64, 128], "na_negm": [128, 64], "dbg_in": [D, NT],
})


def _na_consts():
    sel = np.zeros((2, 31, 64, 128), np.float32)
    negm = np.full((128, 64), -30000.0, np.float32)
    for c in range(64):
        w0 = min(max(c - 8, 0), 48)
        for kc in range(w0, w0 + 16):
            x = kc - c + 15
            sel[0, x, c, kc] = 1.0
            sel[1, x, c, 64 + kc] = 1.0
            negm[kc, c] = 0.0
            negm[64 + kc, c] = 0.0
    return sel, negm


def host_inputs(inputs, names=None):
    f = lambda a: np.ascontiguousarray(np.asarray(a, dtype=np.float32))
    g = lambda k: np.asarray(inputs[k], np.float32)
    x, c, ctx, c_ctx = g("x"), g("c"), g("ctx"), g("c_ctx")
    consts = np.zeros((128, 4, 128), np.float32)
    consts[:, 0, :] = np.eye(128, dtype=np.float32)
    consts[:, 1, :] = 1.0
    consts[:64, 2, :64] = 1.0
    consts[64:, 2, 64:] = 1.0
    consts[:, 3, :] = -1.0
    def pf(a):
        return np.moveaxis(a.reshape(a.shape[:-1] + (4, 128)), -1, 0)
    lru_small = np.zeros((2, 128, 4, 11), np.float32)
    cw, cb, gb, lam = g("lru_conv_w"), g("lru_conv_b"), g("lru_gate_b"), g("lru_lambda")
    for o in range(2):
        lru_small[o, :, :, 0:4] = np.moveaxis(pf(cw[o]), 1, 2)
        lru_small[o, :, :, 4] = pf(cb[o])
        lru_small[o, :, :, 5:9] = np.moveaxis(pf(gb[o]).reshape(128, 4, 4), 1, 2)
        lru_small[o, :, :, 9:11] = np.moveaxis(pf(lam[o]), 1, 2)
    gw = g("lru_gate_w")
    lru_bd = np.zeros((2, 2, 2, 4, 128, 128), np.float32)
    for ch in range(4):
        for n2 in range(2):
            lru_bd[:, :, :, ch, n2 * 64:(n2 + 1) * 64, n2 * 64:(n2 + 1) * 64] = gw[:, :, :, ch * 2 + n2]
    sel, negm = _na_consts()
    shared = {
        "ada_w": g("ada_w"),
        "ada_bT": f(g("ada_b").reshape(DEPTH, 48, 128).transpose(0, 2, 1)),
        "mix_out_w": g("mix_out_w"), "mlp_w1": g("mlp_w1"), "mlp_w2": g("mlp_w2"),
        "ev_in_w": g("ev_in_w"), "od_in_w": g("od_in_w"),
        "consts": consts,
        "fnwT": f(g("final_norm_w").reshape(8, 128).T),
        "lru_small": lru_small, "lru_bd": lru_bd,
        "rpbT": f(g("na_rpb").transpose(0, 3, 1, 2)),
        "na_sel": sel, "na_negm": negm,
    }
    shared.update(host_even(inputs))
    maps = []
    for core in range(NCORES):
        bs = [core * NB + i for i in range(NB)]
        tok = np.concatenate([np.concatenate([ctx[b], x[b]], axis=0) for b in bs], axis=0)
        cond = np.stack([c[bs[0]], c[bs[1]], c_ctx], axis=1)
        m = dict(shared)
        m["xT"] = f(tok.T)
        m["condT"] = f(cond.reshape(8, 128, 3).transpose(1, 0, 2))
        if names is not None:
            m = {k: v for k, v in m.items() if k in names}
        maps.append({k: f(v) for k, v in m.items()})
    return maps


EXT_SHAPES.update({
    "rope": [2, TT, 64], "tri": [2, 128, 128], "ml_gb": [2, 16], "ml_nw": [2, 512],
})
ML_ORDER = [list(range(18)), [1, 0] + list(range(17, 1, -1))]


def phase_mlstm(P, C, XT, HD, QKV, b, e):
    W3 = C.ev_in_w[e].rearrange("(kc p) n -> p kc n", p=128)
    with P.scope():
        Wq = P.tile("m_wq", [128, 8, 512])
        Wk = P.tile("m_wk", [128, 8, 512])
        Wv = P.tile("m_wv", [128, 8, 512])
        Wg = P.tile("m_wg", [128, 8, 16])
        P.dma("sp", Wq, W3[:, :, 0:512])
        P.dma("pool", Wk, W3[:, :, 512:1024])
        P.dma("sp", Wv, W3[:, :, 1024:1536])
        P.dma("pool", Wg, W3[:, :, 2048:2064])
        CS = P.tile("m_cs", [128, 18, 2, 64])
        for i in range(2):
            P.dma("sp", CS[:, :, i, :], C.rope[i].rearrange("(t p) f -> p t f", p=128))
        tri = P.tile("m_tri", [128, 2, 128])
        P.dma("sp", tri, C.tri.rearrange("d p n -> p d n"))
        GB = P.tile("m_gb", [128, 16])
        P.dma("sp", GB, C.ml_gb[e:e + 1, :].pbc(128))
        Caug = P.tile("m_c", [128, 4, 129])
        VAs = [P.tile(f"m_va{i}", [128, 4, 129]) for i in range(2)]
        for i in range(2):
            P.dve("memset", ap=VAs[i], constant=1.0)
        G = P.tile("m_g", [128, 16])
        LF = P.tile("m_lf", [128, 4])
        QS = P.tile("m_qs", [128, 4])
        KS = P.tile("m_ks", [128, 4])
        EBT = P.tile("m_ebt", [128, 4])
        Qc = P.tile("m_qc", [128, 4, 2, 64])
        Kc = P.tile("m_kc", [128, 4, 2, 64])
        QRs = [P.tile(f"m_qr{i}", [128, 4, 2, 64]) for i in range(2)]
        KRs = [P.tile(f"m_kr{i}", [128, 4, 2, 64]) for i in range(2)]
        A1 = P.tile("m_a1", [128, 4, 64])
        A2 = P.tile("m_a2", [128, 4, 64])
        A3 = P.tile("m_a3", [128, 4, 64])
        A4 = P.tile("m_a4", [128, 4, 64])
        Qs = P.tile("m_qsc", [128, 4, 128])
        Ks = P.tile("m_ksc", [128, 4, 128])
        Ke = P.tile("m_ke", [128, 4, 128])
        QsT = P.tile("m_qst", [128, 4, 128])
        KsT = P.tile("m_kst", [128, 4, 128])
        ST = [P.tile(f"m_st{i}", [128, 128]) for i in range(2)]
        dn = P.tile("m_dn", [128, 4])
        Hout = [P.tile(f"m_ho{i}", [128, 4, 128]) for i in range(2)]
        ps_g = P.psum("m_psg", [128, 3, 16])
        ps_q = P.psum("m_psq", [128, 512])
        ps_k = P.psum("m_psk", [128, 512])
        ps_v = P.psum("m_psv", [128, 512])
        ps_t = P.psum("m_pst", [128, 4, 128])
        ps_s = P.psum("m_pss", [128, 128])
        ps_n = [P.psum(f"m_psn{i}", [128, 129]) for i in range(2)]
        it = 0
        for d in range(2):
            P.dve("memset", ap=Caug, constant=0.0)
            for t in ML_ORDER[d]:
                xt = lambda kc: XT[:, kc, t * 128:(t + 1) * 128]
                for kc in range(8):
                    P.mm(ps_g[:, 0, :], lhsT=xt(kc), rhs=Wg[:, kc, :], start=(kc == 0), stop=(kc == 7))
                P.dve("tensor_tensor", out=G, in0=ps_g[:, 0, :], in1=GB, op=ALU.add)
                P.act("activation", out=LF, in_=G[:, 8 + d * 4:12 + d * 4], func=AF.Sigmoid)
                P.act("activation", out=LF, in_=LF, func=AF.Ln)
                P.mm(ps_g[:, 1, 0:4], lhsT=tri[:, d, :], rhs=LF)
                P.mm(ps_g[:, 2, 0:4], lhsT=C.ones, rhs=LF)
                P.act("activation", out=QS, in_=ps_g[:, 1, 0:4], func=AF.Exp)
                P.dve("tensor_tensor", out=KS, in0=G[:, d * 4:d * 4 + 4], in1=ps_g[:, 1, 0:4], op=ALU.subtract)
                P.act("activation", out=KS, in_=KS, func=AF.Exp)
                P.dve("tensor_scalar", out=KS, in0=KS, scalar1=float(128 ** -0.5), scalar2=None, op0=ALU.mult)
                P.act("activation", out=EBT, in_=ps_g[:, 2, 0:4], func=AF.Exp)
                QR, KR, VA = QRs[it % 2], KRs[it % 2], VAs[it % 2]
                rows = slice(b * TT + t * 128, b * TT + (t + 1) * 128)
                if d == 0:
                    for (ps, W) in ((ps_q, Wq), (ps_k, Wk), (ps_v, Wv)):
                        for kc in range(8):
                            P.mm(ps, lhsT=xt(kc), rhs=W[:, kc, :], start=(kc == 0), stop=(kc == 7))
                    P.act("activation", out=Qc, in_=ps_q.rearrange("p (h s f) -> p h s f", h=4, s=2), func=AF.Copy)
                    P.act("activation", out=Kc, in_=ps_k.rearrange("p (h s f) -> p h s f", h=4, s=2), func=AF.Copy)
                    P.act("activation", out=VA[:, :, 0:128], in_=ps_v.rearrange("p (h f) -> p h f", h=4), func=AF.Copy)
                    cosb = CS[:, t, 0, :].ins(1, 4)
                    sinb = CS[:, t, 1, :].ins(1, 4)
                    for (src, dst, eng1, eng2) in ((Qc, QR, "dve", "pool"), (Kc, KR, "pool", "dve")):
                        P.op(eng1, "tensor_tensor", out=A1, in0=src[:, :, 0, :], in1=cosb, op=ALU.mult)
                        P.op(eng2, "tensor_tensor", out=A2, in0=src[:, :, 1, :], in1=sinb, op=ALU.mult)
                        P.op(eng1, "tensor_tensor", out=dst[:, :, 0, :], in0=A1, in1=A2, op=ALU.subtract)
                        P.op(eng2, "tensor_tensor", out=A3, in0=src[:, :, 0, :], in1=sinb, op=ALU.mult)
                        P.op(eng1, "tensor_tensor", out=A4, in0=src[:, :, 1, :], in1=cosb, op=ALU.mult)
                        P.op(eng2, "tensor_tensor", out=dst[:, :, 1, :], in0=A3, in1=A4, op=ALU.add)
                    P.dma("pool", QKV[rows, 0, :], QR.rearrange("p h s f -> p (h s f)"))
                    P.dma("pool", QKV[rows, 1, :], KR.rearrange("p h s f -> p (h s f)"))
                    P.dma("pool", QKV[rows, 2, :].rearrange("p (h f) -> p h f", h=4), VA[:, :, 0:128])
                else:
                    P.dma("sp", QR.rearrange("p h s f -> p (h s f)"), QKV[rows, 0, :])
                    P.dma("sp", KR.rearrange("p h s f -> p (h s f)"), QKV[rows, 1, :])
                    P.dma("sp", VA[:, :, 0:128], QKV[rows, 2, :].rearrange("p (h f) -> p h f", h=4))
                P.dve("tensor_tensor", out=Qs, in0=QR.rearrange("p h s f -> p h (s f)"), in1=QS.ins(2, 128), op=ALU.mult)
                P.pool("tensor_tensor", out=Ks, in0=KR.rearrange("p h s f -> p h (s f)"), in1=KS.ins(2, 128), op=ALU.mult)
                P.pool("tensor_tensor", out=Ke, in0=Ks, in1=EBT.ins(2, 128), op=ALU.mult)
                for h in range(4):
                    P.pe("transpose", out=ps_t[:, h, :], in_=Qs[:, h, :], identity=C.ident)
                P.act("activation", out=QsT, in_=ps_t, func=AF.Copy)
                for h in range(4):
                    P.pe("transpose", out=ps_t[:, h, :], in_=Ks[:, h, :], identity=C.ident)
                P.dve("tensor_copy", out=KsT, in_=ps_t)
                ho = Hout[it % 2]
                it += 1
                for h in range(4):
                    st = ST[h % 2]
                    pn = ps_n[h % 2]
                    P.mm(ps_s, lhsT=KsT[:, h, :], rhs=QsT[:, h, :])
                    P.dve("tensor_tensor", out=st, in0=ps_s, in1=tri[:, d, :], op=ALU.mult)
                    P.mm(pn, lhsT=QsT[:, h, :], rhs=Caug[:, h, :], start=True, stop=False)
                    P.mm(pn, lhsT=st, rhs=VA[:, h, :], start=False, stop=True)
                    P.act("activation", out=dn[:, h:h + 1], in_=pn[:, 128:129], func=AF.Abs)
                    P.dve("tensor_scalar", out=dn[:, h:h + 1], in0=dn[:, h:h + 1], scalar1=1.0, scalar2=None,
                          op0=ALU.max)
                    P.dve("reciprocal", out=dn[:, h:h + 1], in_=dn[:, h:h + 1])
                    P.act("activation", out=ho[:, h, :], in_=pn[:, 0:128], func=AF.Copy, scale=dn[:, h:h + 1])
                    P.mm(pn, lhsT=Ke[:, h, :], rhs=VA[:, h, :])
                    P.dve("scalar_tensor_tensor", out=Caug[:, h, :], in0=Caug[:, h, :], scalar=EBT[:, h:h + 1],
                          in1=pn, op0=ALU.mult, op1=ALU.add)
                P.dma("pool", HD[d][b * TT + t * 128:b * TT + (t + 1) * 128, :], ho.rearrange("p h f -> p (h f)"))


def phase_mlstm_out(P, C, XT, HD, MIXT, b, e):
    W3 = C.ev_in_w[e].rearrange("(kc p) n -> p kc n", p=128)
    with P.scope():
        Wo = P.tile("mo_wo", [128, 8, 512])
        P.dma("sp", Wo, W3[:, :, 1536:2048])
        NW = P.tile("mo_nw", [128, 512])
        P.dma("sp", NW, C.ml_nw[e:e + 1, :].pbc(128))
        H0 = [P.tile(f"mo_h0{i}", [128, 4, 128]) for i in range(2)]
        H1 = [P.tile(f"mo_h1{i}", [128, 4, 128]) for i in range(2)]
        SQ = P.tile("mo_sq", [128, 4, 128])
        ss = P.tile("mo_ss", [128, 4])
        SG = P.tile("mo_sg", [128, 4, 128])
        OT = [P.tile(f"mo_ot{i}", [128, 4, 128]) for i in range(2)]
        ps_o = P.psum("mo_pso", [128, 512])
        ps_t = P.psum("mo_pst", [128, 4, 128])
        mo = MIXT[0:512, :].rearrange("(c p) t -> p c t", p=128)
        for t in range(18):
            i = t % 2
            rows = slice(b * TT + t * 128, b * TT + (t + 1) * 128)
            P.dma("sp", H0[i], HD[0][rows, :].rearrange("p (h f) -> p h f", h=4))
            P.dma("pool", H1[i], HD[1][rows, :].rearrange("p (h f) -> p h f", h=4))
            for kc in range(8):
                P.mm(ps_o, lhsT=XT[:, kc, t * 128:(t + 1) * 128], rhs=Wo[:, kc, :], start=(kc == 0), stop=(kc == 7))
            P.act("activation", out=SG, in_=ps_o.rearrange("p (h f) -> p h f", h=4), func=AF.Sigmoid)
            P.dve("tensor_tensor", out=H0[i], in0=H0[i], in1=H1[i], op=ALU.add)
            P.pool("tensor_tensor", out=SQ, in0=H0[i], in1=H0[i], op=ALU.mult)
            P.dve("tensor_reduce", out=ss, in_=SQ, axis=AX.X, op=ALU.add)
            P.act("activation", out=ss, in_=ss, func=AF.Sqrt, scale=1.0 / 128, bias=C.epsc)
            P.dve("reciprocal", out=ss, in_=ss)
            P.dve("tensor_tensor", out=H0[i], in0=H0[i], in1=ss.ins(2, 128), op=ALU.mult)
            P.pool("tensor_tensor", out=SG, in0=SG, in1=NW.rearrange("p (h f) -> p h f", h=4), op=ALU.mult)
            P.dve("tensor_tensor", out=H0[i], in0=H0[i], in1=SG, op=ALU.mult)
            for h in range(4):
                P.pe("transpose", out=ps_t[:, h, :], in_=H0[i][:, h, :], identity=C.ident)
            P.act("activation", out=OT[i], in_=ps_t, func=AF.Copy)
            P.dma("pool", mo[:, :, b * TT + t * 128:b * TT + (t + 1) * 128], OT[i])


def _rw_sel():
    z = np.zeros((128, 256), np.float32)
    z[:64, 128] = 1.0
    z[64:, 129] = 1.0
    return z


def host_even(inputs):
    g = lambda k: np.asarray(inputs[k], np.float32)
    t = np.arange(TL)
    row = (t // 64).astype(np.float32)
    col = (t % 64).astype(np.float32)
    n_freq = 32
    inv = (np.float32(10000.0) ** (-np.arange(n_freq, dtype=np.float32) / np.float32(n_freq))).astype(np.float32)
    ang = np.concatenate([row[:, None] * inv, col[:, None] * inv], axis=-1).astype(np.float32)
    rope = np.zeros((2, TT, 64), np.float32)
    rope[0, :CTX] = 1.0
    rope[0, CTX:] = np.cos(ang)
    rope[1, CTX:] = np.sin(ang)
    tri = np.zeros((2, 128, 128), np.float32)
    tri[0] = np.triu(np.ones((128, 128), np.float32))
    tri[1] = np.tril(np.ones((128, 128), np.float32))
    gb = g("ml_gate_b")
    ml_gb = np.concatenate([gb[:, :, 0, :].reshape(2, 8), gb[:, :, 1, :].reshape(2, 8)], axis=1)
    pf = lambda a: np.moveaxis(a.reshape(a.shape[:-1] + (4, 128)), -1, 0)
    rw_small = np.zeros((2, 128, 4, 9), np.float32)
    w0, a0 = g("rw_w0"), g("rw_a0")
    for e in range(2):
        rw_small[e, :, :, 0:2] = np.moveaxis(pf(w0[e]), 1, 2)
        rw_small[e, :, :, 2:4] = np.moveaxis(pf(a0[e]), 1, 2)
        rw_small[e, :, :, 4] = pf(g("rw_k_k")[e])
        rw_small[e, :, :, 5] = pf(g("rw_k_a")[e])
        rw_small[e, :, :, 6] = pf(g("rw_r_k")[e].reshape(512))
        rw_small[e, :, :, 7] = pf(g("rw_ln_w")[e])
        rw_small[e, :, :, 8] = pf(g("rw_ln_b")[e])
    return {"rope": rope, "tri": tri, "ml_gb": ml_gb, "ml_nw": g("ml_norm_w"),
            "rw_muT": np.ascontiguousarray(g("rw_mu").reshape(2, 15, 128).transpose(0, 2, 1)),
            "rw_small": rw_small,
            "rw_w_up": g("rw_w_up").reshape(2, 128, 512), "rw_a_up": g("rw_a_up").reshape(2, 128, 512),
            "rw_g_up": g("rw_g_up"), "rw_sel": _rw_sel()}


EXT_SHAPES.update({
    "rw_muT": [2, 128, 15], "rw_small": [2, 128, 4, 9], "rw_w_up": [2, 128, 512], "rw_a_up": [2, 128, 512],
    "rw_g_up": [2, 128, 512], "rw_sel": [128, 256],
})
SEGS = ((0, CTX), (CTX, TT))
RW_DECAY_SCALE = -math.exp(-0.5)


def phase_rwkv_prep(P, C, ZR, SC, VT, b, e):
    cols = slice(b * TT, (b + 1) * TT)
    with P.scope():
        mu = P.tile("rp_mu", [128, 15])
        P.dma("sp", mu, C.rw_muT[e])
        sm = P.tile("rp_sm", [128, 4, 9])
        P.dma("sp", sm, C.rw_small[e])
        WU = P.tile("rp_wu", [128, 512])
        AU = P.tile("rp_au", [128, 512])
        GU = P.tile("rp_gu", [128, 512])
        P.dma("sp", WU, C.rw_w_up[e])
        P.dma("sp", AU, C.rw_a_up[e])
        P.dma("sp", GU, C.rw_g_up[e])
        nm = ["z", "s", "tw", "ad", "sg", "r", "k", "v", "kk", "nr", "x1", "x2", "x3"]
        T = {n: P.tile("rp_" + n, [128, TT]) for n in nm}
        pss = [P.psum(f"rp_ps{i}", [128, 512]) for i in range(4)]
        pst = [P.psum(f"rp_pt{i}", [128, 128]) for i in range(2)]
        vst = [P.tile(f"rp_vst{i}", [128, 128]) for i in range(2)]
        kctr = [0]

        def nps():
            kctr[0] += 1
            return pss[kctr[0] % 4]

        def shiftmix(j, out):
            Zt, S = T["z"], T["s"]
            P.dma("sp", Zt, ZR[j * 128:(j + 1) * 128, cols])
            for (a0, a1) in SEGS:
                P.pool("tensor_copy", out=S[:, a0 + 1:a1], in_=Zt[:, a0:a1 - 1])
                P.pool("memset", ap=S[:, a0:a0 + 1], constant=0.0)
                P.dve("tensor_tensor", out=S[:, a0:a1 - 1], in0=S[:, a0:a1 - 1], in1=Zt[:, a0 + 1:a1], op=ALU.add)
            P.dve("scalar_tensor_tensor", out=S, in0=S, scalar=0.5, in1=Zt, op0=ALU.mult, op1=ALU.subtract)
            P.dve("scalar_tensor_tensor", out=out, in0=S, scalar=mu[:, j:j + 1], in1=Zt, op0=ALU.mult, op1=ALU.add)

        TW, AD, SG = T["tw"], T["ad"], T["sg"]
        shiftmix(12, TW)
        P.act("activation", out=TW, in_=TW, func=AF.Tanh)
        shiftmix(13, AD)
        shiftmix(14, SG)
        P.act("activation", out=SG, in_=SG, func=AF.Sigmoid)
        Rt, Kt, Vt, KK, NR, X1, X2, X3 = (T[n] for n in ("r", "k", "v", "kk", "nr", "x1", "x2", "x3"))
        for hp in range(4):
            rows = slice(hp * 128, (hp + 1) * 128)
            shiftmix(hp, Rt)
            P.dma("pool", SC["r"][rows, cols], Rt)
            shiftmix(4 + hp, Kt)
            P.dma("pool", SC["k"][rows, cols], Kt)
            shiftmix(8 + hp, Vt)
            P.dma("pool", SC["v"][rows, cols], Vt)
            for t in range(18):
                pt = pst[t % 2]
                vs = vst[t % 2]
                P.pe("transpose", out=pt, in_=Vt[:, t * 128:(t + 1) * 128], identity=C.ident)
                P.act("activation", out=vs, in_=pt, func=AF.Copy)
                P.dma("pool", VT.raw((t * 128) * 1024 + b * 256 + hp * 64, [[1024, 128], [512, 2], [1, 64]]),
                      vs.rearrange("p (h f) -> p h f", h=2))
            P.act("activation", out=KK, in_=Kt, func=AF.Copy, scale=sm[:, hp, 4:5])
            P.pool("tensor_tensor", out=X1, in0=KK, in1=KK, op=ALU.mult)
            for tt in range(NTT):
                ps = nps()
                P.mm(ps[:, 0:TN], lhsT=C.blk64, rhs=X1[:, tt * TN:(tt + 1) * TN])
                P.act("activation", out=NR[:, tt * TN:(tt + 1) * TN], in_=ps[:, 0:TN], func=AF.Sqrt)
            P.dve("tensor_scalar", out=NR, in0=NR, scalar1=1e-12, scalar2=None, op0=ALU.max)
            P.dve("reciprocal", out=NR, in_=NR)
            P.dve("tensor_tensor", out=KK, in0=KK, in1=NR, op=ALU.mult)
            P.pool("tensor_scalar", out=X1, in0=KK, scalar1=-1.0, scalar2=None, op0=ALU.mult)
            P.dma("pool", SC["a"][rows, cols], X1)
            for d in range(2):
                pr = slice(d * 64, (d + 1) * 64)
                for tt in range(NTT):
                    tc_ = slice(tt * TN, (tt + 1) * TN)
                    ps = nps()
                    P.mm(ps[:, 0:TN], lhsT=WU[pr, rows], rhs=TW[pr, tc_])
                    P.act("activation", out=X2[:, tc_], in_=ps[:, 0:TN], func=AF.Sigmoid, bias=sm[:, hp, d:d + 1])
                    ps = nps()
                    P.mm(ps[:, 0:TN], lhsT=AU[pr, rows], rhs=AD[pr, tc_])
                    P.act("activation", out=X3[:, tc_], in_=ps[:, 0:TN], func=AF.Sigmoid, bias=sm[:, hp, 2 + d:3 + d])
                P.act("activation", out=X2, in_=X2, func=AF.Exp, scale=RW_DECAY_SCALE)
                P.dma("pool", SC[f"w{d}"][rows, cols], X2)
                P.pool("tensor_tensor", out=NR, in0=KK, in1=X3, op=ALU.mult)
                P.dma("pool", SC[f"bv{d}"][rows, cols], NR)
                P.dve("tensor_scalar", out=X3, in0=X3, scalar1=-1.0, scalar2=sm[:, hp, 5:6], op0=ALU.add, op1=ALU.mult)
                P.dve("scalar_tensor_tensor", out=X3, in0=X3, scalar=1.0, in1=Kt, op0=ALU.add, op1=ALU.mult)
                P.dma("pool", SC[f"kd{d}"][rows, cols], X3)
            for tt in range(NTT):
                tc_ = slice(tt * TN, (tt + 1) * TN)
                ps = nps()
                P.mm(ps[:, 0:TN], lhsT=GU[:, rows], rhs=SG[:, tc_])
                P.act("activation", out=X1[:, tc_], in_=ps[:, 0:TN], func=AF.Copy)
            P.dma("pool", SC["g"][rows, cols], X1)


RW_CS = 64
RW_CV = 8


def tau(d, n):
    if d == 0:
        return n
    return CTX - 1 - n if n < CTX else TT + CTX - 1 - n


def phase_rwkv_scan(P, C, SC, VT, YT, nsteps=TT):
    CS, CV = RW_CS, RW_CV
    qn = ["a", "w", "kd", "bv", "r"]
    with P.scope():
        X = {q: [P.tile(f"rs_{q}{i}", [128, 16, CS]) for i in range(2)] for q in qn}
        VB = [P.tile(f"rs_vb{i}", [128, CV, 16, 64]) for i in range(2)]
        Z = P.tile("rs_z", [128, 256])
        P.dma("sp", Z, C.rw_sel)
        tp_ps = P.psum("rs_tp", [64, 4, 128])
        st = []
        for d in range(2):
            t = {}
            t["S"] = [P.tile(f"rs_s{d}{i}", [128, 8, 64]) for i in range(2)]
            t["P1"] = P.tile(f"rs_p1{d}", [128, 8, 64])
            t["P2"] = [P.tile(f"rs_p2{d}{i}", [128, 8, 64]) for i in range(2)]
            t["Q1"] = P.tile(f"rs_q1{d}", [128, 8, 64])
            t["T2"] = [P.tile(f"rs_t2{d}{i}", [128, 8, 64]) for i in range(2)]
            t["Tt"] = P.tile(f"rs_t{d}", [128, 8, 64])
            t["YS"] = P.tile(f"rs_ys{d}", [128, 8, 64])
            t["YB"] = [P.tile(f"rs_yb{d}{i}", [64, 8, 2, 64]) for i in range(2)]
            t["sa"] = [P.psum(f"rs_sa{d}{i}", [128, 512]) for i in range(2)]
            t["y"] = P.psum(f"rs_y{d}", [128, 512])
            P.dve("memset", ap=t["S"][0], constant=0.0)
            st.append(t)

        def col(d, i):
            return i if d == 0 else CS - 1 - i

        def bc(xt, d, i):
            ps_ = xt.ap.ap[0][0]
            return xt.raw(d * 8 * CS + col(d, i), [[ps_, 128], [CS, 8], [0, 64]])

        def p1_op(d, n):
            i, ci = n % CS, (n // CS) % 2
            t = st[d]
            P.dve("tensor_tensor", out=t["P1"], in0=t["S"][n % 2], in1=bc(X["a"][ci], d, i), op=ALU.mult)
            P.mm(t["sa"][n % 2], lhsT=C.blk64, rhs=t["P1"].rearrange("p g f -> p (g f)"))

        def loads(n):
            i = n % CS
            ci = (n // CS) % 2
            if i == 0:
                for q in qn:
                    for d in range(2):
                        src = SC[q if q in ("a", "r") else f"{q}{d}"]
                        t0 = tau(d, n) if d == 0 else tau(d, n + CS - 1)
                        for b in range(NB):
                            g0 = d * 8 + b * 4
                            P.dma("sp", X[q][ci][:, g0:g0 + 4, :],
                                  src[:, b * TT + t0:b * TT + t0 + CS].rearrange("(hp p) t -> p hp t", p=128))
            iv = n % CV
            vi = (n // CV) % 2
            if iv == 0 and not getattr(C, "no_vb", False):
                vb = VB[vi]
                ps_v = vb.ap.ap[0][0]
                for d in range(2):
                    sgn = 1 if d == 0 else -1
                    for h2 in range(2):
                        o = vb[h2 * 64:(h2 + 1) * 64].raw(d * 8 * 64, [[ps_v, 64], [16 * 64, CV], [1, 512]])
                        s_ = VT.raw(tau(d, n) * 1024 + h2 * 512, [[0, 64], [sgn * 1024, CV], [1, 512]])
                        P.dma("act" if (d + h2) % 2 else "sp", o, s_)

        loads(0)
        for d in range(2):
            p1_op(d, 0)
        for n in range(nsteps):
            i = n % CS
            ci = (n // CS) % 2
            iv = n % CV
            vi = (n // CV) % 2
            nl = n % 64
            t2all = []
            for d in range(2):
                t = st[d]
                t2 = t["T2"][n % 2]
                parts = []
                for g in range(8):
                    tg = P.part(t2, g)
                    parts.append(tg.bufs[0])
                    P.act("activation", out=tg[:, g, :], in_=VB[vi][:, iv, d * 8 + g, :], func=AF.Copy,
                          scale=X["kd"][ci][:, d * 8 + g, col(d, i):col(d, i) + 1])
                t2all.append(V(t2.ap, tuple(parts)))
            for d in range(2):
                t = st[d]
                P.pool("tensor_tensor", out=t["Q1"], in0=t["S"][n % 2], in1=bc(X["w"][ci], d, i), op=ALU.mult)
                P.pool("tensor_tensor", out=t["Q1"], in0=t["Q1"], in1=t2all[d], op=ALU.add)
            for d in range(2):
                t = st[d]
                P.dve("tensor_tensor", out=t["Tt"], in0=t["sa"][n % 2].rearrange("p (g f) -> p g f", f=64),
                      in1=bc(X["bv"][ci], d, i), op=ALU.mult)
                P.dve("tensor_tensor", out=t["S"][(n + 1) % 2], in0=t["Q1"], in1=t["Tt"], op=ALU.add)
            if n + 1 < nsteps:
                loads(n + 1)
                for d in range(2):
                    p1_op(d, n + 1)
            for d in range(2):
                t = st[d]
                p2 = t["P2"][n % 2]
                P.dve("tensor_tensor", out=p2, in0=t["S"][(n + 1) % 2], in1=bc(X["r"][ci], d, i), op=ALU.mult)
                P.mm(t["y"], lhsT=Z[:, 128 - 2 * nl:256 - 2 * nl], rhs=p2.rearrange("p g f -> p (g f)"),
                     start=(nl == 0), stop=(nl == 63))
            if nl == 63:
                n64 = n - 63
                for d in range(2):
                    t = st[d]
                    yb = t["YB"][(n // 64) % 2]
                    ps_y = yb.ap.ap[0][0]
                    ps_t = tp_ps.ap.ap[0][0]
                    P.act("activation", out=t["YS"].rearrange("p g f -> p (g f)"), in_=t["y"], func=AF.Copy)
                    for gb in range(2):
                        for k in range(4):
                            P.pe("transpose", out=tp_ps[:, k, :], in_=t["YS"][:, gb * 4 + k, :], identity=C.ident)
                        if d == 0:
                            o = yb.raw(gb * 4 * 128, [[ps_y, 64], [128, 4], [64, 2], [1, 64]])
                        else:
                            o = yb.raw(gb * 4 * 128 + 63, [[ps_y, 64], [128, 4], [64, 2], [-1, 64]])
                        s_ = tp_ps.raw(0, [[ps_t, 64], [128, 4], [1, 2], [2, 64]])
                        P.dve("tensor_copy", out=o, in_=s_)
                    for b in range(NB):
                        tok0 = n64 if d == 0 else tau(1, n64) - 63
                        s_ = yb.raw(b * 4 * 128, [[ps_y, 64], [64, 8], [1, 64]])
                        o = YT[d].raw(b * TT + tok0, [[NT, 64], [64 * NT, 8], [1, 64]])
                        P.dma("pool", o, s_)


def phase_rwkv_out(P, C, SC, YT, MIXT, b, e):
    with P.scope():
        sm = P.tile("ro_sm", [128, 4, 9])
        P.dma("sp", sm, C.rw_small[e])
        epl = P.tile("ro_eps", [128, 1])
        P.dve("memset", ap=epl, constant=64e-5)
        nm = ["y0", "y1", "r", "k", "v", "g", "yc", "sq", "rs", "o"]
        T = [{n: P.tile(f"ro_{n}{i}", [128, TN]) for n in nm} for i in range(2)]
        pss = [P.psum(f"ro_ps{i}", [128, 512]) for i in range(6)]
        it = 0
        for hp in range(4):
            rows = slice(hp * 128, (hp + 1) * 128)
            for tt in range(NTT):
                t = T[it % 2]
                pm, pv, pb = pss[(it % 2) * 3], pss[(it % 2) * 3 + 1], pss[(it % 2) * 3 + 2]
                it += 1
                cs = slice(b * TT + tt * TN, b * TT + (tt + 1) * TN)
                P.dma("sp", t["y0"], YT[0][rows, cs])
                P.dma("sp", t["y1"], YT[1][rows, cs])
                for q in ("r", "k", "v", "g"):
                    P.dma("act", t[q], SC[q][rows, cs])
                P.dve("tensor_tensor", out=t["y0"], in0=t["y0"], in1=t["y1"], op=ALU.add)
                P.mm(pm[:, 0:TN], lhsT=C.blk64, rhs=t["y0"])
                P.dve("scalar_tensor_tensor", out=t["yc"], in0=pm[:, 0:TN], scalar=-1.0 / 64, in1=t["y0"],
                      op0=ALU.mult, op1=ALU.add)
                P.pool("tensor_tensor", out=t["sq"], in0=t["yc"], in1=t["yc"], op=ALU.mult)
                P.mm(pv[:, 0:TN], lhsT=C.blk64, rhs=t["sq"])
                P.act("activation", out=t["rs"], in_=pv[:, 0:TN], func=AF.Sqrt, scale=1.0 / 64, bias=epl)
                P.dve("reciprocal", out=t["rs"], in_=t["rs"])
                P.dve("tensor_tensor", out=t["yc"], in0=t["yc"], in1=t["rs"], op=ALU.mult)
                P.act("activation", out=t["yc"], in_=t["yc"], func=AF.Identity, scale=sm[:, hp, 7:8], bias=sm[:, hp, 8:9])
                P.dve("scalar_tensor_tensor", out=t["sq"], in0=t["r"], scalar=sm[:, hp, 6:7], in1=t["k"],
                      op0=ALU.mult, op1=ALU.mult)
                P.mm(pb[:, 0:TN], lhsT=C.blk64, rhs=t["sq"])
                P.dve("tensor_tensor", out=t["o"], in0=pb[:, 0:TN], in1=t["v"], op=ALU.mult)
                P.pool("tensor_tensor", out=t["o"], in0=t["o"], in1=t["yc"], op=ALU.add)
                P.pool("tensor_tensor", out=t["o"], in0=t["o"], in1=t["g"], op=ALU.mult)
                P.dma("pool", MIXT[512 + hp * 128:512 + (hp + 1) * 128, cs], t["o"])


def build_full(nlayers=DEPTH, final=True):
    nc, P, C = build()
    C.outT = V(nc.dram_tensor("outT", [D, NB * TL], F32, kind="ExternalOutput").ap(), Buf("outT"))
    H = P.dram("H", [D, NT])
    MIXT = P.dram("MIXT", [D, NT])
    ZR = P.dram("ZR", [RW_IN, NT])
    SC = {q: P.dram("SC_" + q, [512, NT]) for q in ["w0", "w1", "kd0", "kd1", "bv0", "bv1", "a", "r", "k", "v", "g"]}
    VT = P.dram("VT", [TT, 1024])
    YT = [P.dram("YT0", [512, NT]), P.dram("YT1", [512, NT])]
    HD = [P.dram("HD0", [NT, 512]), P.dram("HD1", [NT, 512])]
    QKV = P.dram("QKV", [NT, 3, 512])
    ZO = P.dram("ZO", [2048, NT])
    VTOK = P.dram("VTOK", [NT, 512])
    with nc.named_scope('mod'):
        phase_mod(P, C)
    for l in range(nlayers):
        hsrc = C.xT if l == 0 else H
        if l % 2 == 0:
            e = l // 2
            for b in range(NB):
                with P.scope():
                    XT = P.tile("XT", [128, 8, TT])
                    with nc.named_scope(f'nm1_{l}_{b}'):
                        phase_normmod(P, C, hsrc, b, l, 0, 1, XT)
                    with nc.named_scope(f'inproj_{l}_{b}'):
                        with P.scope():
                            lin_fm(P, C, XT, 8, C.ev_in_w[e], ML_IN, RW_IN, store_fm(P, ZR, -ML_IN, b, "zr"), name="ei")
                    with nc.named_scope(f'mlstm_{l}_{b}'):
                        phase_mlstm(P, C, XT, HD, QKV, b, e)
                    with nc.named_scope(f'mlstmout_{l}_{b}'):
                        phase_mlstm_out(P, C, XT, HD, MIXT, b, e)
                with nc.named_scope(f'rwprep_{l}_{b}'):
                    phase_rwkv_prep(P, C, ZR, SC, VT, b, e)
            with nc.named_scope(f'rwscan_{l}'):
                phase_rwkv_scan(P, C, SC, VT, YT)
            for b in range(NB):
                with nc.named_scope(f'rwout_{l}_{b}'):
                    phase_rwkv_out(P, C, SC, YT, MIXT, b, e)
        else:
            o = l // 2
            with P.scope():
                TB2 = P.tile("TB2", [128, 8, 14, 64])
                with nc.named_scope(f'natab_{l}'):
                    phase_na_table(P, C, o, TB2)
                for b in range(NB):
                    with P.scope():
                        XT = P.tile("XT", [128, 8, TT])
                        with nc.named_scope(f'nm1_{l}_{b}'):
                            phase_normmod(P, C, hsrc, b, l, 0, 1, XT)
                        with nc.named_scope(f'inproj_{l}_{b}'):
                            with P.scope():
                                lin_fm(P, C, XT, 8, C.od_in_w[o], 0, 2048, store_fm(P, ZO, 0, b, "zo"), name="oi")
                            lin_tm(P, C, XT, C.od_in_w[o], 2048, 512, VTOK, b)
                    with nc.named_scope(f'lru_{l}_{b}'):
                        phase_lru(P, C, ZO, MIXT, b, o)
                    with nc.named_scope(f'na_{l}_{b}'):
                        phase_na(P, C, ZO, VTOK, MIXT, b, TB2, l != DEPTH - 1)
        for b in range(NB):
            with P.scope():
                XT = P.tile("XT", [128, 8, TT])
                with nc.named_scope(f'outproj_{l}_{b}'):
                    phase_outproj(P, C, MIXT, b, l, hsrc, H, XT)
            with nc.named_scope(f'mlp_{l}_{b}'):
                phase_mlp(P, C, H, b, l)
    if final:
        with nc.named_scope('final'):
            phase_final(P, C, H)
    P.finish()
    return nc, P, C


def kernel(**inputs):
    nc, P, C = build_full()
    names = set(C._ext.keys())
    maps = host_inputs(inputs, names)
    res = run_bass_kernel_spmd(nc, maps, core_ids=list(range(NCORES)))
    out = np.empty((NCORES * NB, TL, D), np.float32)
    for core in range(NCORES):
        o = np.asarray(res.results[core]["outT"], np.float32)
        for i in range(NB):
            out[core * NB + i] = o[:, i * TL:(i + 1) * TL].T
    return out
```

```python
import contextlib
import math
import numpy as np
import concourse.bass as bass
import concourse.mybir as mybir
from concourse.ap import AP
from concourse.bass_utils import run_bass_kernel_spmd

F32 = mybir.dt.float32
AF = mybir.ActivationFunctionType
ALU = mybir.AluOpType
AX = mybir.AxisListType

NCORES = 8
D = 1024
DEPTH = 4
NB = 2
CTX = 256
TL = 2048
TT = CTX + TL
NT = NB * TT
TN = 384
NTT = TT // TN
DFF = 4096
EVEN_IN = 3984
ODD_IN = 2560
ML_IN = 2064
RW_IN = 1920
EPS = 1e-6


class Buf:
    __slots__ = ("w", "r", "name")

    def __init__(self, name=""):
        self.w = {}
        self.r = {}
        self.name = name


class V:
    __slots__ = ("ap", "bufs")

    def __init__(self, ap, bufs):
        self.ap = ap
        self.bufs = bufs if isinstance(bufs, tuple) else (bufs,)

    def __getitem__(self, key):
        return V(self.ap[key], self.bufs)

    def rearrange(self, pattern, **kw):
        return V(self.ap.rearrange(pattern, **kw), self.bufs)

    def with_ap(self, ap):
        return V(ap, self.bufs)

    def raw(self, off, dims):
        return V(AP(self.ap.tensor, self.ap.offset + off, [list(d) for d in dims]), self.bufs)

    def ins(self, axis, n):
        dims = [list(d) for d in self.ap.ap]
        dims.insert(axis, [0, n])
        return V(AP(self.ap.tensor, self.ap.offset, dims), self.bufs)

    def pbc(self, n=128):
        return V(self.ap.partition_broadcast(n), self.bufs)

    @property
    def shape(self):
        return self.ap.shape


HMAP = {"pe": "tensor", "dve": "vector", "act": "scalar", "pool": "gpsimd", "sp": "sync"}


class Prog:
    NDSEM = 10

    def __init__(self, nc):
        self.nc = nc
        self.es = contextlib.ExitStack()
        self.engs = ["pe", "dve", "act", "pool", "sp"]
        self.h = {e: getattr(nc, HMAP[e]) for e in self.engs}
        self.sem = {e: self.es.enter_context(nc.semaphore("s_" + e)) for e in self.engs}
        self.cnt = {e: 0 for e in self.engs}
        self.seen = {e: {} for e in self.engs}
        self.dq = ["sp", "act", "pool"]
        self.dsem = {}
        self.dcount = {q: 0 for q in self.dq}
        for q in self.dq:
            for i in range(self.NDSEM):
                self.dsem[(q, i)] = self.es.enter_context(nc.semaphore(f"d_{q}_{i}"))
        self.nparts = {}
        self.nwaits = 0
        self.nops = 0
        self.scopes = []

    @contextlib.contextmanager
    def scope(self):
        es = contextlib.ExitStack()
        self.scopes.append(es)
        try:
            yield
        finally:
            self.barrier()
            self.scopes.pop()
            es.close()

    def _es(self):
        return self.scopes[-1] if self.scopes else self.es

    def _nm(self, name):
        self.uid = getattr(self, "uid", 0) + 1
        return f"{name}_{self.uid}"

    def tile(self, name, shape, dtype=F32):
        t = self._es().enter_context(self.nc.sbuf_tensor(self._nm("t_" + name), list(shape), dtype))
        return V(t[:], Buf(name))

    def psum(self, name, shape, dtype=F32):
        t = self._es().enter_context(self.nc.psum_tensor(self._nm("p_" + name), list(shape), dtype))
        return V(t[:], Buf(name))

    def dram(self, name, shape, dtype=F32, kind="Internal"):
        t = self.nc.dram_tensor("d_" + name, list(shape), dtype, kind=kind)
        return V(t.ap(), Buf(name))

    def part(self, v, key):
        k = (id(v.bufs[0]), key)
        if k not in self.nparts:
            self.nparts[k] = Buf(str(key))
        return V(v.ap, self.nparts[k])

    def _semh(self, k):
        return self.sem[k] if isinstance(k, str) else self.dsem[k]

    def _wait(self, eng, tok):
        if tok is None:
            return
        semkey, val, src = tok
        if src == eng and eng == "pe":
            return
        if self.seen[eng].get(semkey, 0) >= val:
            return
        self.seen[eng][semkey] = val
        self.h[eng].wait_ge(self._semh(semkey), val)
        self.nwaits += 1

    def _deps(self, eng, reads, writes):
        for v in reads:
            for b in v.bufs:
                for sk, (val, src) in b.w.items():
                    self._wait(eng, (sk, val, src))
        for v in writes:
            for b in v.bufs:
                for sk, (val, src) in b.w.items():
                    self._wait(eng, (sk, val, src))
                for sk, (val, src) in b.r.items():
                    self._wait(eng, (sk, val, src))

    def _mark(self, tok, reads, writes):
        wb = set()
        for v in writes:
            for b in v.bufs:
                b.w[tok[0]] = (tok[1], tok[2])
                b.r = {}
                wb.add(id(b))
        for v in reads:
            for b in v.bufs:
                if id(b) in wb:
                    continue
                b.r[tok[0]] = (tok[1], tok[2])

    def op(self, eng, method, **kw):
        reads, writes = [], []
        kw2 = {}
        for k, a in kw.items():
            if isinstance(a, V):
                (writes if k in ("out", "accum_out", "ap") else reads).append(a)
                kw2[k] = a.ap
            else:
                kw2[k] = a
        self._deps(eng, reads, writes)
        self.cnt[eng] += 1
        tok = (eng, self.cnt[eng], eng)
        getattr(self.h[eng], method)(**kw2).then_inc(self.sem[eng], 1)
        self._mark(tok, reads, writes)
        self.nops += 1
        return tok

    def dma(self, q, out, in_, **kw):
        k = self.dcount[q]
        self.dcount[q] += 1
        idx = k % self.NDSEM
        rnd = k // self.NDSEM
        semkey = (q, idx)
        if rnd > 0:
            self._wait(q, (semkey, 16 * rnd, "dma"))
        self._deps(q, [in_], [out])
        tok = (semkey, 16 * (rnd + 1), "dma")
        self.h[q].dma_start(out=out.ap, in_=in_.ap, **kw).then_inc(self.dsem[semkey], 16)
        self._mark(tok, [in_], [out])
        return tok

    def pe(self, method, **kw):
        return self.op("pe", method, **kw)

    def dve(self, method, **kw):
        return self.op("dve", method, **kw)

    def act(self, method, **kw):
        return self.op("act", method, **kw)

    def pool(self, method, **kw):
        return self.op("pool", method, **kw)

    def mm(self, out, lhsT, rhs, start=True, stop=True):
        return self.op("pe", "matmul", out=out, lhsT=lhsT, rhs=rhs, start=start, stop=stop)

    def barrier(self, engs=None):
        engs = engs or self.engs
        for e in engs:
            for q in self.dq:
                k = self.dcount[q]
                for idx in range(self.NDSEM):
                    n = (k - idx + self.NDSEM - 1) // self.NDSEM if k > idx else 0
                    if n > 0:
                        self._wait(e, ((q, idx), 16 * n, "dma"))
            for e2 in ["pe", "dve", "act", "pool"]:
                if e2 != e and self.cnt[e2] > 0:
                    self._wait(e, (e2, self.cnt[e2], e2))

    def finish(self):
        self.barrier(["sp"])
        self.es.close()


def segs(b, lo, n):
    hi = lo + n
    out = []
    if lo < CTX:
        out.append((0, min(hi, CTX) - lo, 2))
    if hi > CTX:
        out.append((max(lo, CTX) - lo, n, b))
    return out


def phase_mod(P, C):
    M = C.M
    with P.scope():
        cs = P.tile("cs", [128, 8, 3])
        P.dma("sp", cs, C.condT)
        P.act("activation", out=cs, in_=cs, func=AF.Silu)
        bt = P.tile("adab", [128, 4, 48])
        P.dma("sp", bt, C.ada_bT.rearrange("l p f -> p l f"))
        wb = [P.tile(f"mw{i}", [128, 8, 512]) for i in range(2)]
        ps = [P.psum(f"mps{i}", [128, 4, 4]) for i in range(2)]
        it = 0
        for l in range(DEPTH):
            W3 = C.ada_w[l].rearrange("(kc p) f -> p kc f", p=128)
            for g in range(12):
                w = wb[it % 2]
                pp = ps[it % 2]
                P.dma("sp" if it % 2 == 0 else "pool", w, W3[:, :, g * 512:(g + 1) * 512])
                for f4 in range(4):
                    for kc in range(8):
                        P.mm(pp[:, f4, 0:3], lhsT=w[:, kc, f4 * 128:(f4 + 1) * 128], rhs=cs[:, kc, :],
                             start=(kc == 0), stop=(kc == 7))
                P.dve("tensor_tensor", out=M[:, l, g * 4:(g + 1) * 4, :], in0=pp[:, :, 0:3],
                      in1=bt[:, l, g * 4:(g + 1) * 4].ins(2, 3), op=ALU.add)
                it += 1
            for j in (1, 4):
                P.dve("tensor_scalar", out=M[:, l, j * 8:(j + 1) * 8, :], in0=M[:, l, j * 8:(j + 1) * 8, :],
                      scalar1=1.0, scalar2=None, op0=ALU.add)


def norm_tile(P, C, src3, g0, n, Ht, SQ, pss, rt, out3, kcs=8):
    P.dma("sp", Ht[:, :, 0:n], src3[:, :, g0:g0 + n])
    P.act("activation", out=SQ[:, :, 0:n], in_=Ht[:, :, 0:n], func=AF.Square)
    for kc in range(8):
        P.mm(pss[:, 0:n], lhsT=C.ones, rhs=SQ[:, kc, 0:n], start=(kc == 0), stop=(kc == 7))
    P.act("activation", out=rt[:, 0:n], in_=pss[:, 0:n], func=AF.Sqrt, scale=1.0 / D, bias=C.epsc)
    P.dve("reciprocal", out=rt[:, 0:n], in_=rt[:, 0:n])
    P.dve("tensor_tensor", out=out3, in0=Ht[:, :, 0:n], in1=rt[:, 0:n].ins(1, 8), op=ALU.mult)


def modulate(P, C, l, jsh, jsc, b, lo, n, x3):
    for (s0, s1, col) in segs(b, lo, n):
        for kc in range(8):
            P.act("activation", out=x3[:, kc, s0:s1], in_=x3[:, kc, s0:s1], func=AF.Identity,
                  scale=C.M[:, l, jsc * 8 + kc, col:col + 1], bias=C.M[:, l, jsh * 8 + kc, col:col + 1])


def phase_normmod(P, C, src, b, l, jsh, jsc, XT):
    src3 = src.rearrange("(kc p) t -> p kc t", p=128)
    with P.scope():
        Ht = [P.tile(f"nm_h{i}", [128, 8, TN]) for i in range(2)]
        SQ = [P.tile(f"nm_sq{i}", [128, 8, TN]) for i in range(2)]
        rt = [P.tile(f"nm_r{i}", [128, TN]) for i in range(2)]
        pss = [P.psum(f"nm_ps{i}", [128, 512]) for i in range(2)]
        for tt in range(NTT):
            i = tt % 2
            x3 = XT[:, :, tt * TN:(tt + 1) * TN]
            norm_tile(P, C, src3, b * TT + tt * TN, TN, Ht[i], SQ[i], pss[i], rt[i], x3)
            modulate(P, C, l, jsh, jsc, b, tt * TN, TN, x3)


def lin_fm(P, C, XT, KC, Wd, col0, ncols, handler, GW=512, ntt=NTT, tn=TN, name="lf"):
    W3 = Wd.rearrange("(kc p) n -> p kc n", p=128)
    with P.scope():
        wb = [P.tile(f"{name}_w{i}", [128, KC, GW]) for i in range(2)]
        pss = [P.psum(f"{name}_ps{i}", [128, 512]) for i in range(3)]
        k = 0
        ng = (ncols + GW - 1) // GW
        for g in range(ng):
            c0 = col0 + g * GW
            gw = min(GW, col0 + ncols - c0)
            w = wb[g % 2]
            P.dma("sp", w[:, :, 0:gw], W3[:, :, c0:c0 + gw])
            for oc in range((gw + 127) // 128):
                m = min(128, gw - oc * 128)
                for tt in range(ntt):
                    ps = pss[k % 3]
                    k += 1
                    for kc in range(KC):
                        P.mm(ps[0:m, 0:tn], lhsT=w[:, kc, oc * 128:oc * 128 + m],
                             rhs=XT[:, kc, tt * tn:(tt + 1) * tn], start=(kc == 0), stop=(kc == KC - 1))
                    handler(c0 + oc * 128, m, tt, ps)


EXT_SHAPES = {
    "xT": [D, NT], "condT": [128, 8, 3], "ada_w": [DEPTH, D, 6 * D], "ada_bT": [DEPTH, 128, 48],
    "mix_out_w": [DEPTH, D, D], "mlp_w1": [DEPTH, D, DFF], "mlp_w2": [DEPTH, DFF, D],
    "ev_in_w": [2, D, EVEN_IN], "od_in_w": [2, D, ODD_IN], "consts": [128, 4, 128],
    "fnwT": [128, 8],
}


class Ctx:
    def __init__(self, nc, P):
        self.nc = nc
        self.P = P
        self._ext = {}

    def ext(self, name):
        if name not in self._ext:
            t = self.nc.dram_tensor(name, list(EXT_SHAPES[name]), F32, kind="ExternalInput")
            self._ext[name] = V(t.ap(), Buf(name))
        return self._ext[name]

    def __getattr__(self, name):
        if name in EXT_SHAPES:
            return self.ext(name)
        raise AttributeError(name)


def build(dbg=None):
    nc = bass.Bass("TRN2", target_bir_lowering=False)
    P = Prog(nc)
    C = Ctx(nc, P)
    C.dbg = {}
    for nm, shp in (dbg or {}).items():
        C.dbg[nm] = V(nc.dram_tensor("dbg_" + nm, list(shp), F32, kind="ExternalOutput").ap(), Buf(nm))
    cst = P.tile("cst", [128, 4, 128])
    P.dma("sp", cst, C.consts)
    C.ident = cst[:, 0, :]
    C.ones = cst[:, 1, :]
    C.blk64 = cst[:, 2, :]
    C.negones = cst[:, 3, :]
    C.epsc = P.tile("epsc", [128, 1])
    P.dve("memset", ap=C.epsc, constant=EPS)
    C.M = P.tile("M", [128, DEPTH, 48, 3])
    return nc, P, C


def phase_outproj(P, C, MIXT, b, l, hsrc, hdst, XT):
    m3 = MIXT.rearrange("(kc p) t -> p kc t", p=128)
    for kc in range(8):
        P.dma("sp" if kc % 2 == 0 else "pool", XT[:, kc, :], m3[:, kc, b * TT:(b + 1) * TT])
    with P.scope():
        ho = [P.tile(f"op_ho{i}", [128, TN]) for i in range(3)]
        hn = [P.tile(f"op_hn{i}", [128, TN]) for i in range(3)]
        cnt = [0]

        def handler(col, m, tt, ps):
            i = cnt[0] % 3
            cnt[0] += 1
            oc = col // 128
            g0 = b * TT + tt * TN
            key = ("h", oc, b, tt)
            P.dma("sp", ho[i], P.part(hsrc, key)[col:col + 128, g0:g0 + TN])
            for (s0, s1, cc) in segs(b, tt * TN, TN):
                P.dve("scalar_tensor_tensor", out=hn[i][:, s0:s1], in0=ps[:, s0:s1],
                      scalar=C.M[:, l, 16 + oc, cc:cc + 1], in1=ho[i][:, s0:s1], op0=ALU.mult, op1=ALU.add)
            P.dma("pool", P.part(hdst, key)[col:col + 128, g0:g0 + TN], hn[i])

        lin_fm(P, C, XT, 8, C.mix_out_w[l], 0, D, handler, name="op")


def phase_mlp(P, C, H, b, l):
    src3 = H.rearrange("(kc p) t -> p kc t", p=128)
    with P.scope():
        Ht = P.tile("ml_h", [128, 8, TN])
        SQ = P.tile("ml_sq", [128, 8, TN])
        Xt = P.tile("ml_x", [128, 8, TN])
        rt = P.tile("ml_r", [128, TN])
        pss = P.psum("ml_ps", [128, 512])
        H1 = P.tile("ml_h1", [128, 32, TN])
        hn = [P.tile(f"ml_hn{i}", [128, TN]) for i in range(2)]
        cnt = [0]
        for tt in range(NTT):
            g0 = b * TT + tt * TN
            for kc in range(8):
                pass
            hv = V(src3.ap, tuple(P.part(H, ("h", kc, b, tt)).bufs[0] for kc in range(8)))
            norm_tile(P, C, hv, g0, TN, Ht, SQ, pss, rt, Xt)
            modulate(P, C, l, 3, 4, b, tt * TN, TN, Xt)

            def h1(col, m, _t, ps):
                oc = col // 128
                P.act("activation", out=H1[:, oc, :], in_=ps[:, 0:TN], func=AF.Relu)
                P.pool("tensor_tensor", out=H1[:, oc, :], in0=H1[:, oc, :], in1=H1[:, oc, :], op=ALU.mult)

            lin_fm(P, C, Xt, 8, C.mlp_w1[l], 0, DFF, h1, ntt=1, name="m1")

            def h2(col, m, _t, ps):
                i = cnt[0] % 2
                cnt[0] += 1
                oc = col // 128
                for (s0, s1, cc) in segs(b, tt * TN, TN):
                    P.dve("scalar_tensor_tensor", out=hn[i][:, s0:s1], in0=ps[:, s0:s1],
                          scalar=C.M[:, l, 40 + oc, cc:cc + 1], in1=Ht[:, oc, s0:s1], op0=ALU.mult, op1=ALU.add)
                P.dma("pool", P.part(H, ("h", oc, b, tt))[col:col + 128, g0:g0 + TN], hn[i])

            lin_fm(P, C, H1, 32, C.mlp_w2[l], 0, D, h2, GW=256, ntt=1, name="m2")


def phase_final(P, C, H):
    src3 = H.rearrange("(kc p) t -> p kc t", p=128)
    o3 = C.outT.rearrange("(kc p) t -> p kc t", p=128)
    with P.scope():
        fw = P.tile("fn_w", [128, 8])
        P.dma("sp", fw, C.fnwT)
        Ht = [P.tile(f"fn_h{i}", [128, 8, TN]) for i in range(2)]
        SQ = [P.tile(f"fn_sq{i}", [128, 8, TN]) for i in range(2)]
        Xo = [P.tile(f"fn_x{i}", [128, 8, TN]) for i in range(2)]
        rt = [P.tile(f"fn_r{i}", [128, TN]) for i in range(2)]
        pss = [P.psum(f"fn_ps{i}", [128, 512]) for i in range(2)]
        k = 0
        for b in range(NB):
            for lo in range(CTX, TT, TN):
                n = min(TN, TT - lo)
                i = k % 2
                k += 1
                norm_tile(P, C, src3, b * TT + lo, n, Ht[i], SQ[i], pss[i], rt[i], Xo[i][:, :, 0:n])
                P.dve("tensor_tensor", out=Xo[i][:, :, 0:n], in0=Xo[i][:, :, 0:n], in1=fw.ins(2, n), op=ALU.mult)
                P.dma("pool", o3[:, :, b * TL + lo - CTX:b * TL + lo - CTX + n], Xo[i][:, :, 0:n])


def store_fm(P, dst, row_off, b, name):
    st = [P.tile(f"{name}_st{i}", [128, TN]) for i in range(3)]
    cnt = [0]

    def handler(col, m, tt, ps):
        s = st[cnt[0] % 3]
        cnt[0] += 1
        if cnt[0] % 2:
            P.act("activation", out=s[0:m, :], in_=ps[0:m, 0:TN], func=AF.Copy)
        else:
            P.dve("tensor_copy", out=s[0:m, :], in_=ps[0:m, 0:TN])
        g0 = b * TT + tt * TN
        P.dma("pool", dst[row_off + col:row_off + col + m, g0:g0 + TN], s[0:m, :])
    return handler


def lin_tm(P, C, XT, Wd, col0, ncols, dst, b, name="lt"):
    W3 = Wd.rearrange("(kc p) n -> p kc n", p=128)
    with P.scope():
        w = P.tile(f"{name}_w", [128, 8, ncols])
        P.dma("sp", w, W3[:, :, col0:col0 + ncols])
        pss = [P.psum(f"{name}_ps{i}", [128, 512]) for i in range(2)]
        st = [P.tile(f"{name}_st{i}", [128, ncols]) for i in range(2)]
        for t in range(TT // 128):
            ps = pss[t % 2]
            for kc in range(8):
                P.mm(ps[:, 0:ncols], lhsT=XT[:, kc, t * 128:(t + 1) * 128], rhs=w[:, kc, :],
                     start=(kc == 0), stop=(kc == 7))
            s = st[t % 2]
            if t % 2:
                P.act("activation", out=s, in_=ps[:, 0:ncols], func=AF.Copy)
            else:
                P.dve("tensor_copy", out=s, in_=ps[:, 0:ncols])
            P.dma("pool", dst[b * TT + t * 128:b * TT + (t + 1) * 128, 0:ncols], s)


def rev(v, a0, a1):
    d = v.ap.ap
    return v.raw(a1 - 1, [[d[0][0], d[0][1]], [-1, a1 - a0]])


GELU_C = 2.0 * math.sqrt(2.0 / math.pi)


def phase_lru(P, C, ZO, MIXT, b, o):
    with P.scope():
        sv = P.tile("lru_sv", [128, 4, 11])
        P.dma("sp", sv, C.lru_small[o])
        c8 = P.tile("lru_c8", [128, 4, 2])
        P.act("activation", out=c8, in_=sv[:, :, 9:11], func=AF.Exp, scale=-1.0)
        P.dve("tensor_scalar", out=c8, in0=c8, scalar1=1.0, scalar2=None, op0=ALU.add)
        P.act("activation", out=c8, in_=c8, func=AF.Ln)
        P.dve("tensor_scalar", out=c8, in0=c8, scalar1=-8.0, scalar2=None, op0=ALU.mult)
        bd = P.tile("lru_bd", [128, 4, 128])
        nm = ["up", "gp", "u", "rg0", "rg1", "ig0", "ig1", "a", "x", "t", "h0", "h1"]
        T = {n: P.tile("lru_" + n, [128, TT]) for n in nm}
        pss = [P.psum(f"lru_ps{i}", [128, 512]) for i in range(2)]
        up, gp, u, A, X, Tm = T["up"], T["gp"], T["u"], T["a"], T["x"], T["t"]
        k = 0
        for c in range(4):
            P.dma("sp", bd, C.lru_bd[o, :, :, c].rearrange("z g p n -> p (z g) n"))
            P.dma("sp", up, ZO[c * 128:(c + 1) * 128, b * TT:(b + 1) * TT])
            P.dma("pool", gp, ZO[512 + c * 128:512 + (c + 1) * 128, b * TT:(b + 1) * TT])
            P.act("activation", out=u, in_=up, func=AF.Identity, scale=sv[:, c, 1:2], bias=sv[:, c, 4:5])
            for (a0, a1) in ((0, CTX), (CTX, TT)):
                for (kk, off) in ((0, -1), (2, 1), (3, 2)):
                    lo = a0 + max(0, -off)
                    hi = a1 - max(0, off)
                    P.dve("scalar_tensor_tensor", out=u[:, lo:hi], in0=up[:, lo + off:hi + off],
                          scalar=sv[:, c, kk:kk + 1], in1=u[:, lo:hi], op0=ALU.mult, op1=ALU.add)
            for z in range(2):
                for g in range(2):
                    dst = T[("rg" if g == 0 else "ig") + str(z)]
                    for tt in range(NTT):
                        ps = pss[k % 2]
                        k += 1
                        P.mm(ps[:, 0:TN], lhsT=bd[:, z * 2 + g, :], rhs=u[:, tt * TN:(tt + 1) * TN])
                        P.act("activation", out=dst[:, tt * TN:(tt + 1) * TN], in_=ps[:, 0:TN], func=AF.Sigmoid,
                              bias=sv[:, c, 5 + z * 2 + g:6 + z * 2 + g])
            for z in range(2):
                rg, ig, Hh = T["rg" + str(z)], T["ig" + str(z)], T["h" + str(z)]
                P.act("activation", out=A, in_=rg, func=AF.Exp, scale=c8[:, c, z:z + 1])
                P.pool("tensor_tensor", out=Tm, in0=A, in1=A, op=ALU.mult)
                P.dve("tensor_scalar", out=Tm, in0=Tm, scalar1=-1.0, scalar2=1.0, op0=ALU.mult, op1=ALU.add)
                P.act("activation", out=Tm, in_=Tm, func=AF.Sqrt)
                P.pool("tensor_tensor", out=X, in0=Tm, in1=ig, op=ALU.mult)
                P.dve("tensor_tensor", out=X, in0=X, in1=u, op=ALU.mult)
                if z == 0:
                    P.dve("tensor_tensor_scan", out=Hh[:, 0:CTX], data0=A[:, 0:CTX], data1=X[:, 0:CTX],
                          initial=0.0, op0=ALU.mult, op1=ALU.add)
                    P.dve("tensor_tensor_scan", out=Hh[:, CTX:TT], data0=A[:, CTX:TT], data1=X[:, CTX:TT],
                          initial=Hh[:, CTX - 1:CTX], op0=ALU.mult, op1=ALU.add)
                else:
                    P.dve("tensor_tensor_scan", out=rev(Hh, 0, CTX), data0=rev(A, 0, CTX), data1=rev(X, 0, CTX),
                          initial=0.0, op0=ALU.mult, op1=ALU.add)
                    P.dve("tensor_tensor_scan", out=rev(Hh, CTX, TT), data0=rev(A, CTX, TT), data1=rev(X, CTX, TT),
                          initial=Hh[:, 0:1], op0=ALU.mult, op1=ALU.add)
            Y = T["h0"]
            P.dve("tensor_tensor", out=Y, in0=T["h0"], in1=T["h1"], op=ALU.add)
            P.pool("tensor_tensor", out=Tm, in0=gp, in1=gp, op=ALU.mult)
            P.dve("tensor_scalar", out=Tm, in0=Tm, scalar1=0.044715, scalar2=1.0, op0=ALU.mult, op1=ALU.add)
            P.pool("tensor_tensor", out=Tm, in0=Tm, in1=gp, op=ALU.mult)
            P.act("activation", out=Tm, in_=Tm, func=AF.Sigmoid, scale=GELU_C)
            P.pool("tensor_tensor", out=Tm, in0=Tm, in1=gp, op=ALU.mult)
            P.dve("tensor_tensor", out=Y, in0=Y, in1=Tm, op=ALU.mult)
            P.dma("pool", MIXT[c * 128:(c + 1) * 128, b * TT:(b + 1) * TT], Y)


def phase_na_table(P, C, o, TB2):
    with P.scope():
        R = P.tile("nt_r", [31, 2, 8, 14])
        for krp in range(2):
            P.dma("sp", R[:, krp], C.rpbT[o, :, :, krp:krp + 14])
        sel = P.tile("nt_sel", [31, 2, 64, 128])
        P.dma("sp", sel[:, 0], C.na_sel[0])
        P.dma("pool", sel[:, 1], C.na_sel[1])
        negm = P.tile("nt_neg", [128, 64])
        P.dma("sp", negm, C.na_negm)
        pss = [P.psum(f"nt_ps{i}", [128, 8, 14]) for i in range(2)]
        for c in range(64):
            ps = pss[c % 2]
            P.mm(ps, lhsT=sel[:, 0, c, :], rhs=R[:, 0], start=True, stop=False)
            P.mm(ps, lhsT=sel[:, 1, c, :], rhs=R[:, 1], start=False, stop=True)
            P.dve("tensor_scalar", out=TB2[:, :, :, c], in0=ps, scalar1=negm[:, c:c + 1], scalar2=None, op0=ALU.add)


def phase_na(P, C, ZO, VTOK, MIXT, b, TB2, do_ctx):
    with P.scope():
        QT = [P.tile(f"na_q{i}", [128, TT]) for i in range(4)]
        KT = [P.tile(f"na_k{i}", [128, TT]) for i in range(4)]
        for hp in range(4):
            P.dma("sp", QT[hp], ZO[1024 + hp * 128:1024 + (hp + 1) * 128, b * TT:(b + 1) * TT])
            P.dma("pool", KT[hp], ZO[1536 + hp * 128:1536 + (hp + 1) * 128, b * TT:(b + 1) * TT])
            P.act("activation", out=QT[hp], in_=QT[hp], func=AF.Copy, scale=0.125)
        VC = P.tile("na_vc", [128, 2, 8, 65])
        P.dve("memset", ap=VC, constant=1.0)
        for j in range(2):
            P.dma("sp", VC[:, j, :, 0:64],
                  VTOK[b * TT + j * 128:b * TT + (j + 1) * 128, :].rearrange("p (h f) -> p h f", h=8))
        VW = [P.tile(f"na_vw{i}", [128, 4, 8, 65]) for i in range(2)]
        QBD = [P.tile(f"na_qbd{i}", [128, 2, 64]) for i in range(2)]
        QK = [P.tile(f"na_qk{i}", [128, 2, 64]) for i in range(2)]
        for i in range(2):
            P.dve("memset", ap=VW[i], constant=1.0)
            P.dve("memset", ap=QBD[i], constant=0.0)
            P.dve("memset", ap=QK[i], constant=0.0)
        Tt = [P.tile(f"na_t{i}", [128, 6, 2, 64]) for i in range(2)]
        E = [P.tile(f"na_e{i}", [128, 6, 2, 64]) for i in range(2)]
        RS = [P.tile(f"na_rs{i}", [64, 1]) for i in range(2)]
        MBS = [P.tile(f"na_mbs{i}", [128, 2, 64]) for i in range(2)]
        OB = [P.tile(f"na_ob{i}", [64, 8, 64]) for i in range(2)]
        OT = [P.tile(f"na_ot{i}", [128, 4, 64]) for i in range(2)]
        STp = [P.psum(f"na_st{i}", [128, 6, 2, 64]) for i in range(2)]
        MBp = P.psum("na_mb", [128, 2, 64])
        OSp = [P.psum(f"na_os{i}", [64, 65]) for i in range(2)]
        TPp = P.psum("na_tp", [128, 4, 64])
        it = [0]
        io = [0]
        d = TB2.ap.ap

        def block(hp, q0, kchunks, vchf, off, ob):
            i = it[0] % 2
            it[0] += 1
            nk = len(kchunks)
            qk, qbd, st, e, tt, mbs = QK[i], QBD[i], STp[i], E[i], Tt[i], MBS[i]
            for h2 in range(2):
                pr = slice(h2 * 64, (h2 + 1) * 64)
                P.dve("tensor_tensor", out=qk[pr, h2, :], in0=QT[hp][pr, q0:q0 + 64], in1=KT[hp][pr, q0:q0 + 64],
                      op=ALU.mult)
                P.act("activation", out=qbd[pr, h2, :], in_=QT[hp][pr, q0:q0 + 64], func=AF.Copy)
            P.mm(MBp, lhsT=C.ones, rhs=qk)
            P.act("activation", out=mbs, in_=MBp, func=AF.Copy)
            for j, k0 in enumerate(kchunks):
                P.mm(st[:, j], lhsT=KT[hp][:, k0:k0 + 128], rhs=qbd)
            if off is not None:
                bias = TB2.raw((2 * hp) * 14 * 64 + off * 64, [[d[0][0], 128], [128, 4], [14 * 64, 2], [1, 64]])
                P.dve("tensor_tensor", out=tt[:, 0:4], in0=st[:, 0:4], in1=bias, op=ALU.add)
                P.dve("tensor_tensor", out=tt[:, 0:4], in0=tt[:, 0:4], in1=mbs.ins(1, 4), op=ALU.subtract)
                P.dve("tensor_tensor", out=tt[:, 4:6], in0=st[:, 4:6], in1=mbs.ins(1, 2), op=ALU.subtract)
            else:
                P.dve("tensor_tensor", out=tt[:, 0:nk], in0=st[:, 0:nk], in1=mbs.ins(1, nk), op=ALU.subtract)
            P.act("activation", out=e[:, 0:nk], in_=tt[:, 0:nk], func=AF.Exp)
            for h2 in range(2):
                h = hp * 2 + h2
                k = io[0] % 2
                io[0] += 1
                os_ = OSp[k]
                for j in range(nk):
                    P.mm(os_, lhsT=e[:, j, h2, :], rhs=vchf(h, j), start=(j == 0), stop=(j == nk - 1))
                P.dve("reciprocal", out=RS[k], in_=os_[:, 64:65])
                P.act("activation", out=ob[:, h, :], in_=os_[:, 0:64], func=AF.Copy, scale=RS[k])

        mo = MIXT[512:1024, :].rearrange("(c p) t -> p c t", p=128)
        nrow = [0]

        def flush(ob, g0):
            k = nrow[0] % 2
            nrow[0] += 1
            obf = ob.rearrange("p h f -> p (h f)")
            for c in range(4):
                P.pe("transpose", out=TPp[:, c, :], in_=obf[:, c * 128:(c + 1) * 128], identity=C.ident[0:64, 0:64])
            P.dve("tensor_copy", out=OT[k], in_=TPp)
            P.dma("pool", mo[:, :, g0:g0 + 64], OT[k])

        n = 0
        for r in range(32):
            r0 = min(max(r - 4, 0), 24)
            off = r0 - r + 7
            vw = VW[n % 2]
            ob = OB[n % 2]
            n += 1
            t0 = b * TT + CTX + r0 * 64
            for j in range(4):
                P.dma("sp" if j % 2 == 0 else "act", vw[:, j, :, 0:64],
                      VTOK[t0 + j * 128:t0 + (j + 1) * 128, :].rearrange("p (h f) -> p h f", h=8))
            q0 = CTX + r * 64
            kch = [CTX + r0 * 64 + 128 * j for j in range(4)] + [0, 128]
            vf = lambda h, j, vw=vw: (vw[:, j, h, :] if j < 4 else VC[:, j - 4, h, :])
            for hp in range(4):
                block(hp, q0, kch, vf, off, ob)
            flush(ob, b * TT + q0)
        if do_ctx:
            for qi in range(4):
                ob = OB[n % 2]
                n += 1
                vf = lambda h, j: VC[:, j, h, :]
                for hp in range(4):
                    block(hp, qi * 64, [0, 128], vf, None, ob)
                flush(ob, b * TT + qi * 64)


EXT_SHAPES.update({
    "lru_small": [2, 128, 4, 11], "lru_bd": [2, 2, 2, 4, 128, 128], "rpbT": [2, 31, 8, 15],
    "na_sel": [2, 31, 64, 128], "na_negm": [128, 64], "dbg_in": [D, NT],
})


def _na_consts():
    sel = np.zeros((2, 31, 64, 128), np.float32)
    negm = np.full((128, 64), -30000.0, np.float32)
    for c in range(64):
        w0 = min(max(c - 8, 0), 48)
        for kc in range(w0, w0 + 16):
            x = kc - c + 15
            sel[0, x, c, kc] = 1.0
            sel[1, x, c, 64 + kc] = 1.0
            negm[kc, c] = 0.0
            negm[64 + kc, c] = 0.0
    return sel, negm


def host_inputs(inputs, names=None):
    f = lambda a: np.ascontiguousarray(np.asarray(a, dtype=np.float32))
    g = lambda k: np.asarray(inputs[k], np.float32)
    x, c, ctx, c_ctx = g("x"), g("c"), g("ctx"), g("c_ctx")
    consts = np.zeros((128, 4, 128), np.float32)
    consts[:, 0, :] = np.eye(128, dtype=np.float32)
    consts[:, 1, :] = 1.0
    consts[:64, 2, :64] = 1.0
    consts[64:, 2, 64:] = 1.0
    consts[:, 3, :] = -1.0
    def pf(a):
        return np.moveaxis(a.reshape(a.shape[:-1] + (4, 128)), -1, 0)
    lru_small = np.zeros((2, 128, 4, 11), np.float32)
    cw, cb, gb, lam = g("lru_conv_w"), g("lru_conv_b"), g("lru_gate_b"), g("lru_lambda")
    for o in range(2):
        lru_small[o, :, :, 0:4] = np.moveaxis(pf(cw[o]), 1, 2)
        lru_small[o, :, :, 4] = pf(cb[o])
        lru_small[o, :, :, 5:9] = np.moveaxis(pf(gb[o]).reshape(128, 4, 4), 1, 2)
        lru_small[o, :, :, 9:11] = np.moveaxis(pf(lam[o]), 1, 2)
    gw = g("lru_gate_w")
    lru_bd = np.zeros((2, 2, 2, 4, 128, 128), np.float32)
    for ch in range(4):
        for n2 in range(2):
            lru_bd[:, :, :, ch, n2 * 64:(n2 + 1) * 64, n2 * 64:(n2 + 1) * 64] = gw[:, :, :, ch * 2 + n2]
    sel, negm = _na_consts()
    shared = {
        "ada_w": g("ada_w"),
        "ada_bT": f(g("ada_b").reshape(DEPTH, 48, 128).transpose(0, 2, 1)),
        "mix_out_w": g("mix_out_w"), "mlp_w1": g("mlp_w1"), "mlp_w2": g("mlp_w2"),
        "ev_in_w": g("ev_in_w"), "od_in_w": g("od_in_w"),
        "consts": consts,
        "fnwT": f(g("final_norm_w").reshape(8, 128).T),
        "lru_small": lru_small, "lru_bd": lru_bd,
        "rpbT": f(g("na_rpb").transpose(0, 3, 1, 2)),
        "na_sel": sel, "na_negm": negm,
    }
    shared.update(host_even(inputs))
    maps = []
    for core in range(NCORES):
        bs = [core * NB + i for i in range(NB)]
        tok = np.concatenate([np.concatenate([ctx[b], x[b]], axis=0) for b in bs], axis=0)
        cond = np.stack([c[bs[0]], c[bs[1]], c_ctx], axis=1)
        m = dict(shared)
        m["xT"] = f(tok.T)
        m["condT"] = f(cond.reshape(8, 128, 3).transpose(1, 0, 2))
        if names is not None:
            m = {k: v for k, v in m.items() if k in names}
        maps.append({k: f(v) for k, v in m.items()})
    return maps


EXT_SHAPES.update({
    "rope": [2, TT, 64], "tri": [2, 128, 128], "ml_gb": [2, 16], "ml_nw": [2, 512],
})
ML_ORDER = [list(range(18)), [1, 0] + list(range(17, 1, -1))]


def phase_mlstm(P, C, XT, HD, QKV, b, e):
    W3 = C.ev_in_w[e].rearrange("(kc p) n -> p kc n", p=128)
    with P.scope():
        Wq = P.tile("m_wq", [128, 8, 512])
        Wk = P.tile("m_wk", [128, 8, 512])
        Wv = P.tile("m_wv", [128, 8, 512])
        Wg = P.tile("m_wg", [128, 8, 16])
        P.dma("sp", Wq, W3[:, :, 0:512])
        P.dma("pool", Wk, W3[:, :, 512:1024])
        P.dma("sp", Wv, W3[:, :, 1024:1536])
        P.dma("pool", Wg, W3[:, :, 2048:2064])
        CS = P.tile("m_cs", [128, 18, 2, 64])
        for i in range(2):
            P.dma("sp", CS[:, :, i, :], C.rope[i].rearrange("(t p) f -> p t f", p=128))
        tri = P.tile("m_tri", [128, 2, 128])
        P.dma("sp", tri, C.tri.rearrange("d p n -> p d n"))
        GB = P.tile("m_gb", [128, 16])
        P.dma("sp", GB, C.ml_gb[e:e + 1, :].pbc(128))
        Caug = P.tile("m_c", [128, 4, 129])
        VAs = [P.tile(f"m_va{i}", [128, 4, 129]) for i in range(2)]
        for i in range(2):
            P.dve("memset", ap=VAs[i], constant=1.0)
        G = P.tile("m_g", [128, 16])
        LF = P.tile("m_lf", [128, 4])
        QS = P.tile("m_qs", [128, 4])
        KS = P.tile("m_ks", [128, 4])
        EBT = P.tile("m_ebt", [128, 4])
        Qc = P.tile("m_qc", [128, 4, 2, 64])
        Kc = P.tile("m_kc", [128, 4, 2, 64])
        QRs = [P.tile(f"m_qr{i}", [128, 4, 2, 64]) for i in range(2)]
        KRs = [P.tile(f"m_kr{i}", [128, 4, 2, 64]) for i in range(2)]
        A1 = P.tile("m_a1", [128, 4, 64])
        A2 = P.tile("m_a2", [128, 4, 64])
        A3 = P.tile("m_a3", [128, 4, 64])
        A4 = P.tile("m_a4", [128, 4, 64])
        Qs = P.tile("m_qsc", [128, 4, 128])
        Ks = P.tile("m_ksc", [128, 4, 128])
        Ke = P.tile("m_ke", [128, 4, 128])
        QsT = P.tile("m_qst", [128, 4, 128])
        KsT = P.tile("m_kst", [128, 4, 128])
        ST = [P.tile(f"m_st{i}", [128, 128]) for i in range(2)]
        dn = P.tile("m_dn", [128, 4])
        Hout = [P.tile(f"m_ho{i}", [128, 4, 128]) for i in range(2)]
        ps_g = P.psum("m_psg", [128, 3, 16])
        ps_q = P.psum("m_psq", [128, 512])
        ps_k = P.psum("m_psk", [128, 512])
        ps_v = P.psum("m_psv", [128, 512])
        ps_t = P.psum("m_pst", [128, 4, 128])
        ps_s = P.psum("m_pss", [128, 128])
        ps_n = [P.psum(f"m_psn{i}", [128, 129]) for i in range(2)]
        it = 0
        for d in range(2):
            P.dve("memset", ap=Caug, constant=0.0)
            for t in ML_ORDER[d]:
                xt = lambda kc: XT[:, kc, t * 128:(t + 1) * 128]
                for kc in range(8):
                    P.mm(ps_g[:, 0, :], lhsT=xt(kc), rhs=Wg[:, kc, :], start=(kc == 0), stop=(kc == 7))
                P.dve("tensor_tensor", out=G, in0=ps_g[:, 0, :], in1=GB, op=ALU.add)
                P.act("activation", out=LF, in_=G[:, 8 + d * 4:12 + d * 4], func=AF.Sigmoid)
                P.act("activation", out=LF, in_=LF, func=AF.Ln)
                P.mm(ps_g[:, 1, 0:4], lhsT=tri[:, d, :], rhs=LF)
                P.mm(ps_g[:, 2, 0:4], lhsT=C.ones, rhs=LF)
                P.act("activation", out=QS, in_=ps_g[:, 1, 0:4], func=AF.Exp)
                P.dve("tensor_tensor", out=KS, in0=G[:, d * 4:d * 4 + 4], in1=ps_g[:, 1, 0:4], op=ALU.subtract)
                P.act("activation", out=KS, in_=KS, func=AF.Exp)
                P.dve("tensor_scalar", out=KS, in0=KS, scalar1=float(128 ** -0.5), scalar2=None, op0=ALU.mult)
                P.act("activation", out=EBT, in_=ps_g[:, 2, 0:4], func=AF.Exp)
                QR, KR, VA = QRs[it % 2], KRs[it % 2], VAs[it % 2]
                rows = slice(b * TT + t * 128, b * TT + (t + 1) * 128)
                if d == 0:
                    for (ps, W) in ((ps_q, Wq), (ps_k, Wk), (ps_v, Wv)):
                        for kc in range(8):
                            P.mm(ps, lhsT=xt(kc), rhs=W[:, kc, :], start=(kc == 0), stop=(kc == 7))
                    P.act("activation", out=Qc, in_=ps_q.rearrange("p (h s f) -> p h s f", h=4, s=2), func=AF.Copy)
                    P.act("activation", out=Kc, in_=ps_k.rearrange("p (h s f) -> p h s f", h=4, s=2), func=AF.Copy)
                    P.act("activation", out=VA[:, :, 0:128], in_=ps_v.rearrange("p (h f) -> p h f", h=4), func=AF.Copy)
                    cosb = CS[:, t, 0, :].ins(1, 4)
                    sinb = CS[:, t, 1, :].ins(1, 4)
                    for (src, dst, eng1, eng2) in ((Qc, QR, "dve", "pool"), (Kc, KR, "pool", "dve")):
                        P.op(eng1, "tensor_tensor", out=A1, in0=src[:, :, 0, :], in1=cosb, op=ALU.mult)
                        P.op(eng2, "tensor_tensor", out=A2, in0=src[:, :, 1, :], in1=sinb, op=ALU.mult)
                        P.op(eng1, "tensor_tensor", out=dst[:, :, 0, :], in0=A1, in1=A2, op=ALU.subtract)
                        P.op(eng2, "tensor_tensor", out=A3, in0=src[:, :, 0, :], in1=sinb, op=ALU.mult)
                        P.op(eng1, "tensor_tensor", out=A4, in0=src[:, :, 1, :], in1=cosb, op=ALU.mult)
                        P.op(eng2, "tensor_tensor", out=dst[:, :, 1, :], in0=A3, in1=A4, op=ALU.add)
                    P.dma("pool", QKV[rows, 0, :], QR.rearrange("p h s f -> p (h s f)"))
                    P.dma("pool", QKV[rows, 1, :], KR.rearrange("p h s f -> p (h s f)"))
                    P.dma("pool", QKV[rows, 2, :].rearrange("p (h f) -> p h f", h=4), VA[:, :, 0:128])
                else:
                    P.dma("sp", QR.rearrange("p h s f -> p (h s f)"), QKV[rows, 0, :])
                    P.dma("sp", KR.rearrange("p h s f -> p (h s f)"), QKV[rows, 1, :])
                    P.dma("sp", VA[:, :, 0:128], QKV[rows, 2, :].rearrange("p (h f) -> p h f", h=4))
                P.dve("tensor_tensor", out=Qs, in0=QR.rearrange("p h s f -> p h (s f)"), in1=QS.ins(2, 128), op=ALU.mult)
                P.pool("tensor_tensor", out=Ks, in0=KR.rearrange("p h s f -> p h (s f)"), in1=KS.ins(2, 128), op=ALU.mult)
                P.pool("tensor_tensor", out=Ke, in0=Ks, in1=EBT.ins(2, 128), op=ALU.mult)
                for h in range(4):
                    P.pe("transpose", out=ps_t[:, h, :], in_=Qs[:, h, :], identity=C.ident)
                P.act("activation", out=QsT, in_=ps_t, func=AF.Copy)
                for h in range(4):
                    P.pe("transpose", out=ps_t[:, h, :], in_=Ks[:, h, :], identity=C.ident)
                P.dve("tensor_copy", out=KsT, in_=ps_t)
                ho = Hout[it % 2]
                it += 1
                for h in range(4):
                    st = ST[h % 2]
                    pn = ps_n[h % 2]
                    P.mm(ps_s, lhsT=KsT[:, h, :], rhs=QsT[:, h, :])
                    P.dve("tensor_tensor", out=st, in0=ps_s, in1=tri[:, d, :], op=ALU.mult)
                    P.mm(pn, lhsT=QsT[:, h, :], rhs=Caug[:, h, :], start=True, stop=False)
                    P.mm(pn, lhsT=st, rhs=VA[:, h, :], start=False, stop=True)
                    P.act("activation", out=dn[:, h:h + 1], in_=pn[:, 128:129], func=AF.Abs)
                    P.dve("tensor_scalar", out=dn[:, h:h + 1], in0=dn[:, h:h + 1], scalar1=1.0, scalar2=None,
                          op0=ALU.max)
                    P.dve("reciprocal", out=dn[:, h:h + 1], in_=dn[:, h:h + 1])
                    P.act("activation", out=ho[:, h, :], in_=pn[:, 0:128], func=AF.Copy, scale=dn[:, h:h + 1])
                    P.mm(pn, lhsT=Ke[:, h, :], rhs=VA[:, h, :])
                    P.dve("scalar_tensor_tensor", out=Caug[:, h, :], in0=Caug[:, h, :], scalar=EBT[:, h:h + 1],
                          in1=pn, op0=ALU.mult, op1=ALU.add)
                P.dma("pool", HD[d][b * TT + t * 128:b * TT + (t + 1) * 128, :], ho.rearrange("p h f -> p (h f)"))


def phase_mlstm_out(P, C, XT, HD, MIXT, b, e):
    W3 = C.ev_in_w[e].rearrange("(kc p) n -> p kc n", p=128)
    with P.scope():
        Wo = P.tile("mo_wo", [128, 8, 512])
        P.dma("sp", Wo, W3[:, :, 1536:2048])
        NW = P.tile("mo_nw", [128, 512])
        P.dma("sp", NW, C.ml_nw[e:e + 1, :].pbc(128))
        H0 = [P.tile(f"mo_h0{i}", [128, 4, 128]) for i in range(2)]
        H1 = [P.tile(f"mo_h1{i}", [128, 4, 128]) for i in range(2)]
        SQ = P.tile("mo_sq", [128, 4, 128])
        ss = P.tile("mo_ss", [128, 4])
        SG = P.tile("mo_sg", [128, 4, 128])
        OT = [P.tile(f"mo_ot{i}", [128, 4, 128]) for i in range(2)]
        ps_o = P.psum("mo_pso", [128, 512])
        ps_t = P.psum("mo_pst", [128, 4, 128])
        mo = MIXT[0:512, :].rearrange("(c p) t -> p c t", p=128)
        for t in range(18):
            i = t % 2
            rows = slice(b * TT + t * 128, b * TT + (t + 1) * 128)
            P.dma("sp", H0[i], HD[0][rows, :].rearrange("p (h f) -> p h f", h=4))
            P.dma("pool", H1[i], HD[1][rows, :].rearrange("p (h f) -> p h f", h=4))
            for kc in range(8):
                P.mm(ps_o, lhsT=XT[:, kc, t * 128:(t + 1) * 128], rhs=Wo[:, kc, :], start=(kc == 0), stop=(kc == 7))
            P.act("activation", out=SG, in_=ps_o.rearrange("p (h f) -> p h f", h=4), func=AF.Sigmoid)
            P.dve("tensor_tensor", out=H0[i], in0=H0[i], in1=H1[i], op=ALU.add)
            P.pool("tensor_tensor", out=SQ, in0=H0[i], in1=H0[i], op=ALU.mult)
            P.dve("tensor_reduce", out=ss, in_=SQ, axis=AX.X, op=ALU.add)
            P.act("activation", out=ss, in_=ss, func=AF.Sqrt, scale=1.0 / 128, bias=C.epsc)
            P.dve("reciprocal", out=ss, in_=ss)
            P.dve("tensor_tensor", out=H0[i], in0=H0[i], in1=ss.ins(2, 128), op=ALU.mult)
            P.pool("tensor_tensor", out=SG, in0=SG, in1=NW.rearrange("p (h f) -> p h f", h=4), op=ALU.mult)
            P.dve("tensor_tensor", out=H0[i], in0=H0[i], in1=SG, op=ALU.mult)
            for h in range(4):
                P.pe("transpose", out=ps_t[:, h, :], in_=H0[i][:, h, :], identity=C.ident)
            P.act("activation", out=OT[i], in_=ps_t, func=AF.Copy)
            P.dma("pool", mo[:, :, b * TT + t * 128:b * TT + (t + 1) * 128], OT[i])


def _rw_sel():
    z = np.zeros((128, 256), np.float32)
    z[:64, 128] = 1.0
    z[64:, 129] = 1.0
    return z


def host_even(inputs):
    g = lambda k: np.asarray(inputs[k], np.float32)
    t = np.arange(TL)
    row = (t // 64).astype(np.float32)
    col = (t % 64).astype(np.float32)
    n_freq = 32
    inv = (np.float32(10000.0) ** (-np.arange(n_freq, dtype=np.float32) / np.float32(n_freq))).astype(np.float32)
    ang = np.concatenate([row[:, None] * inv, col[:, None] * inv], axis=-1).astype(np.float32)
    rope = np.zeros((2, TT, 64), np.float32)
    rope[0, :CTX] = 1.0
    rope[0, CTX:] = np.cos(ang)
    rope[1, CTX:] = np.sin(ang)
    tri = np.zeros((2, 128, 128), np.float32)
    tri[0] = np.triu(np.ones((128, 128), np.float32))
    tri[1] = np.tril(np.ones((128, 128), np.float32))
    gb = g("ml_gate_b")
    ml_gb = np.concatenate([gb[:, :, 0, :].reshape(2, 8), gb[:, :, 1, :].reshape(2, 8)], axis=1)
    pf = lambda a: np.moveaxis(a.reshape(a.shape[:-1] + (4, 128)), -1, 0)
    rw_small = np.zeros((2, 128, 4, 9), np.float32)
    w0, a0 = g("rw_w0"), g("rw_a0")
    for e in range(2):
        rw_small[e, :, :, 0:2] = np.moveaxis(pf(w0[e]), 1, 2)
        rw_small[e, :, :, 2:4] = np.moveaxis(pf(a0[e]), 1, 2)
        rw_small[e, :, :, 4] = pf(g("rw_k_k")[e])
        rw_small[e, :, :, 5] = pf(g("rw_k_a")[e])
        rw_small[e, :, :, 6] = pf(g("rw_r_k")[e].reshape(512))
        rw_small[e, :, :, 7] = pf(g("rw_ln_w")[e])
        rw_small[e, :, :, 8] = pf(g("rw_ln_b")[e])
    return {"rope": rope, "tri": tri, "ml_gb": ml_gb, "ml_nw": g("ml_norm_w"),
            "rw_muT": np.ascontiguousarray(g("rw_mu").reshape(2, 15, 128).transpose(0, 2, 1)),
            "rw_small": rw_small,
            "rw_w_up": g("rw_w_up").reshape(2, 128, 512), "rw_a_up": g("rw_a_up").reshape(2, 128, 512),
            "rw_g_up": g("rw_g_up"), "rw_sel": _rw_sel()}


EXT_SHAPES.update({
    "rw_muT": [2, 128, 15], "rw_small": [2, 128, 4, 9], "rw_w_up": [2, 128, 512], "rw_a_up": [2, 128, 512],
    "rw_g_up": [2, 128, 512], "rw_sel": [128, 256],
})
SEGS = ((0, CTX), (CTX, TT))
RW_DECAY_SCALE = -math.exp(-0.5)


def phase_rwkv_prep(P, C, ZR, SC, VT, b, e):
    cols = slice(b * TT, (b + 1) * TT)
    with P.scope():
        mu = P.tile("rp_mu", [128, 15])
        P.dma("sp", mu, C.rw_muT[e])
        sm = P.tile("rp_sm", [128, 4, 9])
        P.dma("sp", sm, C.rw_small[e])
        WU = P.tile("rp_wu", [128, 512])
        AU = P.tile("rp_au", [128, 512])
        GU = P.tile("rp_gu", [128, 512])
        P.dma("sp", WU, C.rw_w_up[e])
        P.dma("sp", AU, C.rw_a_up[e])
        P.dma("sp", GU, C.rw_g_up[e])
        nm = ["z", "s", "tw", "ad", "sg", "r", "k", "v", "kk", "nr", "x1", "x2", "x3"]
        T = {n: P.tile("rp_" + n, [128, TT]) for n in nm}
        pss = [P.psum(f"rp_ps{i}", [128, 512]) for i in range(4)]
        pst = [P.psum(f"rp_pt{i}", [128, 128]) for i in range(2)]
        vst = [P.tile(f"rp_vst{i}", [128, 128]) for i in range(2)]
        kctr = [0]

        def nps():
            kctr[0] += 1
            return pss[kctr[0] % 4]

        def shiftmix(j, out):
            Zt, S = T["z"], T["s"]
            P.dma("sp", Zt, ZR[j * 128:(j + 1) * 128, cols])
            for (a0, a1) in SEGS:
                P.pool("tensor_copy", out=S[:, a0 + 1:a1], in_=Zt[:, a0:a1 - 1])
                P.pool("memset", ap=S[:, a0:a0 + 1], constant=0.0)
                P.dve("tensor_tensor", out=S[:, a0:a1 - 1], in0=S[:, a0:a1 - 1], in1=Zt[:, a0 + 1:a1], op=ALU.add)
            P.dve("scalar_tensor_tensor", out=S, in0=S, scalar=0.5, in1=Zt, op0=ALU.mult, op1=ALU.subtract)
            P.dve("scalar_tensor_tensor", out=out, in0=S, scalar=mu[:, j:j + 1], in1=Zt, op0=ALU.mult, op1=ALU.add)

        TW, AD, SG = T["tw"], T["ad"], T["sg"]
        shiftmix(12, TW)
        P.act("activation", out=TW, in_=TW, func=AF.Tanh)
        shiftmix(13, AD)
        shiftmix(14, SG)
        P.act("activation", out=SG, in_=SG, func=AF.Sigmoid)
        Rt, Kt, Vt, KK, NR, X1, X2, X3 = (T[n] for n in ("r", "k", "v", "kk", "nr", "x1", "x2", "x3"))
        for hp in range(4):
            rows = slice(hp * 128, (hp + 1) * 128)
            shiftmix(hp, Rt)
            P.dma("pool", SC["r"][rows, cols], Rt)
            shiftmix(4 + hp, Kt)
            P.dma("pool", SC["k"][rows, cols], Kt)
            shiftmix(8 + hp, Vt)
            P.dma("pool", SC["v"][rows, cols], Vt)
            for t in range(18):
                pt = pst[t % 2]
                vs = vst[t % 2]
                P.pe("transpose", out=pt, in_=Vt[:, t * 128:(t + 1) * 128], identity=C.ident)
                P.act("activation", out=vs, in_=pt, func=AF.Copy)
                P.dma("pool", VT.raw((t * 128) * 1024 + b * 256 + hp * 64, [[1024, 128], [512, 2], [1, 64]]),
                      vs.rearrange("p (h f) -> p h f", h=2))
            P.act("activation", out=KK, in_=Kt, func=AF.Copy, scale=sm[:, hp, 4:5])
            P.pool("tensor_tensor", out=X1, in0=KK, in1=KK, op=ALU.mult)
            for tt in range(NTT):
                ps = nps()
                P.mm(ps[:, 0:TN], lhsT=C.blk64, rhs=X1[:, tt * TN:(tt + 1) * TN])
                P.act("activation", out=NR[:, tt * TN:(tt + 1) * TN], in_=ps[:, 0:TN], func=AF.Sqrt)
            P.dve("tensor_scalar", out=NR, in0=NR, scalar1=1e-12, scalar2=None, op0=ALU.max)
            P.dve("reciprocal", out=NR, in_=NR)
            P.dve("tensor_tensor", out=KK, in0=KK, in1=NR, op=ALU.mult)
            P.pool("tensor_scalar", out=X1, in0=KK, scalar1=-1.0, scalar2=None, op0=ALU.mult)
            P.dma("pool", SC["a"][rows, cols], X1)
            for d in range(2):
                pr = slice(d * 64, (d + 1) * 64)
                for tt in range(NTT):
                    tc_ = slice(tt * TN, (tt + 1) * TN)
                    ps = nps()
                    P.mm(ps[:, 0:TN], lhsT=WU[pr, rows], rhs=TW[pr, tc_])
                    P.act("activation", out=X2[:, tc_], in_=ps[:, 0:TN], func=AF.Sigmoid, bias=sm[:, hp, d:d + 1])
                    ps = nps()
                    P.mm(ps[:, 0:TN], lhsT=AU[pr, rows], rhs=AD[pr, tc_])
                    P.act("activation", out=X3[:, tc_], in_=ps[:, 0:TN], func=AF.Sigmoid, bias=sm[:, hp, 2 + d:3 + d])
                P.act("activation", out=X2, in_=X2, func=AF.Exp, scale=RW_DECAY_SCALE)
                P.dma("pool", SC[f"w{d}"][rows, cols], X2)
                P.pool("tensor_tensor", out=NR, in0=KK, in1=X3, op=ALU.mult)
                P.dma("pool", SC[f"bv{d}"][rows, cols], NR)
                P.dve("tensor_scalar", out=X3, in0=X3, scalar1=-1.0, scalar2=sm[:, hp, 5:6], op0=ALU.add, op1=ALU.mult)
                P.dve("scalar_tensor_tensor", out=X3, in0=X3, scalar=1.0, in1=Kt, op0=ALU.add, op1=ALU.mult)
                P.dma("pool", SC[f"kd{d}"][rows, cols], X3)
            for tt in range(NTT):
                tc_ = slice(tt * TN, (tt + 1) * TN)
                ps = nps()
                P.mm(ps[:, 0:TN], lhsT=GU[:, rows], rhs=SG[:, tc_])
                P.act("activation", out=X1[:, tc_], in_=ps[:, 0:TN], func=AF.Copy)
            P.dma("pool", SC["g"][rows, cols], X1)


RW_CS = 64
RW_CV = 8


def tau(d, n):
    if d == 0:
        return n
    return CTX - 1 - n if n < CTX else TT + CTX - 1 - n


def phase_rwkv_scan(P, C, SC, VT, YT, nsteps=TT):
    CS, CV = RW_CS, RW_CV
    qn = ["a", "w", "kd", "bv", "r"]
    with P.scope():
        X = {q: [P.tile(f"rs_{q}{i}", [128, 16, CS]) for i in range(2)] for q in qn}
        VB = [P.tile(f"rs_vb{i}", [128, CV, 16, 64]) for i in range(2)]
        Z = P.tile("rs_z", [128, 256])
        P.dma("sp", Z, C.rw_sel)
        tp_ps = P.psum("rs_tp", [64, 4, 128])
        st = []
        for d in range(2):
            t = {}
            t["S"] = [P.tile(f"rs_s{d}{i}", [128, 8, 64]) for i in range(2)]
            t["P1"] = P.tile(f"rs_p1{d}", [128, 8, 64])
            t["P2"] = [P.tile(f"rs_p2{d}{i}", [128, 8, 64]) for i in range(2)]
            t["Q1"] = P.tile(f"rs_q1{d}", [128, 8, 64])
            t["T2"] = [P.tile(f"rs_t2{d}{i}", [128, 8, 64]) for i in range(2)]
            t["Tt"] = P.tile(f"rs_t{d}", [128, 8, 64])
            t["YS"] = P.tile(f"rs_ys{d}", [128, 8, 64])
            t["YB"] = [P.tile(f"rs_yb{d}{i}", [64, 8, 2, 64]) for i in range(2)]
            t["sa"] = [P.psum(f"rs_sa{d}{i}", [128, 512]) for i in range(2)]
            t["y"] = P.psum(f"rs_y{d}", [128, 512])
            P.dve("memset", ap=t["S"][0], constant=0.0)
            st.append(t)

        def col(d, i):
            return i if d == 0 else CS - 1 - i

        def bc(xt, d, i):
            ps_ = xt.ap.ap[0][0]
            return xt.raw(d * 8 * CS + col(d, i), [[ps_, 128], [CS, 8], [0, 64]])

        def p1_op(d, n):
            i, ci = n % CS, (n // CS) % 2
            t = st[d]
            P.dve("tensor_tensor", out=t["P1"], in0=t["S"][n % 2], in1=bc(X["a"][ci], d, i), op=ALU.mult)
            P.mm(t["sa"][n % 2], lhsT=C.blk64, rhs=t["P1"].rearrange("p g f -> p (g f)"))

        def loads(n):
            i = n % CS
            ci = (n // CS) % 2
            if i == 0:
                for q in qn:
                    for d in range(2):
                        src = SC[q if q in ("a", "r") else f"{q}{d}"]
                        t0 = tau(d, n) if d == 0 else tau(d, n + CS - 1)
                        for b in range(NB):
                            g0 = d * 8 + b * 4
                            P.dma("sp", X[q][ci][:, g0:g0 + 4, :],
                                  src[:, b * TT + t0:b * TT + t0 + CS].rearrange("(hp p) t -> p hp t", p=128))
            iv = n % CV
            vi = (n // CV) % 2
            if iv == 0 and not getattr(C, "no_vb", False):
                vb = VB[vi]
                ps_v = vb.ap.ap[0][0]
                for d in range(2):
                    sgn = 1 if d == 0 else -1
                    for h2 in range(2):
                        o = vb[h2 * 64:(h2 + 1) * 64].raw(d * 8 * 64, [[ps_v, 64], [16 * 64, CV], [1, 512]])
                        s_ = VT.raw(tau(d, n) * 1024 + h2 * 512, [[0, 64], [sgn * 1024, CV], [1, 512]])
                        P.dma("act" if (d + h2) % 2 else "sp", o, s_)

        loads(0)
        for d in range(2):
            p1_op(d, 0)
        for n in range(nsteps):
            i = n % CS
            ci = (n // CS) % 2
            iv = n % CV
            vi = (n // CV) % 2
            nl = n % 64
            t2all = []
            for d in range(2):
                t = st[d]
                t2 = t["T2"][n % 2]
                parts = []
                for g in range(8):
                    tg = P.part(t2, g)
                    parts.append(tg.bufs[0])
                    P.act("activation", out=tg[:, g, :], in_=VB[vi][:, iv, d * 8 + g, :], func=AF.Copy,
                          scale=X["kd"][ci][:, d * 8 + g, col(d, i):col(d, i) + 1])
                t2all.append(V(t2.ap, tuple(parts)))
            for d in range(2):
                t = st[d]
                P.pool("tensor_tensor", out=t["Q1"], in0=t["S"][n % 2], in1=bc(X["w"][ci], d, i), op=ALU.mult)
                P.pool("tensor_tensor", out=t["Q1"], in0=t["Q1"], in1=t2all[d], op=ALU.add)
            for d in range(2):
                t = st[d]
                P.dve("tensor_tensor", out=t["Tt"], in0=t["sa"][n % 2].rearrange("p (g f) -> p g f", f=64),
                      in1=bc(X["bv"][ci], d, i), op=ALU.mult)
                P.dve("tensor_tensor", out=t["S"][(n + 1) % 2], in0=t["Q1"], in1=t["Tt"], op=ALU.add)
            if n + 1 < nsteps:
                loads(n + 1)
                for d in range(2):
                    p1_op(d, n + 1)
            for d in range(2):
                t = st[d]
                p2 = t["P2"][n % 2]
                P.dve("tensor_tensor", out=p2, in0=t["S"][(n + 1) % 2], in1=bc(X["r"][ci], d, i), op=ALU.mult)
                P.mm(t["y"], lhsT=Z[:, 128 - 2 * nl:256 - 2 * nl], rhs=p2.rearrange("p g f -> p (g f)"),
                     start=(nl == 0), stop=(nl == 63))
            if nl == 63:
                n64 = n - 63
                for d in range(2):
                    t = st[d]
                    yb = t["YB"][(n // 64) % 2]
                    ps_y = yb.ap.ap[0][0]
                    ps_t = tp_ps.ap.ap[0][0]
                    P.act("activation", out=t["YS"].rearrange("p g f -> p (g f)"), in_=t["y"], func=AF.Copy)
                    for gb in range(2):
                        for k in range(4):
                            P.pe("transpose", out=tp_ps[:, k, :], in_=t["YS"][:, gb * 4 + k, :], identity=C.ident)
                        if d == 0:
                            o = yb.raw(gb * 4 * 128, [[ps_y, 64], [128, 4], [64, 2], [1, 64]])
                        else:
                            o = yb.raw(gb * 4 * 128 + 63, [[ps_y, 64], [128, 4], [64, 2], [-1, 64]])
                        s_ = tp_ps.raw(0, [[ps_t, 64], [128, 4], [1, 2], [2, 64]])
                        P.dve("tensor_copy", out=o, in_=s_)
                    for b in range(NB):
                        tok0 = n64 if d == 0 else tau(1, n64) - 63
                        s_ = yb.raw(b * 4 * 128, [[ps_y, 64], [64, 8], [1, 64]])
                        o = YT[d].raw(b * TT + tok0, [[NT, 64], [64 * NT, 8], [1, 64]])
                        P.dma("pool", o, s_)


def phase_rwkv_out(P, C, SC, YT, MIXT, b, e):
    with P.scope():
        sm = P.tile("ro_sm", [128, 4, 9])
        P.dma("sp", sm, C.rw_small[e])
        epl = P.tile("ro_eps", [128, 1])
        P.dve("memset", ap=epl, constant=64e-5)
        nm = ["y0", "y1", "r", "k", "v", "g", "yc", "sq", "rs", "o"]
        T = [{n: P.tile(f"ro_{n}{i}", [128, TN]) for n in nm} for i in range(2)]
        pss = [P.psum(f"ro_ps{i}", [128, 512]) for i in range(6)]
        it = 0
        for hp in range(4):
            rows = slice(hp * 128, (hp + 1) * 128)
            for tt in range(NTT):
                t = T[it % 2]
                pm, pv, pb = pss[(it % 2) * 3], pss[(it % 2) * 3 + 1], pss[(it % 2) * 3 + 2]
                it += 1
                cs = slice(b * TT + tt * TN, b * TT + (tt + 1) * TN)
                P.dma("sp", t["y0"], YT[0][rows, cs])
                P.dma("sp", t["y1"], YT[1][rows, cs])
                for q in ("r", "k", "v", "g"):
                    P.dma("act", t[q], SC[q][rows, cs])
                P.dve("tensor_tensor", out=t["y0"], in0=t["y0"], in1=t["y1"], op=ALU.add)
                P.mm(pm[:, 0:TN], lhsT=C.blk64, rhs=t["y0"])
                P.dve("scalar_tensor_tensor", out=t["yc"], in0=pm[:, 0:TN], scalar=-1.0 / 64, in1=t["y0"],
                      op0=ALU.mult, op1=ALU.add)
                P.pool("tensor_tensor", out=t["sq"], in0=t["yc"], in1=t["yc"], op=ALU.mult)
                P.mm(pv[:, 0:TN], lhsT=C.blk64, rhs=t["sq"])
                P.act("activation", out=t["rs"], in_=pv[:, 0:TN], func=AF.Sqrt, scale=1.0 / 64, bias=epl)
                P.dve("reciprocal", out=t["rs"], in_=t["rs"])
                P.dve("tensor_tensor", out=t["yc"], in0=t["yc"], in1=t["rs"], op=ALU.mult)
                P.act("activation", out=t["yc"], in_=t["yc"], func=AF.Identity, scale=sm[:, hp, 7:8], bias=sm[:, hp, 8:9])
                P.dve("scalar_tensor_tensor", out=t["sq"], in0=t["r"], scalar=sm[:, hp, 6:7], in1=t["k"],
                      op0=ALU.mult, op1=ALU.mult)
                P.mm(pb[:, 0:TN], lhsT=C.blk64, rhs=t["sq"])
                P.dve("tensor_tensor", out=t["o"], in0=pb[:, 0:TN], in1=t["v"], op=ALU.mult)
                P.pool("tensor_tensor", out=t["o"], in0=t["o"], in1=t["yc"], op=ALU.add)
                P.pool("tensor_tensor", out=t["o"], in0=t["o"], in1=t["g"], op=ALU.mult)
                P.dma("pool", MIXT[512 + hp * 128:512 + (hp + 1) * 128, cs], t["o"])


def build_full(nlayers=DEPTH, final=True):
    nc, P, C = build()
    C.outT = V(nc.dram_tensor("outT", [D, NB * TL], F32, kind="ExternalOutput").ap(), Buf("outT"))
    H = P.dram("H", [D, NT])
    MIXT = P.dram("MIXT", [D, NT])
    ZR = P.dram("ZR", [RW_IN, NT])
    SC = {q: P.dram("SC_" + q, [512, NT]) for q in ["w0", "w1", "kd0", "kd1", "bv0", "bv1", "a", "r", "k", "v", "g"]}
    VT = P.dram("VT", [TT, 1024])
    YT = [P.dram("YT0", [512, NT]), P.dram("YT1", [512, NT])]
    HD = [P.dram("HD0", [NT, 512]), P.dram("HD1", [NT, 512])]
    QKV = P.dram("QKV", [NT, 3, 512])
    ZO = P.dram("ZO", [2048, NT])
    VTOK = P.dram("VTOK", [NT, 512])
    with nc.named_scope('mod'):
        phase_mod(P, C)
    for l in range(nlayers):
        hsrc = C.xT if l == 0 else H
        if l % 2 == 0:
            e = l // 2
            for b in range(NB):
                with P.scope():
                    XT = P.tile("XT", [128, 8, TT])
                    with nc.named_scope(f'nm1_{l}_{b}'):
                        phase_normmod(P, C, hsrc, b, l, 0, 1, XT)
                    with nc.named_scope(f'inproj_{l}_{b}'):
                        with P.scope():
                            lin_fm(P, C, XT, 8, C.ev_in_w[e], ML_IN, RW_IN, store_fm(P, ZR, -ML_IN, b, "zr"), name="ei")
                    with nc.named_scope(f'mlstm_{l}_{b}'):
                        phase_mlstm(P, C, XT, HD, QKV, b, e)
                    with nc.named_scope(f'mlstmout_{l}_{b}'):
                        phase_mlstm_out(P, C, XT, HD, MIXT, b, e)
                with nc.named_scope(f'rwprep_{l}_{b}'):
                    phase_rwkv_prep(P, C, ZR, SC, VT, b, e)
            with nc.named_scope(f'rwscan_{l}'):
                phase_rwkv_scan(P, C, SC, VT, YT)
            for b in range(NB):
                with nc.named_scope(f'rwout_{l}_{b}'):
                    phase_rwkv_out(P, C, SC, YT, MIXT, b, e)
        else:
            o = l // 2
            with P.scope():
                TB2 = P.tile("TB2", [128, 8, 14, 64])
                with nc.named_scope(f'natab_{l}'):
                    phase_na_table(P, C, o, TB2)
                for b in range(NB):
                    with P.scope():
                        XT = P.tile("XT", [128, 8, TT])
                        with nc.named_scope(f'nm1_{l}_{b}'):
                            phase_normmod(P, C, hsrc, b, l, 0, 1, XT)
                        with nc.named_scope(f'inproj_{l}_{b}'):
                            with P.scope():
                                lin_fm(P, C, XT, 8, C.od_in_w[o], 0, 2048, store_fm(P, ZO, 0, b, "zo"), name="oi")
                            lin_tm(P, C, XT, C.od_in_w[o], 2048, 512, VTOK, b)
                    with nc.named_scope(f'lru_{l}_{b}'):
                        phase_lru(P, C, ZO, MIXT, b, o)
                    with nc.named_scope(f'na_{l}_{b}'):
                        phase_na(P, C, ZO, VTOK, MIXT, b, TB2, l != DEPTH - 1)
        for b in range(NB):
            with P.scope():
                XT = P.tile("XT", [128, 8, TT])
                with nc.named_scope(f'outproj_{l}_{b}'):
                    phase_outproj(P, C, MIXT, b, l, hsrc, H, XT)
            with nc.named_scope(f'mlp_{l}_{b}'):
                phase_mlp(P, C, H, b, l)
    if final:
        with nc.named_scope('final'):
            phase_final(P, C, H)
    P.finish()
    return nc, P, C


def kernel(**inputs):
    nc, P, C = build_full()
    names = set(C._ext.keys())
    maps = host_inputs(inputs, names)
    res = run_bass_kernel_spmd(nc, maps, core_ids=list(range(NCORES)))
    out = np.empty((NCORES * NB, TL, D), np.float32)
    for core in range(NCORES):
        o = np.asarray(res.results[core]["outT"], np.float32)
        for i in range(NB):
            out[core * NB + i] = o[:, i * TL:(i + 1) * TL].T
    return out
```
